# Optimizing a Trainium2 kernel written in Bass

```python
import math
import jax, jax.numpy as jnp
from jax import lax
import numpy as np

D_MODEL = 1024
BATCH = 16
SEQ = 256
DEPTH = 2
DEC_BATCH = 8
DEC_SEQ = 1024
PAST_LEN = 256

GRID_W = 64
POOL_WINDOWS = (2, 4, 8, 16)
N_POOL_GROUPS = len(POOL_WINDOWS)
POOL_WIDTH = D_MODEL // 4
POOL_GROUP = POOL_WIDTH // N_POOL_GROUPS
ATTN_WIDTH = D_MODEL // 2
N_HEADS = 4
V_DIM = ATTN_WIDTH // N_HEADS
QK_DIM = V_DIM // 2
Q_BLOCK = 128
ROPE_BASE = 10000.0
ROPE_AXIS_DIM = QK_DIM // 2
CHUNK = 128
SGU_WIDTH = D_MODEL // 4
SGU_GROUPS = 4
SGU_GROUP_DIM = SGU_WIDTH // SGU_GROUPS
QK_WIDTH = N_HEADS * 2 * QK_DIM
IN_WIDTH = POOL_WIDTH + 2 * QK_WIDTH + ATTN_WIDTH + 2 * SGU_WIDTH
MIX_WIDTH = POOL_WIDTH + ATTN_WIDTH + SGU_WIDTH
SPLITS = (POOL_WIDTH, POOL_WIDTH + QK_WIDTH, POOL_WIDTH + 2 * QK_WIDTH,
          POOL_WIDTH + 2 * QK_WIDTH + ATTN_WIDTH)
D_FF = -(-8 * D_MODEL // (3 * 256)) * 256
N_MOD = 6
EPS = 1e-6

kernel_name = "hybrid_pool_diffattn_sgu_prefix_dit_step"


def rms_norm(x, g):
    xf = x.astype(jnp.float32)
    y = xf * lax.rsqrt(jnp.mean(xf * xf, axis=-1, keepdims=True) + EPS)
    return (y * g.astype(jnp.float32)).astype(x.dtype)


def layer_norm(x, g):
    xf = x.astype(jnp.float32)
    mu = jnp.mean(xf, axis=-1, keepdims=True)
    xc = xf - mu
    y = xc * lax.rsqrt(jnp.mean(xc * xc, axis=-1, keepdims=True) + EPS)
    return (y * g.astype(jnp.float32)).astype(x.dtype)


def ada_modulation(cond, w, b):
    m = jax.nn.silu(cond) @ w + b
    return jnp.split(m[:, None, :], N_MOD, axis=-1)


def multiscale_pool(x):
    L = x.shape[1]
    xf = x.astype(jnp.float32)
    cs = jnp.concatenate([jnp.zeros_like(xf[:, :1]), jnp.cumsum(xf, axis=1)], axis=1)
    t = np.arange(L)
    outs = []
    for g, w in enumerate(POOL_WINDOWS):
        lo = np.clip(t - w // 2, 0, L)
        hi = np.clip(t + w - w // 2, 0, L)
        sl = slice(g * POOL_GROUP, (g + 1) * POOL_GROUP)
        seg = cs[..., sl]
        cnt = jnp.asarray((hi - lo).astype(np.float32))[None, :, None]
        outs.append((seg[:, hi] - seg[:, lo]) / cnt - xf[..., sl])
    return jnp.concatenate(outs, axis=-1).astype(x.dtype)


def axial_rope_tables(L):
    n_rows = L // GRID_W
    rows = np.repeat(np.arange(n_rows), GRID_W).astype(np.float32)
    cols = np.tile(np.arange(GRID_W), n_rows).astype(np.float32)
    inv = 1.0 / (ROPE_BASE ** (np.arange(0, ROPE_AXIS_DIM, 2, dtype=np.float32) / ROPE_AXIS_DIM))
    ar, ac = rows[:, None] * inv[None], cols[:, None] * inv[None]
    return (jnp.asarray(np.cos(ar)), jnp.asarray(np.sin(ar)),
            jnp.asarray(np.cos(ac)), jnp.asarray(np.sin(ac)))


def _rotate(x, cos, sin):
    n = x.shape[-1] // 2
    x1, x2 = x[..., :n], x[..., n:]
    c, s = cos[None, :, None, :], sin[None, :, None, :]
    return jnp.concatenate([x1 * c - x2 * s, x2 * c + x1 * s], axis=-1)


def apply_axial_rope(x, cos_r, sin_r, cos_c, sin_c):
    xf = x.astype(jnp.float32)
    xr = _rotate(xf[..., :ROPE_AXIS_DIM], cos_r, sin_r)
    xc = _rotate(xf[..., ROPE_AXIS_DIM:], cos_c, sin_c)
    return jnp.concatenate([xr, xc], axis=-1).astype(x.dtype)


def diff_attention(q1, q2, k1, k2, v, lam):
    B, Lq = q1.shape[:2]
    nb = Lq // Q_BLOCK
    scale = QK_DIM ** -0.5

    def to_blocks(q):
        return q.reshape(B, nb, Q_BLOCK, N_HEADS, QK_DIM).transpose(1, 0, 2, 3, 4)

    def block(args):
        a1, a2 = args
        s1 = jnp.einsum('bqhd,bkhd->bhqk', a1, k1, preferred_element_type=jnp.float32) * scale
        s2 = jnp.einsum('bqhd,bkhd->bhqk', a2, k2, preferred_element_type=jnp.float32) * scale
        p = jax.nn.softmax(s1, axis=-1) - lam * jax.nn.softmax(s2, axis=-1)
        return jnp.einsum('bhqk,bkhe->bqhe', p.astype(v.dtype), v)

    out = lax.map(block, (to_blocks(q1), to_blocks(q2)))
    return out.transpose(1, 0, 2, 3, 4).reshape(B, Lq, N_HEADS, V_DIM)


def chunk_spatial_gating(uv, g_n, w_s, b_s):
    B, L, _ = uv.shape
    u, v = jnp.split(jax.nn.gelu(uv), 2, axis=-1)
    v = layer_norm(v, g_n).reshape(B, L // CHUNK, CHUNK, SGU_GROUPS, SGU_GROUP_DIM)
    v = jnp.einsum('gpq,bnqgc->bnpgc', w_s, v) + b_s.T[None, None, :, :, None]
    return u * v.reshape(B, L, SGU_WIDTH)


def setup_inputs(seed: int = 0) -> dict:
    key = jax.random.key(seed)
    ks = jax.random.split(key, 32)
    f32 = jnp.float32
    nrm = lambda k, shape, s: (jax.random.normal(k, shape, f32) * s)
    gain = lambda k, shape: 1.0 + 0.02 * jax.random.normal(k, shape, f32)
    return {
        "x_prompt": nrm(ks[0], (BATCH, SEQ, D_MODEL), 1.0),
        "x_sample": nrm(ks[1], (DEC_BATCH, DEC_SEQ, D_MODEL), 1.0),
        "cache_k": nrm(ks[2], (DEC_BATCH, DEPTH, PAST_LEN, N_HEADS, 2 * QK_DIM), 1.0),
        "cache_v": nrm(ks[3], (DEC_BATCH, DEPTH, PAST_LEN, N_HEADS, V_DIM), 1.0),
        "c": nrm(ks[4], (DEC_BATCH, D_MODEL), 1.0),
        "c_ctx": nrm(ks[5], (D_MODEL,), 1.0),
        "norm1_g": gain(ks[6], (DEPTH, D_MODEL)),
        "w_ada": nrm(ks[7], (DEPTH, D_MODEL, N_MOD * D_MODEL), 0.5 * D_MODEL ** -0.5),
        "b_ada": nrm(ks[8], (DEPTH, N_MOD * D_MODEL), 0.02),
        "w_in": nrm(ks[9], (DEPTH, D_MODEL, IN_WIDTH), D_MODEL ** -0.5),
        "w_pool": nrm(ks[10], (DEPTH, N_POOL_GROUPS, POOL_GROUP, POOL_GROUP), POOL_GROUP ** -0.5),
        "pool_scale": gain(ks[11], (DEPTH, POOL_WIDTH)),
        "lam_q1": nrm(ks[12], (DEPTH, QK_DIM), 0.1),
        "lam_k1": nrm(ks[13], (DEPTH, QK_DIM), 0.1),
        "lam_q2": nrm(ks[14], (DEPTH, QK_DIM), 0.1),
        "lam_k2": nrm(ks[15], (DEPTH, QK_DIM), 0.1),
        "subln_g": gain(ks[16], (DEPTH, V_DIM)),
        "sgu_norm_g": gain(ks[17], (DEPTH, SGU_WIDTH)),
        "w_sgu": nrm(ks[18], (DEPTH, SGU_GROUPS, CHUNK, CHUNK), CHUNK ** -0.5),
        "b_sgu": nrm(ks[19], (DEPTH, SGU_GROUPS, CHUNK), 0.02),
        "w_out": nrm(ks[20], (DEPTH, MIX_WIDTH, D_MODEL), MIX_WIDTH ** -0.5),
        "norm2_g": gain(ks[21], (DEPTH, D_MODEL)),
        "w_ffn_in": nrm(ks[22], (DEPTH, D_MODEL, 2 * D_FF), D_MODEL ** -0.5),
        "w_ffn_out": nrm(ks[23], (DEPTH, D_FF, D_MODEL), D_FF ** -0.5),
        "final_g": gain(ks[24], (D_MODEL,)),
    }


def reference(x_prompt, x_sample, cache_k, cache_v, c, c_ctx, norm1_g, w_ada, b_ada, w_in,
              w_pool, pool_scale, lam_q1, lam_k1, lam_q2, lam_k2, subln_g, sgu_norm_g,
              w_sgu, b_sgu, w_out, norm2_g, w_ffn_in, w_ffn_out, final_g):

    def layer(x, l, cond, rope, ctx_k, ctx_v):
        B, L, _ = x.shape
        sh1, sc1, g1, sh2, sc2, g2 = ada_modulation(cond, w_ada[l], b_ada[l])
        h = rms_norm(x, norm1_g[l]) * (1.0 + sc1) + sh1
        p_pool, p_q, p_k, p_v, p_uv = jnp.split(h @ w_in[l], SPLITS, axis=-1)

        pooled = multiscale_pool(p_pool).reshape(B, L, N_POOL_GROUPS, POOL_GROUP)
        y_a = jnp.einsum('blgc,gcd->blgd', pooled, w_pool[l]).reshape(B, L, POOL_WIDTH) * pool_scale[l]

        q = p_q.reshape(B, L, N_HEADS, 2, QK_DIM)
        k = p_k.reshape(B, L, N_HEADS, 2, QK_DIM)
        q1, q2, k1, k2 = q[..., 0, :], q[..., 1, :], k[..., 0, :], k[..., 1, :]
        if rope is not None:
            q1, q2, k1, k2 = (apply_axial_rope(t, *rope) for t in (q1, q2, k1, k2))
        k_cat = jnp.concatenate([k1, k2], axis=-1)
        v = p_v.reshape(B, L, N_HEADS, V_DIM)
        if ctx_k is not None:
            k_all = jnp.concatenate([ctx_k, k_cat], axis=1)
            v_all = jnp.concatenate([ctx_v, v], axis=1)
        else:
            k_all, v_all = k_cat, v
        lam_init = 0.8 - 0.6 * math.exp(-0.3 * l)
        lam = (jnp.exp(jnp.sum(lam_q1[l].astype(jnp.float32) * lam_k1[l].astype(jnp.float32)))
               - jnp.exp(jnp.sum(lam_q2[l].astype(jnp.float32) * lam_k2[l].astype(jnp.float32)))
               + lam_init)
        y_b = diff_attention(q1, q2, k_all[..., :QK_DIM], k_all[..., QK_DIM:], v_all, lam)
        y_b = (rms_norm(y_b, subln_g[l]) * (1.0 - lam_init)).reshape(B, L, ATTN_WIDTH)

        y_c = chunk_spatial_gating(p_uv, sgu_norm_g[l], w_sgu[l], b_sgu[l])

        y = jnp.concatenate([y_a, y_b, y_c], axis=-1) @ w_out[l]
        x = x + g1 * y
        h2 = rms_norm(x, norm2_g[l]) * (1.0 + sc2) + sh2
        gate, up = jnp.split(h2 @ w_ffn_in[l], 2, axis=-1)
        x = x + g2 * ((jax.nn.silu(gate) * up) @ w_ffn_out[l])
        return x, k_cat, v

    cond_ctx = c_ctx[None, :]
    xp = x_prompt
    ks_new, vs_new = [], []
    for l in range(DEPTH):
        xp, k_l, v_l = layer(xp, l, cond_ctx, None, None, None)
        ks_new.append(k_l)
        vs_new.append(v_l)
    y_prompt = rms_norm(xp, final_g)
    new_cache_k = jnp.stack(ks_new, axis=1)
    new_cache_v = jnp.stack(vs_new, axis=1)

    rope = axial_rope_tables(x_sample.shape[1])
    xs = x_sample
    for l in range(DEPTH):
        xs, _, _ = layer(xs, l, c, rope, cache_k[:, l], cache_v[:, l])
    y_sample = rms_norm(xs, final_g)

    return (y_prompt, y_sample, new_cache_k, new_cache_v)
```

```python
import math
import numpy as np
import concourse.bass as bass
import concourse.mybir as mybir
from concourse.bass_utils import run_bass_kernel_spmd

F32 = mybir.dt.float32
BF16 = mybir.dt.bfloat16
AF = mybir.ActivationFunctionType
ALU = mybir.AluOpType
AX = mybir.AxisListType

D = 1024
NTOK = 1536
NT = 3
DFF = 2816
NFF = 22
EPS = 1e-6
POOL_WINDOWS = (2, 4, 8, 16)
PADW = 1584
SEQ_PAD0 = (8, 280, 552)
SEQ_TOK0 = (0, 256, 512)
SEQ_LEN = (256, 256, 1024)
RING_SLOTS = 4
NWARM = 0
PIECE = 256


class Res:
    __slots__ = ("name", "w", "r", "excl")

    def __init__(self, name, excl=False):
        self.name = name
        self.w = None
        self.r = []
        self.excl = excl


class DSem:
    def __init__(self, nc, name):
        self.sem = nc.alloc_semaphore(name)
        self.count = 0
        self.last = None


class Op:
    __slots__ = ("eng", "idx", "fn", "waits", "sig", "dsem", "dcount", "sigcount")


class Sched:
    ENG = ("pe", "act", "dve", "pool", "sp")

    def __init__(self, nc):
        self.nc = nc
        self.ops = {e: [] for e in self.ENG}
        self.waited = {e: {} for e in self.ENG}
        self.sems = {e: nc.alloc_semaphore("sem_" + e) for e in ("pe", "act", "dve", "pool")}
        self.dsems = []
        self.also = {}

    def dsem(self, name):
        d = DSem(self.nc, name)
        self.dsems.append(d)
        return d

    def add(self, eng, fn, reads=(), writes=(), dsem=None):
        op = Op()
        op.eng = eng
        op.fn = fn
        op.idx = len(self.ops[eng])
        op.sig = False
        op.dsem = dsem
        op.dcount = 0
        op.sigcount = 0
        if dsem is not None:
            dsem.count += 16
            op.dcount = dsem.count
            dsem.last = op
        reads = list(reads)
        for r in list(reads):
            if r in self.also:
                reads.extend(self.also[r])
        deps = []
        for r in reads:
            if r.w is not None:
                deps.append(r.w)
            if r.excl:
                deps.extend(x for x in r.r if x.eng != eng)
        for w in writes:
            if w.w is not None:
                deps.append(w.w)
            deps.extend(w.r)
        best = {}
        for d in deps:
            if d is op:
                continue
            if d.dsem is not None:
                key, order = d.dsem, d.dcount
            else:
                if d.eng == "pe" and eng == "pe" and dsem is None:
                    continue
                key, order = d.eng, d.idx
            if key not in best or best[key][0] < order:
                best[key] = (order, d)
        op.waits = []
        wd = self.waited[eng]
        for key, (order, d) in best.items():
            if wd.get(key, -1) >= order:
                continue
            wd[key] = order
            d.sig = True
            op.waits.append(d)
        for r in reads:
            r.r.append(op)
        for w in writes:
            w.w = op
            w.r = []
        self.ops[eng].append(op)
        return op

    def fence(self, news):
        lst = []
        for e in self.ENG:
            if self.ops[e]:
                for o in reversed(self.ops[e]):
                    if o.dsem is None:
                        lst.append(o)
                        break
        for d in self.dsems:
            if d.last is not None:
                lst.append(d.last)
        for r in news:
            r.w = None
            r.r = list(lst)

    def transfer(self, olds, news):
        lst = []
        for o in olds:
            if o.w is not None:
                lst.append(o.w)
            lst.extend(o.r)
        for r in news:
            r.w = None
            r.r = list(lst)

    def emit(self, eng, e):
        cnt = 0
        for op in self.ops[eng]:
            for d in op.waits:
                if d.dsem is not None:
                    e.wait_ge(d.dsem.sem, d.dcount)
                else:
                    e.wait_ge(self.sems[d.eng], d.sigcount)
            ins = op.fn(e)
            if op.dsem is not None:
                ins.then_inc(op.dsem.sem, 16)
            elif op.sig:
                ins.then_inc(self.sems[eng], 1)

    def finalize_counts(self):
        for eng in self.ENG:
            c = 0
            for op in self.ops[eng]:
                if op.dsem is None and op.sig:
                    c += 1
                op.sigcount = c


def I(name, *a, **kw):
    return lambda e: getattr(e, name)(*a, **kw)


def lam_init(l):
    return 0.8 - 0.6 * math.exp(-0.3 * l)


class _Stop(Exception):
    pass


def build_program(taps=None, stop=None):
    nc = bass.Bass("TRN2", target_bir_lowering=False)
    S = Sched(nc)

    def CK(name):
        if stop == name:
            raise _Stop()

    def din(name, shape):
        return nc.dram_tensor(name, list(shape), F32, kind="ExternalInput").ap()

    def dout(name, shape):
        return nc.dram_tensor(name, list(shape), F32, kind="ExternalOutput").ap()

    xin = din("xin", [NTOK, D])
    ck = din("ck", [2, 256, 512])
    cv = din("cv", [2, 256, 512])
    w_ada = din("w_ada", [2, D, 6 * D])
    w_in = din("w_in", [2, D, 2304])
    w_out = din("w_out", [2, D, D])
    w_fi = din("w_ffn_in", [2, D, 2 * DFF])
    w_fo = din("w_ffn_out", [2, DFF, D])
    w_pool = din("w_pool", [2, 4, 64, 64])
    w_sT = din("w_sT", [2, 4, 128, 128])
    b_sgu = din("b_sgu", [2, 4, 128])
    sgn = din("sgu_norm_g", [2, 256])
    lamv = din("lamv", [1, 512])
    sv_d = din("smallvec", [2, 128, 128])
    c_ident = din("c_ident", [128, 128])
    c_pm = din("c_pm", [128, 128])
    c_ropec = din("c_ropec", [128, 1024])
    c_ropes = din("c_ropes", [128, 1024])
    c_invw = din("c_invw", [128, 2])
    c_edge = din("c_edge", [128, 2, 16])
    yout = dout("yout", [NTOK, D])
    nk_o = dout("nk", [2, 2, 256, 512])
    nv_o = dout("nv", [2, 2, 256, 512])
    tap_out = {}
    if taps:
        for name, shape in taps.items():
            tap_out[name] = dout("tap_" + name, shape)

    budget0 = nc.sbuf_bytes_remaining

    def sb(name, shape, dt):
        return nc.alloc_sbuf_tensor(name, list(shape), dt)

    X = sb("X", [128, 8, NTOK], F32)
    HY = sb("HY", [128, 8, NTOK], BF16)
    RING = sb("RING", [128, RING_SLOTS, 8 * PIECE], BF16)
    ROPEC = sb("ROPEC", [128, 1024], BF16)
    ROPES = sb("ROPES", [128, 1024], BF16)
    IDF = sb("IDF", [128, 128], F32)
    ONES = sb("ONES", [128, 128], BF16)
    OTOP = sb("OTOP", [128, 128], BF16)
    OBOT = sb("OBOT", [128, 128], BF16)
    PM = sb("PM", [128, 128], BF16)
    WPBD = sb("WPBD", [128, 4, 128], BF16)
    WST = sb("WST", [128, 8, 128], BF16)
    BSG = sb("BSG", [128, 4, 128], F32)
    GN = sb("GN", [128, 2, 256], F32)
    SV = sb("SV", [128, 256], F32)
    SCT = sb("SCT", [128, 8, 2], BF16)
    MOD = sb("MOD", [128, 2, 48, 2], F32)
    A1 = sb("A1", [128, 2, 8, 2], F32)
    A2 = sb("A2", [128, 2, 8, 2], F32)
    LAM = sb("LAM", [128, 16], F32)
    GSUB = sb("GSUB", [128, 2], F32)
    PSC = sb("PSC", [128, 4], F32)
    INVW = sb("INVW", [128, 2], F32)
    EDGE = sb("EDGE", [128, 2, 16], F32)
    NBS = sb("NBS", [128, 8, 8], F32)
    NBA = sb("NBA", [128, 16], F32)
    ST6 = sb("ST6", [128, 48], F32)
    RSTD = sb("RSTD", [128, 512], F32)
    STG = sb("STG", [128, 2, 1024], F32)
    POOLED = sb("POOLED", [128, 2, NTOK], BF16)
    TMPF = sb("TMPF", [128, 4, 512], F32)
    SQ0 = sb("SQ0", [128, 1280], BF16)
    SQ1 = sb("SQ1", [128, 1280], BF16)
    SQ = None
    XB = sb("XB", [128, 2, 512], BF16)
    MROW = sb("MROW", [2, 256], F32)
    SVT = sb("SVT", [128, 256], F32)
    EPSB = sb("EPSB", [128, 1], F32)

    A_QT, A_KT, A_VT, A_UT = 0, 12288, 26624, 40960
    A_SH = 47104
    ARENA_BYTES = max(A_SH + 4 * PADW * 4, NFF * NTOK * 2)
    ARENA = sb("ARENA", [128, ARENA_BYTES // 2], BF16)

    def aview(off, nbytes, dt):
        v = ARENA[:, off // 2:(off + nbytes) // 2]
        if dt is F32:
            v = v.bitcast(F32)
        return v

    QT = aview(A_QT, 12288, BF16).rearrange("p (h n) -> p h n", h=4)
    KT = aview(A_KT, 14336, BF16).rearrange("p (h n) -> p h n", h=4)
    VT = aview(A_VT, 14336, BF16).rearrange("p (c n) -> p c n", c=14)
    UT = aview(A_UT, 6144, BF16).rearrange("p (j n) -> p j n", j=2)
    PP = aview(A_SH, 2 * PADW * 4, F32).rearrange("p (j n) -> p j n", j=2)
    PA = aview(A_SH + 2 * PADW * 4, PADW * 4, F32)
    PB = aview(A_SH + 3 * PADW * 4, PADW * 4, F32)
    VNP = aview(A_SH, 8192, BF16).rearrange("p (b c g n) -> p b c g n", b=2, c=4, g=4)
    EB = aview(A_SH + 8192, 8192, BF16).rearrange("p (i n) -> p i n", i=8)
    ATT = None
    XNBS = [aview(o, 16384, F32).rearrange("p (k n) -> p k n", k=8) for o in (0, 16384)]
    ACTB = aview(0, NFF * NTOK * 2, BF16).rearrange("p (j n) -> p j n", j=NFF)

    used = budget0 - nc.sbuf_bytes_remaining
    assert nc.sbuf_bytes_remaining > 128, f"SBUF over budget: used {used}"

    PS = nc.alloc_psum_tensor("PS", [128, 8, 512], F32)

    xR = [[Res(f"x{k}_{t}") for t in range(NT)] for k in range(8)]
    hyR = [[Res(f"hy{k}_{t}") for t in range(NT)] for k in range(8)]
    bankR = [Res(f"bank{i}", excl=True) for i in range(8)]
    ringR = [Res(f"ring{i}") for i in range(RING_SLOTS)]
    ringD = [S.dsem(f"dring{i}") for i in range(RING_SLOTS)]
    constR = Res("consts")
    constD = S.dsem("dconst")
    constR2 = Res("consts2")
    constD2 = S.dsem("dconst2")
    S.also[constR] = [constR2]
    stgR = [Res("stg0"), Res("stg1")]
    stgD = [S.dsem("dstg0"), S.dsem("dstg1")]
    outD = [S.dsem("dout0"), S.dsem("dout1")]
    cvD = [S.dsem("dcv0"), S.dsem("dcv1")]
    rstdR = Res("rstd")
    tmpR = [Res(f"tmp{i}") for i in range(4)]
    sqR = [Res("sq0"), Res("sq1")]
    xbR = [Res("xb0"), Res("xb1")]
    mrowR = Res("mrow")
    modR = [Res("mod0"), Res("mod1")]
    svR = Res("sv")
    miscR = Res("misc")
    qR = [[Res(f"q{h}_{t}") for t in range(NT)] for h in range(4)]
    kR = [[Res(f"k{h}_{t}") for t in range(NT)] for h in range(4)]
    kcR = [Res(f"kc{h}") for h in range(4)]
    vR = [Res(f"v{c}") for c in range(14)]
    uR = [[Res(f"u{j}_{t}") for t in range(NT)] for j in range(2)]
    ppR = [Res("pp0"), Res("pp1")]
    paR, pbR = Res("pa"), Res("pb")
    pooledR = [Res("pooled0"), Res("pooled1")]
    vnpR = [Res("vnp0"), Res("vnp1")]
    ebR = [Res(f"eb{i}") for i in range(8)]
    nbsR = [Res(f"nbs{i}") for i in range(8)]
    nbaR = [Res(f"nba{i}") for i in range(8)]
    attR = Res("att")
    nbR = Res("nb")
    st6R = Res("st6")
    actR = [[Res(f"act{j}_{t}") for t in range(NT)] for j in range(NFF)]
    xnbRs = [Res("xnb0"), Res("xnb1")]
    outR = Res("outdram")

    tile = lambda t: slice(t * 512, (t + 1) * 512)
    cond_of = (0, 1, 1)

    ring_state = {"n": 0}

    def load_piece(src_ap, nk=8, ncols=PIECE):
        i = ring_state["n"] % RING_SLOTS
        ring_state["n"] += 1
        dst = RING[:, i, :].rearrange("p (k n) -> p k n", k=8)
        S.add("pool", lambda e, d=dst[:, 0:nk, 0:ncols], s=src_ap: e.dma_start(out=d, in_=s),
              writes=[ringR[i]], dsem=ringD[i])
        return dst, ringR[i]

    def wrows(w2d, c0, ncols=PIECE, k0=0, nk=8):
        return w2d.rearrange("(k p) n -> p k n", p=128)[:, k0:k0 + nk, c0:c0 + ncols]

    wpbd0R = Res("wpbd0")

    def cdma(eng, dst, src, reads=()):
        S.add(eng, I("dma_start", out=dst, in_=src), reads=list(reads), writes=[Res("c")], dsem=(constD if eng == "sp" else constD2))

    S.add("dve", I("memset", ONES[:], 1.0), writes=[miscR])
    S.add("dve", I("memset", OTOP[:], 0.0), writes=[miscR])
    S.add("dve", I("memset", OTOP[0:64, :], 1.0), writes=[miscR])
    S.add("dve", I("memset", OBOT[:], 0.0), writes=[miscR])
    S.add("dve", I("memset", OBOT[64:128, :], 1.0), writes=[miscR])
    S.add("dve", I("memset", WPBD[:], 0.0), writes=[wpbd0R])
    cdma("sp", IDF[:], c_ident)
    cdma("sp", SV[:].rearrange("p (a n) -> p a n", a=2), sv_d.rearrange("a p n -> p a n"))
    cdma("pool", ROPEC[:], c_ropec)
    cdma("pool", ROPES[:], c_ropes)
    cdma("sp", INVW[:], c_invw)
    cdma("sp", EDGE[:], c_edge)
    cdma("sp", TMPF[:, 0, :], lamv.partition_broadcast(128))
    for l in range(2):
        cdma("sp", GN[:, l, :], sgn[l:l + 1, :].partition_broadcast(128))
        for g in range(4):
            cdma("sp", BSG[(g % 2) * 64:(g % 2) * 64 + 64, l * 2 + g // 2, :],
                 b_sgu[l, g:g + 1, :].partition_broadcast(64))
            cdma("pool", WPBD[(g % 2) * 64:(g % 2) * 64 + 64, l * 2 + g // 2, (g % 2) * 64:(g % 2) * 64 + 64],
                 w_pool[l, g], reads=[wpbd0R])
            cdma("pool", WST[:, l * 4 + g, :], w_sT[l, g])
    cdma("pool", PM[:], c_pm)
    constR.w = constD.last
    constR2.w = constD2.last
    stop_early = False
    try:
        CK("consts")
    except _Stop:
        stop_early = True

    for a in range(2):
        S.add("pe", I("transpose", PS[:, 6, a * 128:(a + 1) * 128], SV[:, a * 128:(a + 1) * 128], IDF[:]),
              reads=[constR], writes=[bankR[6]])
    S.add("dve", I("tensor_copy", out=SVT[:], in_=PS[:, 6, 0:256]), reads=[bankR[6]], writes=[svR])
    S.add("act", I("activation", out=SCT[:].rearrange("p k c -> p c k"),
                                        in_=SVT[:, 0:16].rearrange("p (c k) -> p c k", c=2), func=AF.Silu),
          reads=[svR], writes=[miscR])
    lv = TMPF[:, 0, :].rearrange("p (l b two n) -> p l b two n", l=2, b=2, two=2)
    S.add("dve", I("tensor_tensor", out=TMPF[:, 1, 0:256].rearrange("p (l b n) -> p l b n", l=2, b=2),
                                           in0=lv[:, :, :, 0, :], in1=lv[:, :, :, 1, :], op=ALU.mult),
          reads=[constR], writes=[tmpR[1]])
    S.add("dve", I("tensor_reduce", out=LAM[:, 0:4], in_=TMPF[:, 1, 0:256].rearrange("p (a n) -> p a n", a=4),
                                           axis=AX.X, op=ALU.add), reads=[tmpR[1]], writes=[miscR])
    S.add("act", I("activation", out=LAM[:, 4:8], in_=LAM[:, 0:4], func=AF.Exp), reads=[miscR], writes=[miscR])
    for l in range(2):
        S.add("dve", I("scalar_tensor_tensor", out=LAM[:, 8 + l:9 + l], in0=LAM[:, 5 + 2 * l:6 + 2 * l],
                                                           scalar=-lam_init(l), in1=LAM[:, 4 + 2 * l:5 + 2 * l],
                                                           op0=ALU.add, op1=ALU.subtract),
              reads=[miscR], writes=[miscR])
        S.add("dve", I("tensor_scalar", out=GSUB[:, l:l + 1], in0=SVT[:, 60 + l:61 + l],
                                                    scalar1=1.0 - lam_init(l), scalar2=None, op0=ALU.mult),
              reads=[svR], writes=[miscR])

    for c in range(12):
        b = c % 2
        S.add("sp", I("dma_start", out=STG[:, b, :], in_=xin[c * 128:(c + 1) * 128, :]),
              writes=[stgR[b]], dsem=stgD[b])
        for half in range(2):
            bk = 2 * b + half
            for kk in range(4):
                k = half * 4 + kk
                S.add("pe", I("transpose", PS[:, bk, kk * 128:(kk + 1) * 128],
                                                                          STG[:, b, k * 128:(k + 1) * 128], IDF[:]),
                      reads=[stgR[b], constR], writes=[bankR[bk]])
            eng = "dve" if half == 0 else "act"
            dst = X[:, half * 4:half * 4 + 4, c * 128:(c + 1) * 128]
            src = PS[:, bk, :].rearrange("p (k n) -> p k n", k=4)
            t = c // 4
            if eng == "dve":
                S.add("dve", I("tensor_copy", out=dst, in_=src), reads=[bankR[bk]],
                      writes=[xR[half * 4 + i][t] for i in range(4)])
            else:
                S.add("act", I("activation", out=dst, in_=src, func=AF.Copy), reads=[bankR[bk]],
                      writes=[xR[half * 4 + i][t] for i in range(4)])

    def ada(l):
        for _ in ada_steps(l):
            pass

    def ada_steps(l):
        for j in range(24):
            slot, sres = load_piece(wrows(w_ada[l], j * PIECE))
            for k in range(8):
                S.add("pe", I("matmul", PS[0:2, 7, 0:PIECE], SCT[:, k, :], slot[:, k, :], start=(k == 0), stop=(k == 7)),
                      reads=[sres, miscR], writes=[bankR[7]])
            S.add("act", I("activation", out=MROW[:, 0:PIECE], in_=PS[0:2, 7, 0:PIECE], func=AF.Copy),
                  reads=[bankR[7]], writes=[mrowR])
            for i in range(2):
                S.add("pe", I("transpose", PS[:, 7, 256 + 2 * i:258 + 2 * i], MROW[:, i * 128:(i + 1) * 128], IDF[0:2, 0:2]),
                      reads=[mrowR, constR], writes=[bankR[7]])
            a0 = 2 * j
            S.add("dve", I("tensor_tensor", out=MOD[:, l, a0:a0 + 2, :], in0=PS[:, 7, 256:260].rearrange("p (a c) -> p a c", c=2),
                           in1=SVT[:, 128 + 48 * l + a0:130 + 48 * l + a0].unsqueeze(2).to_broadcast([128, 2, 2]), op=ALU.add),
                  reads=[bankR[7], svR], writes=[modR[l]])
            for (AA, goff, moff, jdone) in ((A1, 16, 8, 7), (A2, 32, 32, 19)):
                if j == jdone:
                    S.add("dve", I("scalar_tensor_tensor", out=AA[:, l, :, :], in0=MOD[:, l, moff:moff + 8, :], scalar=1.0,
                                   in1=SVT[:, goff + 8 * l:goff + 8 * l + 8].unsqueeze(2).to_broadcast([128, 8, 2]),
                                   op0=ALU.add, op1=ALU.mult), reads=[modR[l], svR], writes=[modR[l]])
            if j < 23:
                yield j

    def modap(l, i, k, c):
        return MOD[:, l, i * 8 + k, c:c + 1]

    def rms_stats(t, bank=6):
        rb, rr = ((RSTD[:], rstdR), (TMPF[:, 3, :], tmpR[3]))[t % 2]
        for k in range(8):
            b = k % 2
            S.add("act", I("activation", out=(SQ0, SQ1)[b][:, 0:512], in_=X[:, k, tile(t)], func=AF.Square),
                  reads=[xR[k][t]], writes=[sqR[b]])
            S.add("pe", I("matmul", PS[:, bank, :], ONES[:], (SQ0, SQ1)[b][:, 0:512], start=(k == 0), stop=(k == 7)),
                  reads=[sqR[b], miscR], writes=[bankR[bank]])
        S.add("act", I("activation", out=rb, in_=PS[:, bank, :], func=AF.Ln, scale=1.0 / D, bias=EPSB[:]),
              reads=[bankR[bank], miscR], writes=[rr])
        S.add("act", I("activation", out=rb, in_=rb, func=AF.Exp, scale=-0.5), reads=[rr], writes=[rr])
        return rb, rr

    S.add("dve", I("memset", EPSB[:], EPS), writes=[miscR])

    def norm_apply(l, AA, shi, t, st):
        rb, rr = st
        c = cond_of[t]
        for k in range(8):
            b = k % 2
            S.add("dve", I("tensor_tensor", out=TMPF[:, b, :], in0=X[:, k, tile(t)], in1=rb, op=ALU.mult),
                  reads=[xR[k][t], rr], writes=[tmpR[b]])
            if k % 2 == 0:
                S.add("act", I("activation", out=HY[:, k, tile(t)], in_=TMPF[:, b, :], func=AF.Identity,
                               scale=AA[:, l, k, c:c + 1], bias=modap(l, shi, k, c)),
                      reads=[tmpR[b], modR[l]], writes=[hyR[k][t]])
            else:
                S.add("dve", I("tensor_scalar", out=HY[:, k, tile(t)], in0=TMPF[:, b, :], scalar1=AA[:, l, k, c:c + 1],
                               scalar2=modap(l, shi, k, c), op0=ALU.mult, op1=ALU.add),
                      reads=[tmpR[b], modR[l]], writes=[hyR[k][t]])

    def norm_seq(l, AA, shi, filler=None, after=None):
        for t in range(NT):
            st = rms_stats(t)
            if filler is not None:
                filler()
            norm_apply(l, AA, shi, t, st)
            if after is not None:
                after(t)

    fb = {"i": 0}

    def fm_one(slot, sres, col0, nk, t, evac):
        bk = fb["i"] % 6
        fb["i"] += 1
        for k in range(nk):
            S.add("pe", I("matmul", PS[:, bk, :], slot[:, k, col0:col0 + 128], HY[:, k, tile(t)], start=(k == 0), stop=(k == nk - 1)),
                  reads=[sres, hyR[k][t]], writes=[bankR[bk]])
        evac(t, bk)

    dstate = {"set": 0}

    def fm_group(slot, sres, col0, nk, rhs_fn, rhs_res_fn, evac, t_outer=False):
        base = 3 * dstate["set"]
        dstate["set"] ^= 1
        order = [(k, t) for t in range(NT) for k in range(nk)] if t_outer else [(k, t) for k in range(nk) for t in range(NT)]
        for (k, t) in order:
            S.add("pe", I("matmul", PS[:, base + t, :], slot[:, k, col0:col0 + 128], rhs_fn(k, t),
                          start=(k == 0), stop=(k == nk - 1)),
                  reads=[sres] + rhs_res_fn(k, t), writes=[bankR[base + t]])
            if t_outer and k == nk - 1:
                evac(t, base + t)
        if not t_outer:
            for t in range(NT):
                evac(t, base + t)

    hy_rhs = lambda k, t: HY[:, k, tile(t)]
    hy_res = lambda k, t: [hyR[k][t]]

    kcol_of_tile = (1280, 256, 768)

    def gelu_to(dst_fn, src_ap, n, dres, sres_list, tb=0):
        a, w = TMPF[:, tb, 0:n], TMPF[:, tb + 1, 0:n]
        S.add("act", I("activation", out=a, in_=src_ap, func=AF.Square, scale=math.sqrt(0.044715)),
              reads=sres_list, writes=[tmpR[tb]])
        S.add("dve", I("scalar_tensor_tensor", out=w, in0=a, scalar=1.0, in1=src_ap, op0=ALU.add, op1=ALU.mult),
              reads=[tmpR[tb]] + sres_list, writes=[tmpR[tb + 1]])
        S.add("act", I("activation", out=a, in_=w, func=AF.Sigmoid, scale=1.5957691216057308),
              reads=[tmpR[tb + 1]], writes=[tmpR[tb]])
        S.add("dve", I("tensor_tensor", out=dst_fn, in0=src_ap, in1=a, op=ALU.mult),
              reads=[tmpR[tb]] + sres_list, writes=dres)

    def layer(l, ada_cur=iter(())):
        def ada_more(n):
            for _ in range(n):
                next(ada_cur, None)

        S.fence([r for hh in qR for r in hh] + [r for hh in kR for r in hh] + kcR + vR + [r for uu in uR for r in uu]
                + ppR + [paR, pbR])
        S.add("dve", I("memset", PP[:, :, :], 0.0), writes=ppR)

        for cc in range(2):
            S.add("sp", I("dma_start", out=STG[:, cc, 0:512], in_=ck[l, cc * 128:(cc + 1) * 128, :]),
                  writes=[stgR[cc]], dsem=stgD[cc])
            S.add("pool", I("dma_start", out=VT[:, cc, :], in_=cv[l, cc * 128:(cc + 1) * 128, :]),
                  writes=[vR[cc]], dsem=cvD[cc])
        for cc in range(2):
            for h in range(4):
                S.add("pe", I("transpose", PS[:, 6 + cc, h * 128:(h + 1) * 128],
                                                              STG[:, cc, h * 128:(h + 1) * 128], IDF[:]),
                      reads=[stgR[cc], constR], writes=[bankR[6 + cc]])
            S.add("act", I("activation", out=KT[:, :, cc * 128:(cc + 1) * 128],
                                                       in_=PS[:, 6 + cc, :].rearrange("p (h n) -> p h n", h=4), func=AF.Copy),
                  reads=[bankR[6 + cc]], writes=kcR)

        W = w_in[l]

        def ev_pool_of(j):
            def ev_pool(t, bk):
                if t == 0:
                    dst = PP[:, j, 8:552].rearrange("p (s w) -> p s w", w=272)[:, :, 0:256]
                    src = PS[:, bk, :].rearrange("p (s w) -> p s w", w=256)
                else:
                    dst = PP[:, j, 552 + (t - 1) * 512:552 + t * 512]
                    src = PS[:, bk, :]
                S.add("act", I("activation", out=dst, in_=src, func=AF.Copy), reads=[bankR[bk]], writes=[ppR[j]])
            return ev_pool

        def ev_qk(dstT, dres, h, colfn):
            def ev(t, bk):
                if t == 0:
                    S.add("act", I("activation", out=dstT[:, h, colfn(t):colfn(t) + 512], in_=PS[:, bk, :], func=AF.Copy),
                          reads=[bankR[bk]], writes=[dres[h][t]])
                    return
                b = t % 2
                rb = 6 + b
                cs = slice((t - 1) * 512, t * 512)
                S.add("act", I("activation", out=XB[:, b, :], in_=PS[:, bk, :], func=AF.Copy), reads=[bankR[bk]], writes=[xbR[b]])
                S.add("pe", I("matmul", PS[:, rb, :], PM[:], XB[:, b, :], start=True, stop=True),
                      reads=[xbR[b], constR], writes=[bankR[rb]])
                S.add("dve", I("tensor_tensor", out=TMPF[:, 2 * b, :], in0=PS[:, bk, :], in1=ROPEC[:, cs], op=ALU.mult),
                      reads=[bankR[bk], constR], writes=[tmpR[2 * b]])
                S.add("dve", I("tensor_tensor", out=TMPF[:, 2 * b + 1, :], in0=PS[:, rb, :], in1=ROPES[:, cs], op=ALU.mult),
                      reads=[bankR[rb], constR], writes=[tmpR[2 * b + 1]])
                S.add("dve", I("tensor_tensor", out=dstT[:, h, colfn(t):colfn(t) + 512], in0=TMPF[:, 2 * b, :],
                               in1=TMPF[:, 2 * b + 1, :], op=ALU.add),
                      reads=[tmpR[2 * b], tmpR[2 * b + 1]], writes=[dres[h][t]])
            return ev

        qcol = lambda t: t * 512
        kcol = lambda t: kcol_of_tile[t]
        def load_first():
            first = []
            for (c0, evs) in ((0, [ev_pool_of(0), ev_pool_of(1)]),
                              (256, [ev_qk(QT, qR, 0, qcol), ev_qk(QT, qR, 1, qcol)]),
                              (512, [ev_qk(QT, qR, 2, qcol), ev_qk(QT, qR, 3, qcol)]),
                              (768, [ev_qk(KT, kR, 0, kcol), ev_qk(KT, kR, 1, kcol)])):
                slot, sres = load_piece(wrows(W, c0))
                first.append((slot, sres, evs))
            return first

        def first_groups(first, t):
            for (slot, sres, evs) in first:
                for i, ev in enumerate(evs):
                    fm_one(slot, sres, i * 128, 8, t, ev)

        if l == 0:
            norm_seq(l, A1, 0, filler=lambda: ada_more(6))
            first = load_first()
            for t in range(NT):
                first_groups(first, t)
        else:
            first = load_first()
            norm_seq(l, A1, 0, after=lambda t: first_groups(first, t))
        CK(f"normA{l}")
        k0slot, k0res = first[3][0], first[3][1]

        def tokmajor_k(hp, slot, sres):
            for c in range(4):
                bk = 4 + (c % 4)
                for k in range(8):
                    S.add("pe", I("matmul", PS[:, bk, 0:256], HY[:, k, c * 128:(c + 1) * 128], slot[:, k, :], start=(k == 0), stop=(k == 7)),
                          reads=[sres, hyR[k][0]], writes=[bankR[bk]])
                b = c % 2
                S.add("act", I("activation", out=STG[:, b, 0:256], in_=PS[:, bk, 0:256], func=AF.Copy), reads=[bankR[bk]], writes=[stgR[b]])
                S.add("sp", I("dma_start", out=nk_o[c // 2, l, (c % 2) * 128:(c % 2) * 128 + 128, hp * 256:(hp + 1) * 256], in_=STG[:, b, 0:256]),
                      reads=[stgR[b]], writes=[outR], dsem=outD[b])

        tokmajor_k(0, k0slot, k0res)

        Wd = PADW

        def shift_add(dst, src, lo, hi, sh_a, sh_b, dres, sres_, p0=0, p1=128):
            S.add("dve", I("tensor_tensor", out=dst[p0:p1, lo:hi], in0=src[p0:p1, lo + sh_a:hi + sh_a],
                                                   in1=src[p0:p1, lo + sh_b:hi + sh_b], op=ALU.add),
                  reads=sres_, writes=dres)

        def pool_finish(j, lo_src, hi_src, lo_res, hi_res):
            for (p0, p1, src, sres_) in ((0, 64, lo_src, lo_res), (64, 128, hi_src, hi_res)):
                for s in range(3):
                    a0, t0, L = SEQ_PAD0[s], SEQ_TOK0[s], SEQ_LEN[s]
                    S.add("dve", I("scalar_tensor_tensor",
                        out=POOLED[p0:p1, j, t0:t0 + L], in0=src[p0:p1, a0:a0 + L], scalar=INVW[p0:p1, j:j + 1],
                        in1=PP[p0:p1, j, a0:a0 + L], op0=ALU.mult, op1=ALU.subtract),
                        reads=[sres_, ppR[j], constR], writes=[pooledR[j]])
                    for (ca, ea) in ((0, 0), (L - 8, 8)):
                        S.add("dve", I("tensor_tensor",
                            out=TMPF[p0:p1, 2, 0:8], in0=src[p0:p1, a0 + ca:a0 + ca + 8], in1=EDGE[p0:p1, j, ea:ea + 8], op=ALU.mult),
                            reads=[sres_, constR], writes=[tmpR[2]])
                        S.add("dve", I("tensor_tensor",
                            out=POOLED[p0:p1, j, t0 + ca:t0 + ca + 8], in0=TMPF[p0:p1, 2, 0:8],
                            in1=PP[p0:p1, j, a0 + ca:a0 + ca + 8], op=ALU.subtract),
                            reads=[tmpR[2], ppR[j]], writes=[pooledR[j]])

        shift_add(PA, PP[:, 0, :], 1, Wd, -1, 0, [paR], [ppR[0]])
        shift_add(PB, PA, 2, Wd - 1, -1, 1, [pbR], [paR], 64, 128)
        pool_finish(0, PA, PB, paR, pbR)
        shift_add(PA, PP[:, 1, :], 1, Wd, -1, 0, [paR], [ppR[1], pooledR[0]])
        shift_add(PB, PA, 2, Wd - 1, -1, 1, [pbR], [paR, pooledR[0]])
        shift_add(PA, PB, 4, Wd - 3, -2, 2, [paR], [pbR])
        shift_add(PB, PA, 8, Wd - 7, -4, 4, [pbR], [paR], 64, 128)
        pool_finish(1, PA, PB, paR, pbR)

        CK(f"pool{l}")
        slot, sres = load_piece(wrows(W, 768 + 256))
        for hh in range(2):
            fm_group(slot, sres, hh * 128, 8, hy_rhs, hy_res, ev_qk(KT, kR, 2 + hh, kcol))
        tokmajor_k(1, slot, sres)
        CK(f"qk{l}")
        grp = (
            dict(qa=0, qn=512, ka=1280, kn=512, qres=lambda h: [qR[h][0]], kres=lambda h: [kR[h][0]]),
            dict(qa=512, qn=1024, ka=0, kn=1280, qres=lambda h: [qR[h][1], qR[h][2]], kres=lambda h: [kcR[h], kR[h][1], kR[h][2]]),
        )

        def bounds_a(u):
            gi, h = u // 4, u % 4
            G = grp[gi]
            for (src, base, n, res, col, SQX, sqx) in ((QT, G["qa"], G["qn"], G["qres"](h), 0, SQ0, sqR[0]), (KT, G["ka"], G["kn"], G["kres"](h), 2, SQ1, sqR[1])):
                S.add("act", I("activation", out=SQX[:, 0:n], in_=src[:, h, base:base + n], func=AF.Square), reads=res, writes=[sqx])

        def bounds(u):
            gi, h = u // 4, u % 4
            G = grp[gi]
            for (src, base, n, res, col, SQX, sqx) in ((QT, G["qa"], G["qn"], G["qres"](h), 0, SQ0, sqR[0]), (KT, G["ka"], G["kn"], G["kres"](h), 2, SQ1, sqR[1])):
                n256 = n // 256
                for br, ones in ((0, OTOP), (1, OBOT)):
                    for i in range(n256):
                        slot_i = br * n256 + i
                        bk, half = slot_i // 2, slot_i % 2
                        S.add("pe", I("matmul", PS[:, bk, half * 256:(half + 1) * 256], ones[:], SQX[:, i * 256:(i + 1) * 256], start=True, stop=True),
                              reads=[sqx, miscR], writes=[bankR[bk]])
                nb = (2 * n256 + 1) // 2
                flat = PS[:, 0:nb, :].rearrange("p b n -> p (b n)")[:, 0:2 * n256 * 256]
                S.add("dve", I("tensor_reduce", out=NBS[:, u, col:col + 2], in_=flat.rearrange("p (b n) -> p b n", b=2), axis=AX.X, op=ALU.max),
                      reads=[bankR[i] for i in range(nb)], writes=[nbsR[u]])
            S.add("dve", I("tensor_tensor", out=NBS[:, u, 4:6], in0=NBS[:, u, 0:2], in1=NBS[:, u, 2:4], op=ALU.mult), reads=[nbsR[u]], writes=[nbsR[u]])
            S.add("act", I("activation", out=NBS[:, u, 6:8], in_=NBS[:, u, 4:6], func=AF.Ln, scale=1.0 / 64.0), reads=[nbsR[u]], writes=[nbsR[u]])
            S.add("act", I("activation", out=NBS[:, u, 4:6], in_=NBS[:, u, 6:8], func=AF.Exp, scale=0.5), reads=[nbsR[u]], writes=[nbsR[u]])
            S.add("dve", I("tensor_scalar", out=NBS[:, u, 6:8], in0=NBS[:, u, 4:6], scalar1=-1.0, scalar2=None, op0=ALU.mult),
                  reads=[nbsR[u]], writes=[nbsR[u]])
            S.add("dve", I("tensor_tensor", out=NBA[:, 2 * u:2 * u + 1], in0=NBS[:, u, 6:7], in1=NBS[:, u, 7:8], op=ALU.min),
                  reads=[nbsR[u]], writes=[nbaR[u]])

        bseq = iter(range(8))

        def bstep():
            u = next(bseq, None)
            if u is not None:
                bounds(u)
            if u is not None and u + 1 < 8:
                bounds_a(u + 1)

        bounds_a(0)
        for hp in range(2):
            slot, sres = load_piece(wrows(W, 1280 + hp * 256))
            for c in range(12):
                if c % 6 == 3:
                    bstep()
                bk = 4 + (c % 4)
                t = c // 4
                for k in range(8):
                    S.add("pe", I("matmul", PS[:, bk, 0:256], HY[:, k, c * 128:(c + 1) * 128], slot[:, k, :],
                                                                             start=(k == 0), stop=(k == 7)),
                          reads=[sres, hyR[k][t]], writes=[bankR[bk]])
                vc = 10 + c if c < 4 else c - 2
                S.add("dve", I("tensor_copy", out=VT[:, vc, hp * 256:(hp + 1) * 256], in_=PS[:, bk, 0:256]),
                      reads=[bankR[bk]], writes=[vR[vc]])
                if c < 4:
                    b = c % 2
                    S.add("act", I("activation", out=STG[:, b, 0:256], in_=PS[:, bk, 0:256], func=AF.Copy),
                          reads=[bankR[bk]], writes=[stgR[b]])
                    S.add("sp", I("dma_start", out=nv_o[c // 2, l, (c % 2) * 128:(c % 2) * 128 + 128, hp * 256:(hp + 1) * 256],
                                                                     in_=STG[:, b, 0:256]),
                          reads=[stgR[b]], writes=[outR], dsem=outD[b])

        CK(f"v{l}")
        bstep()
        slot, sres = load_piece(wrows(W, 1792))
        for j in range(2):
            def ev_u(t, bk, j=j):
                S.add("act", I("activation", out=UT[:, j, tile(t)], in_=PS[:, bk, :], func=AF.Gelu_apprx_tanh), reads=[bankR[bk]], writes=[uR[j][t]])
            fm_group(slot, sres, j * 128, 8, hy_rhs, hy_res, ev_u)

        CK(f"u{l}")
        bstep()
        S.transfer(ppR + [paR, pbR], vnpR + ebR + [attR])
        S.add("pool", I("memset", VNP[:, :, :, :, :], 0.0), writes=vnpR)

        slot, sres = load_piece(wrows(W, 2048))
        def sgv_mm(t):
            vb = t % 2
            for cc in range(4):
                c = t * 4 + cc
                bk = 6 + cc // 2
                co = (cc % 2) * 256
                for k in range(8):
                    S.add("pe", I("matmul", PS[:, bk, co:co + 256], HY[:, k, c * 128:(c + 1) * 128], slot[:, k, :],
                                  start=(k == 0), stop=(k == 7)),
                          reads=[sres, hyR[k][t]], writes=[bankR[bk]])

        def sgv_chain(t):
            vb = t % 2
            src = PS[:, 6:8, :].rearrange("p b n -> p (b n)")
            ga = TMPF[:, 0:2, :].rearrange("p b n -> p (b n)")
            gv = TMPF[:, 2:4, :].rearrange("p b n -> p (b n)")
            bsrc = [bankR[6], bankR[7]]
            S.add("act", I("activation", out=gv, in_=src, func=AF.Gelu_apprx_tanh), reads=bsrc, writes=[tmpR[2], tmpR[3]])
            for cc in range(4):
                S.add("dve", I("bn_stats", out=ST6[:, cc * 6:(cc + 1) * 6], in_=gv[:, cc * 256:(cc + 1) * 256]), reads=[tmpR[2], tmpR[3]], writes=[st6R])
            for cc in range(4):
                S.add("dve", I("bn_aggr", out=ST6[:, 24 + cc * 2:26 + cc * 2], in_=ST6[:, cc * 6:(cc + 1) * 6]), reads=[st6R], writes=[st6R])
            mv = ST6[:, 24:32].rearrange("p (c two) -> p c two", two=2)
            S.add("act", I("activation", out=ST6[:, 32:36], in_=mv[:, :, 1], func=AF.Ln, bias=EPSB[:]), reads=[st6R, miscR], writes=[st6R])
            S.add("act", I("activation", out=ST6[:, 36:40], in_=ST6[:, 32:36], func=AF.Exp, scale=-0.5), reads=[st6R], writes=[st6R])
            gv3 = gv.rearrange("p (c n) -> p c n", c=4)
            S.add("dve", I("tensor_tensor", out=gv3, in0=gv3, in1=mv[:, :, 0:1].to_broadcast([128, 4, 256]), op=ALU.subtract),
                  reads=[st6R, tmpR[2], tmpR[3]], writes=[tmpR[2], tmpR[3]])
            S.add("dve", I("tensor_tensor", out=gv3, in0=gv3, in1=ST6[:, 36:40].unsqueeze(2).to_broadcast([128, 4, 256]), op=ALU.mult),
                  reads=[st6R, tmpR[2], tmpR[3]], writes=[tmpR[2], tmpR[3]])
            for par in range(2):
                dst = VNP[:, vb, :, :, :].rearrange("p c (a b) n -> p c a b n", b=2)[:, :, :, par, par * 64:par * 64 + 64]
                srcv = gv.rearrange("p (c a b n) -> p c a b n", c=4, a=2, b=2)[:, :, :, par, :]
                gn = GN[:, l, :].rearrange("p (a b n) -> p a b n", a=2, b=2)[:, :, par, :].unsqueeze(1).to_broadcast([128, 4, 2, 64])
                S.add("dve", I("tensor_tensor", out=dst, in0=srcv, in1=gn, op=ALU.mult), reads=[tmpR[2], tmpR[3], constR], writes=[vnpR[vb]])

        def sgv_gate(t):
            vb = t % 2
            for j in range(2):
                bkk = 3 * dstate["set"] + j
                for cc in range(4):
                    for gg in range(2):
                        g = 2 * j + gg
                        S.add("pe", I("matmul",
                            PS[:, bkk, cc * 128:(cc + 1) * 128], VNP[:, vb, cc, g, :], WST[:, l * 4 + g, :],
                            start=(gg == 0), stop=(gg == 1)),
                            reads=[vnpR[vb], constR], writes=[bankR[bkk]])
                S.add("dve", I("tensor_tensor",
                    out=TMPF[:, j, :].rearrange("p (c n) -> p c n", c=4), in0=PS[:, bkk, :].rearrange("p (c n) -> p c n", c=4),
                    in1=BSG[:, l * 2 + j, :].unsqueeze(1).to_broadcast([128, 4, 128]), op=ALU.add),
                    reads=[bankR[bkk], constR], writes=[tmpR[j]])
                S.add("dve", I("tensor_tensor", out=HY[:, 6 + j, tile(t)], in0=TMPF[:, j, :], in1=UT[:, j, tile(t)], op=ALU.mult),
                      reads=[tmpR[j], uR[j][t]], writes=[hyR[6 + j][t]])
            dstate["set"] ^= 1


        sgv_mm(0)
        for t in range(NT):
            sgv_chain(t)
            if t + 1 < NT:
                sgv_mm(t + 1)
            sgv_gate(t)

        CK(f"sgu{l}")
        for j in range(2):
            for t in range(NT):
                bk = 3 * dstate["set"] + t
                S.add("pe", I("matmul", PS[:, bk, :], WPBD[:, l * 2 + j, :], POOLED[:, j, tile(t)], start=True, stop=True),
                      reads=[pooledR[j], constR], writes=[bankR[bk]])
                S.add("act", I("activation", out=HY[:, j, tile(t)], in_=PS[:, bk, :], func=AF.Copy,
                                                                    scale=SVT[:, 56 + 2 * l + j:57 + 2 * l + j]),
                      reads=[bankR[bk], svR], writes=[hyR[j][t]])
            dstate["set"] ^= 1

        CK(f"poolmix{l}")
        for _ in range(8):
            bstep()
        units = []
        for h in range(4):
            units.append(dict(h=h, u=h, nkc=2, ytile=0, ycols=slice(0, 512), qres=[qR[h][0]], kres=[kR[h][0]],
                              halves=[(0, 256, 0, 1280, [10, 11]), (256, 256, 256, 1536, [12, 13])], copies=False))
        for h in range(4):
            for qi in range(2):
                units.append(dict(h=h, u=4 + h, nkc=10, ytile=1 + qi, ycols=slice(512 + qi * 512, 1024 + qi * 512), qres=[qR[h][1 + qi]],
                                  kres=[kcR[h], kR[h][1], kR[h][2]],
                                  halves=[(0, 512, 512 + qi * 512, 0, list(range(10)))], copies=True))
        B_O1, B_O2, B_Z1, B_Z2 = 4, 5, 6, 7
        T = lambda i: TMPF[:, i, :]

        def stage1(U):
            h = U["h"]
            S.add("dve", I("tensor_copy", out=T(0), in_=PS[:, B_Z1, :]), reads=[bankR[B_Z1]], writes=[tmpR[0]])
            S.add("act", I("activation", out=T(2), in_=PS[:, B_Z2, :], func=AF.Copy), reads=[bankR[B_Z2]], writes=[tmpR[2]])
            S.add("dve", I("tensor_copy", out=T(1), in_=PS[:, B_O1, :]), reads=[bankR[B_O1]], writes=[tmpR[1]])
            S.add("act", I("activation", out=T(3), in_=PS[:, B_O2, :], func=AF.Copy), reads=[bankR[B_O2]], writes=[tmpR[3]])
            S.add("dve", I("reciprocal", out=T(0), in_=T(0)), reads=[tmpR[0]], writes=[tmpR[0]])
            S.add("dve", I("tensor_tensor", out=T(1), in0=T(1), in1=T(0), op=ALU.mult), reads=[tmpR[0], tmpR[1]], writes=[tmpR[1]])
            if U["copies"]:
                S.add("dve", I("reciprocal", out=T(2), in_=T(2)), reads=[tmpR[2]], writes=[tmpR[2]])
            else:
                S.add("act", I("activation", out=T(2), in_=T(2), func=AF.Ln), reads=[tmpR[2]], writes=[tmpR[2]])
                S.add("act", I("activation", out=T(2), in_=T(2), func=AF.Exp, scale=-1.0), reads=[tmpR[2]], writes=[tmpR[2]])
            S.add("dve", I("tensor_tensor", out=T(3), in0=T(3), in1=T(2), op=ALU.mult), reads=[tmpR[2], tmpR[3]], writes=[tmpR[3]])
            S.add("dve", I("scalar_tensor_tensor", out=T(1), in0=T(3), scalar=LAM[:, 8 + l:9 + l], in1=T(1), op0=ALU.mult, op1=ALU.add),
                  reads=[tmpR[3], tmpR[1], miscR], writes=[tmpR[1]])

        def stage2(U):
            h = U["h"]
            nbk = 3
            S.add("act", I("activation", out=SQ1[:, 0:512], in_=T(1), func=AF.Square), reads=[tmpR[1]], writes=[sqR[1]])
            S.add("pe", I("matmul", PS[:, nbk, :], ONES[:], SQ1[:, 0:512], start=True, stop=True), reads=[sqR[1], miscR], writes=[bankR[nbk]])
            S.add("act", I("activation", out=T(0), in_=PS[:, nbk, :], func=AF.Ln, scale=1.0 / 128.0, bias=EPSB[:]),
                  reads=[bankR[nbk], miscR], writes=[tmpR[0]])
            S.add("act", I("activation", out=T(0), in_=T(0), func=AF.Exp, scale=-0.5), reads=[tmpR[0]], writes=[tmpR[0]])
            S.add("dve", I("scalar_tensor_tensor", out=HY[:, 2 + h, U["ycols"]], in0=T(1), scalar=GSUB[:, l:l + 1], in1=T(0), op0=ALU.mult, op1=ALU.mult),
                  reads=[tmpR[1], tmpR[0], miscR], writes=[hyR[2 + h][U["ytile"]]])

        pending = None
        for U in units:
            h, nkc, u = U["h"], U["nkc"], U["u"]

            def scores(kc):
                sbk = (kc % 2) * 2
                for (o0, w, qcol, kcol, vch) in U["halves"]:
                    kcs = slice(kcol + kc * 128, kcol + (kc + 1) * 128)
                    qcs = slice(qcol, qcol + w)
                    S.add("pe", I("matmul", PS[:, sbk, o0:o0 + w], KT[0:64, h, kcs], QT[0:64, h, qcs], start=True, stop=True),
                          reads=U["kres"] + U["qres"], writes=[bankR[sbk]])
                    S.add("pe", I("matmul", PS[:, sbk + 1, o0:o0 + w], KT[64:128, h, kcs], QT[64:128, h, qcs], start=True, stop=True),
                          reads=U["kres"] + U["qres"], writes=[bankR[sbk + 1]])
                for br in range(2):
                    eb = (kc % 4) * 2 + br
                    S.add("act", I("activation", out=EB[:, eb, :], in_=PS[:, sbk + br, :], func=AF.Exp, scale=0.125, bias=NBA[:, 2 * u:2 * u + 1]),
                          reads=[bankR[sbk + br], nbaR[u]], writes=[ebR[eb]])

            def pv(kc):
                for br, (bo, bz) in enumerate(((B_O1, B_Z1), (B_O2, B_Z2))):
                    eb = (kc % 4) * 2 + br
                    for hi_, (o0, w, qcol, kcol, vch) in enumerate(U["halves"]):
                        vc = vch[kc]
                        S.add("pe", I("matmul", PS[:, bo, o0:o0 + w], VT[:, vc, h * 128:(h + 1) * 128], EB[:, eb, o0:o0 + w],
                                      start=(kc == 0 and hi_ == 0), stop=(kc == nkc - 1 and hi_ == len(U["halves"]) - 1)),
                              reads=[vR[vc], ebR[eb]], writes=[bankR[bo]])
                    S.add("pe", I("matmul", PS[:, bz, :], ONES[:], EB[:, eb, :], start=(kc == 0), stop=(kc == nkc - 1)),
                          reads=[miscR, ebR[eb]], writes=[bankR[bz]])
                    if 0 < kc < nkc - 1 and br == 0:
                        for _ in range(NWARM):
                            S.add("pe", I("matmul", PS[:, bo, 0:128], OBOT[64:128, :], OTOP[64:128, :], start=False, stop=False),
                                  reads=[miscR], writes=[bankR[bo]])

            flush_at = min(5, nkc - 1)
            scores(0)
            for kc in range(nkc):
                if kc + 1 < nkc:
                    scores(kc + 1)
                pv(kc)
                if kc == flush_at and pending is not None:
                    stage2(pending)
                    pending = None
            stage1(U)
            pending = U
        stage2(pending)


        ada_more(24)
        CK(f"attn{l}")
        def resid_evac(gi):
            def mk(m):
                def ev(t, bk):
                    c = cond_of[t]
                    S.add("dve", I("scalar_tensor_tensor", out=X[:, m, tile(t)], in0=PS[:, bk, :], scalar=modap(l, gi, m, c),
                                                                  in1=X[:, m, tile(t)], op0=ALU.mult, op1=ALU.add),
                          reads=[bankR[bk], modR[l], xR[m][t]], writes=[xR[m][t]])
                return ev
            return mk

        for pc in range(4):
            slot, sres = load_piece(wrows(w_out[l], pc * 256))
            for mm in range(2):
                fm_group(slot, sres, mm * 128, 8, hy_rhs, hy_res, resid_evac(2)(pc * 2 + mm), t_outer=(pc == 3 and mm == 1))

        CK(f"wout{l}")
        ada_it = ada_steps(l + 1) if l + 1 < 2 else iter(())

        def ada_next(n):
            for _ in range(n):
                next(ada_it, None)
        S.fence([r for jj in actR for r in jj])

        def ffn_evac(j, t, bg, bu):
            b = t % 2
            S.add("act", I("activation", out=XB[:, b, :], in_=PS[:, bg, :], func=AF.Silu), reads=[bankR[bg]], writes=[xbR[b]])
            S.add("dve", I("tensor_tensor", out=ACTB[:, j, tile(t)], in0=PS[:, bu, :], in1=XB[:, b, :], op=ALU.mult),
                  reads=[bankR[bu], xbR[b]], writes=[actR[j][t]])

        def load_fpairs():
            return [(jp, load_piece(wrows(w_fi[l], jp * 256)), load_piece(wrows(w_fi[l], DFF + jp * 256))) for jp in range(2)]

        def ffn_first(t):
            for (jp, (gslot, gres), (uslot, ures)) in fpairs:
                for jj in range(2):
                    j = jp * 2 + jj
                    bg = fb["i"] % 6
                    bu = (fb["i"] + 1) % 6
                    fb["i"] += 2
                    for k in range(8):
                        S.add("pe", I("matmul", PS[:, bg, :], gslot[:, k, jj * 128:(jj + 1) * 128], HY[:, k, tile(t)], start=(k == 0), stop=(k == 7)),
                              reads=[gres, hyR[k][t]], writes=[bankR[bg]])
                    for k in range(8):
                        S.add("pe", I("matmul", PS[:, bu, :], uslot[:, k, jj * 128:(jj + 1) * 128], HY[:, k, tile(t)], start=(k == 0), stop=(k == 7)),
                              reads=[ures, hyR[k][t]], writes=[bankR[bu]])
                    ffn_evac(j, t, bg, bu)

        if l == 0:
            norm_seq(l, A2, 3, filler=lambda: ada_next(4))
            fpairs = load_fpairs()
            for t in range(NT):
                ffn_first(t)
        else:
            fpairs = load_fpairs()
            norm_seq(l, A2, 3, after=ffn_first)
        CK(f"norm2{l}")
        for jp in range(2, 11):
            ada_next(2 if jp in (2, 3, 10) else 1)
            gslot, gres = load_piece(wrows(w_fi[l], jp * 256))
            uslot, ures = load_piece(wrows(w_fi[l], DFF + jp * 256))
            for jj in range(2):
                j = jp * 2 + jj
                for k in range(8):
                    for t in range(NT):
                        S.add("pe", I("matmul", PS[:, t, :], gslot[:, k, jj * 128:(jj + 1) * 128], HY[:, k, tile(t)], start=(k == 0), stop=(k == 7)),
                              reads=[gres, hyR[k][t]], writes=[bankR[t]])
                for k in range(8):
                    for t in range(NT):
                        S.add("pe", I("matmul", PS[:, 3 + t, :], uslot[:, k, jj * 128:(jj + 1) * 128], HY[:, k, tile(t)], start=(k == 0), stop=(k == 7)),
                              reads=[ures, hyR[k][t]], writes=[bankR[3 + t]])
                for t in range(NT):
                    ffn_evac(j, t, t, 3 + t)

        CK(f"ffnin{l}")
        for _ in ada_it:
            pass
        for pc in range(4):
            slots = []
            for kg, (kb, nk) in enumerate(((0, 8), (8, 8), (16, 6))):
                slots.append(load_piece(wrows(w_fo[l], pc * 256, k0=kb, nk=nk), nk=nk) + (kb, nk))
            if pc == 3:
                for t in range(NT):
                    for mm in range(2):
                        m = pc * 2 + mm
                        bk = 3 * mm + t
                        for (slot, sres, kb, nk) in slots:
                            for k in range(nk):
                                kk = kb + k
                                S.add("pe", I("matmul", PS[:, bk, :], slot[:, k, mm * 128:(mm + 1) * 128], ACTB[:, kk, tile(t)],
                                              start=(kk == 0), stop=(kk == NFF - 1)),
                                      reads=[sres, actR[kk][t]], writes=[bankR[bk]])
                        resid_evac(5)(m)(t, bk)
                continue
            for mm in range(2):
                m = pc * 2 + mm
                base = 3 * dstate["set"]
                dstate["set"] ^= 1
                if m == 7:
                    for t in range(NT):
                        for (slot, sres, kb, nk) in slots:
                            for k in range(nk):
                                kk = kb + k
                                S.add("pe", I("matmul", PS[:, base + t, :], slot[:, k, mm * 128:(mm + 1) * 128], ACTB[:, kk, tile(t)],
                                              start=(kk == 0), stop=(kk == NFF - 1)),
                                      reads=[sres, actR[kk][t]], writes=[bankR[base + t]])
                        resid_evac(5)(m)(t, base + t)
                    continue
                for (slot, sres, kb, nk) in slots:
                    for k in range(nk):
                        kk = kb + k
                        for t in range(NT):
                            S.add("pe", I("matmul",
                                PS[:, base + t, :], slot[:, k, mm * 128:(mm + 1) * 128], ACTB[:, kk, tile(t)],
                                start=(kk == 0), stop=(kk == NFF - 1)),
                                reads=[sres, actR[kk][t]], writes=[bankR[base + t]])
                for t in range(NT):
                    resid_evac(5)(m)(t, base + t)

    def final_phase():
        S.fence(xnbRs)

        def head(t):
            XNB, xnbR = XNBS[t % 2], xnbRs[t % 2]
            rb, rr = rms_stats(t)
            for k in range(8):
                S.add("dve", I("scalar_tensor_tensor", out=XNB[:, k, :], in0=X[:, k, tile(t)], scalar=SVT[:, 48 + k:49 + k],
                               in1=rb, op0=ALU.mult, op1=ALU.mult),
                      reads=[xR[k][t], rr, svR], writes=[xnbR])

        def tail(t):
            XNB, xnbR = XNBS[t % 2], xnbRs[t % 2]
            for cc in range(4):
                b = cc % 2
                for half in range(2):
                    bkk = 2 * b + half
                    for kk in range(4):
                        k = half * 4 + kk
                        S.add("pe", I("transpose", PS[:, bkk, kk * 128:(kk + 1) * 128], XNB[:, k, cc * 128:(cc + 1) * 128], IDF[:]),
                              reads=[xnbR, constR], writes=[bankR[bkk]])
                    if half == 0:
                        S.add("dve", I("tensor_copy", out=STG[:, b, half * 512:(half + 1) * 512], in_=PS[:, bkk, :]),
                              reads=[bankR[bkk]], writes=[stgR[b]])
                    else:
                        S.add("act", I("activation", out=STG[:, b, half * 512:(half + 1) * 512], in_=PS[:, bkk, :], func=AF.Copy),
                              reads=[bankR[bkk]], writes=[stgR[b]])
                c = t * 4 + cc
                S.add("sp", I("dma_start", out=yout[c * 128:(c + 1) * 128, :], in_=STG[:, b, :]),
                      reads=[stgR[b]], writes=[outR], dsem=outD[b])

        head(0)
        for t in range(NT):
            if t + 1 < NT:
                head(t + 1)
            tail(t)


    try:
        if stop_early:
            raise _Stop()
        CK("xload")
        ada0_it = ada_steps(0)
        for _ in range(8):
            next(ada0_it, None)
        CK("ada0")
        layer(0, ada0_it)
        CK("L0")
        layer(1)
        CK("L1")
        final_phase()
    except _Stop:
        pass

    def _unused():
        pass

    if taps:
        tapD = S.dsem("dtap")
        tapsrc = {"X": (X[:], [xR[k][t] for k in range(8) for t in range(NT)]),
                  "HY": (HY[:], [hyR[k][t] for k in range(8) for t in range(NT)])}
        for name in taps:
            ap_, res_ = tapsrc[name]
            S.add("sp", I("dma_start", out=tap_out[name], in_=ap_), reads=res_, writes=[outR], dsem=tapD)

    final_waits = [d for d in S.dsems if d.count > 0]

    S.finalize_counts()
    with nc.Block() as block:
        @block.tensor
        def _(e):
            S.emit("pe", e)

        @block.scalar
        def _(e):
            S.emit("act", e)

        @block.vector
        def _(e):
            S.emit("dve", e)

        @block.gpsimd
        def _(e):
            S.emit("pool", e)

        @block.sync
        def _(e):
            S.emit("sp", e)
            for d in final_waits:
                e.wait_ge(d.sem, d.count)
    return nc


def _consts():
    ident = np.eye(128, dtype=np.float32)
    m = np.arange(128)
    partner = np.where((m % 32) < 16, m + 16, m - 16)
    pm = np.zeros((128, 128), np.float32)
    pm[partner, m] = 1.0
    t = np.arange(1024)
    rows = (t // 64).astype(np.float32)
    cols = (t % 64).astype(np.float32)
    inv = (1.0 / (10000.0 ** (np.arange(0, 32, 2, dtype=np.float32) / 32))).astype(np.float32)
    ar, ac = rows[:, None] * inv[None], cols[:, None] * inv[None]
    cr, sr, cc_, sc_ = np.cos(ar), np.sin(ar), np.cos(ac), np.sin(ac)
    C = np.zeros((128, 1024), np.float32)
    Sg = np.zeros((128, 1024), np.float32)
    for p in range(128):
        i = p % 16
        is_col = (p % 64) >= 32
        sign = -1.0 if (p % 32) < 16 else 1.0
        C[p] = (cc_ if is_col else cr)[:, i]
        Sg[p] = sign * (sc_ if is_col else sr)[:, i]
    invw = np.zeros((128, 2), np.float32)
    edge = np.zeros((128, 2, 16), np.float32)
    for j in range(2):
        for half in range(2):
            w = POOL_WINDOWS[2 * j + half]
            ps = slice(half * 64, half * 64 + 64)
            invw[ps, j] = 1.0 / w
            L = 1024
            tt = np.arange(L)
            lo = np.clip(tt - w // 2, 0, L)
            hi = np.clip(tt + w - w // 2, 0, L)
            cnt = (hi - lo).astype(np.float32)
            edge[ps, j, 0:8] = (1.0 / cnt[0:8])[None, :]
            edge[ps, j, 8:16] = (1.0 / cnt[L - 8:L])[None, :]
    return ident, pm, C.astype(np.float32), Sg.astype(np.float32), invw, edge


_NC_CACHE = {}


def kernel(x_prompt, x_sample, cache_k, cache_v, c, c_ctx, norm1_g, w_ada, b_ada, w_in,
           w_pool, pool_scale, lam_q1, lam_k1, lam_q2, lam_k2, subln_g, sgu_norm_g,
           w_sgu, b_sgu, w_out, norm2_g, w_ffn_in, w_ffn_out, final_g, _taps=None, _ncores=8, _stop=None):
    f = lambda a: np.ascontiguousarray(np.asarray(a, dtype=np.float32))
    x_prompt, x_sample, cache_k, cache_v = f(x_prompt), f(x_sample), f(cache_k), f(cache_v)
    ident, pm, ropec, ropes, invw, edge = _consts()
    shared = {
        "w_ada": f(w_ada), "w_in": f(w_in), "w_out": f(w_out), "w_ffn_in": f(w_ffn_in), "w_ffn_out": f(w_ffn_out),
        "w_pool": f(w_pool), "w_sT": np.ascontiguousarray(np.transpose(f(w_sgu), (0, 1, 3, 2))),
        "b_sgu": f(b_sgu), "sgu_norm_g": f(sgu_norm_g),
        "lamv": np.ascontiguousarray(np.stack([f(lam_q1), f(lam_k1), f(lam_q2), f(lam_k2)], axis=1).reshape(1, 512)),
        "c_ident": ident, "c_pm": pm, "c_ropec": ropec, "c_ropes": ropes, "c_invw": invw, "c_edge": edge,
    }
    key = (tuple(sorted(_taps.items())) if _taps else None, _stop)
    if key not in _NC_CACHE:
        _NC_CACHE[key] = build_program(_taps, _stop)
    nc = _NC_CACHE[key]
    in_maps = []
    for i in range(_ncores):
        sv = np.zeros((2, 128, 128), np.float32)
        sv[0, 0:8] = f(c_ctx).reshape(8, 128)
        sv[0, 8:16] = f(c)[i].reshape(8, 128)
        sv[0, 16:32] = f(norm1_g).reshape(16, 128)
        sv[0, 32:48] = f(norm2_g).reshape(16, 128)
        sv[0, 48:56] = f(final_g).reshape(8, 128)
        sv[0, 56:60] = f(pool_scale).reshape(4, 128)
        sv[0, 60:62] = f(subln_g).reshape(2, 128)
        sv[1, 0:96] = f(b_ada).reshape(96, 128)
        m = dict(shared)
        m["xin"] = np.ascontiguousarray(np.concatenate([x_prompt[2 * i], x_prompt[2 * i + 1], x_sample[i]], axis=0))
        m["ck"] = np.ascontiguousarray(cache_k[i].reshape(2, 256, 512))
        m["cv"] = np.ascontiguousarray(cache_v[i].reshape(2, 256, 512))
        m["smallvec"] = sv
        in_maps.append(m)
    res = run_bass_kernel_spmd(nc, in_maps, core_ids=list(range(_ncores)))
    outs = res.results
    nb = 2 * _ncores
    y_prompt = np.zeros((nb, 256, D), np.float32)
    y_sample = np.zeros((_ncores, 1024, D), np.float32)
    nk = np.zeros((nb, 2, 256, 4, 128), np.float32)
    nv = np.zeros((nb, 2, 256, 4, 128), np.float32)
    for i in range(_ncores):
        y = outs[i]["yout"]
        y_prompt[2 * i] = y[0:256]
        y_prompt[2 * i + 1] = y[256:512]
        y_sample[i] = y[512:1536]
        nk[2 * i:2 * i + 2] = outs[i]["nk"].reshape(2, 2, 256, 4, 128)
        nv[2 * i:2 * i + 2] = outs[i]["nv"].reshape(2, 2, 256, 4, 128)
    if _taps:
        kernel._taps_out = [{k: outs[i]["tap_" + k] for k in _taps} for i in range(_ncores)]
    return (y_prompt, y_sample, nk, nv)
```

```python
import math
import numpy as np
import concourse.bass as bass
import concourse.mybir as mybir
from concourse.bass_utils import run_bass_kernel_spmd

F32 = mybir.dt.float32
BF16 = mybir.dt.bfloat16
AF = mybir.ActivationFunctionType
ALU = mybir.AluOpType
AX = mybir.AxisListType

D = 1024
NTOK = 1536
NT = 3
DFF = 2816
NFF = 22
EPS = 1e-6
POOL_WINDOWS = (2, 4, 8, 16)
PADW = 1584
SEQ_PAD0 = (8, 280, 552)
SEQ_TOK0 = (0, 256, 512)
SEQ_LEN = (256, 256, 1024)
RING_SLOTS = 4
NWARM = 0
PIECE = 256


class Res:
    __slots__ = ("name", "w", "r", "excl")

    def __init__(self, name, excl=False):
        self.name = name
        self.w = None
        self.r = []
        self.excl = excl


class DSem:
    def __init__(self, nc, name):
        self.sem = nc.alloc_semaphore(name)
        self.count = 0
        self.last = None


class Op:
    __slots__ = ("eng", "idx", "fn", "waits", "sig", "dsem", "dcount", "sigcount")


class Sched:
    ENG = ("pe", "act", "dve", "pool", "sp")

    def __init__(self, nc):
        self.nc = nc
        self.ops = {e: [] for e in self.ENG}
        self.waited = {e: {} for e in self.ENG}
        self.sems = {e: nc.alloc_semaphore("sem_" + e) for e in ("pe", "act", "dve", "pool")}
        self.dsems = []
        self.also = {}

    def dsem(self, name):
        d = DSem(self.nc, name)
        self.dsems.append(d)
        return d

    def add(self, eng, fn, reads=(), writes=(), dsem=None):
        op = Op()
        op.eng = eng
        op.fn = fn
        op.idx = len(self.ops[eng])
        op.sig = False
        op.dsem = dsem
        op.dcount = 0
        op.sigcount = 0
        if dsem is not None:
            dsem.count += 16
            op.dcount = dsem.count
            dsem.last = op
        reads = list(reads)
        for r in list(reads):
            if r in self.also:
                reads.extend(self.also[r])
        deps = []
        for r in reads:
            if r.w is not None:
                deps.append(r.w)
            if r.excl:
                deps.extend(x for x in r.r if x.eng != eng)
        for w in writes:
            if w.w is not None:
                deps.append(w.w)
            deps.extend(w.r)
        best = {}
        for d in deps:
            if d is op:
                continue
            if d.dsem is not None:
                key, order = d.dsem, d.dcount
            else:
                if d.eng == "pe" and eng == "pe" and dsem is None:
                    continue
                key, order = d.eng, d.idx
            if key not in best or best[key][0] < order:
                best[key] = (order, d)
        op.waits = []
        wd = self.waited[eng]
        for key, (order, d) in best.items():
            if wd.get(key, -1) >= order:
                continue
            wd[key] = order
            d.sig = True
            op.waits.append(d)
        for r in reads:
            r.r.append(op)
        for w in writes:
            w.w = op
            w.r = []
        self.ops[eng].append(op)
        return op

    def fence(self, news):
        lst = []
        for e in self.ENG:
            if self.ops[e]:
                for o in reversed(self.ops[e]):
                    if o.dsem is None:
                        lst.append(o)
                        break
        for d in self.dsems:
            if d.last is not None:
                lst.append(d.last)
        for r in news:
            r.w = None
            r.r = list(lst)

    def transfer(self, olds, news):
        lst = []
        for o in olds:
            if o.w is not None:
                lst.append(o.w)
            lst.extend(o.r)
        for r in news:
            r.w = None
            r.r = list(lst)

    def emit(self, eng, e):
        cnt = 0
        for op in self.ops[eng]:
            for d in op.waits:
                if d.dsem is not None:
                    e.wait_ge(d.dsem.sem, d.dcount)
                else:
                    e.wait_ge(self.sems[d.eng], d.sigcount)
            ins = op.fn(e)
            if op.dsem is not None:
                ins.then_inc(op.dsem.sem, 16)
            elif op.sig:
                ins.then_inc(self.sems[eng], 1)

    def finalize_counts(self):
        for eng in self.ENG:
            c = 0
            for op in self.ops[eng]:
                if op.dsem is None and op.sig:
                    c += 1
                op.sigcount = c


def I(name, *a, **kw):
    return lambda e: getattr(e, name)(*a, **kw)


def lam_init(l):
    return 0.8 - 0.6 * math.exp(-0.3 * l)


class _Stop(Exception):
    pass


def build_program(taps=None, stop=None):
    nc = bass.Bass("TRN2", target_bir_lowering=False)
    S = Sched(nc)

    def CK(name):
        if stop == name:
            raise _Stop()

    def din(name, shape):
        return nc.dram_tensor(name, list(shape), F32, kind="ExternalInput").ap()

    def dout(name, shape):
        return nc.dram_tensor(name, list(shape), F32, kind="ExternalOutput").ap()

    xin = din("xin", [NTOK, D])
    ck = din("ck", [2, 256, 512])
    cv = din("cv", [2, 256, 512])
    w_ada = din("w_ada", [2, D, 6 * D])
    w_in = din("w_in", [2, D, 2304])
    w_out = din("w_out", [2, D, D])
    w_fi = din("w_ffn_in", [2, D, 2 * DFF])
    w_fo = din("w_ffn_out", [2, DFF, D])
    w_pool = din("w_pool", [2, 4, 64, 64])
    w_sT = din("w_sT", [2, 4, 128, 128])
    b_sgu = din("b_sgu", [2, 4, 128])
    sgn = din("sgu_norm_g", [2, 256])
    lamv = din("lamv", [1, 512])
    sv_d = din("smallvec", [2, 128, 128])
    c_ident = din("c_ident", [128, 128])
    c_pm = din("c_pm", [128, 128])
    c_ropec = din("c_ropec", [128, 1024])
    c_ropes = din("c_ropes", [128, 1024])
    c_invw = din("c_invw", [128, 2])
    c_edge = din("c_edge", [128, 2, 16])
    yout = dout("yout", [NTOK, D])
    nk_o = dout("nk", [2, 2, 256, 512])
    nv_o = dout("nv", [2, 2, 256, 512])
    tap_out = {}
    if taps:
        for name, shape in taps.items():
            tap_out[name] = dout("tap_" + name, shape)

    budget0 = nc.sbuf_bytes_remaining

    def sb(name, shape, dt):
        return nc.alloc_sbuf_tensor(name, list(shape), dt)

    X = sb("X", [128, 8, NTOK], F32)
    HY = sb("HY", [128, 8, NTOK], BF16)
    RING = sb("RING", [128, RING_SLOTS, 8 * PIECE], BF16)
    ROPEC = sb("ROPEC", [128, 1024], BF16)
    ROPES = sb("ROPES", [128, 1024], BF16)
    IDF = sb("IDF", [128, 128], F32)
    ONES = sb("ONES", [128, 128], BF16)
    OTOP = sb("OTOP", [128, 128], BF16)
    OBOT = sb("OBOT", [128, 128], BF16)
    PM = sb("PM", [128, 128], BF16)
    WPBD = sb("WPBD", [128, 4, 128], BF16)
    WST = sb("WST", [128, 8, 128], BF16)
    BSG = sb("BSG", [128, 4, 128], F32)
    GN = sb("GN", [128, 2, 256], F32)
    SV = sb("SV", [128, 256], F32)
    SCT = sb("SCT", [128, 8, 2], BF16)
    MOD = sb("MOD", [128, 2, 48, 2], F32)
    A1 = sb("A1", [128, 2, 8, 2], F32)
    A2 = sb("A2", [128, 2, 8, 2], F32)
    LAM = sb("LAM", [128, 16], F32)
    GSUB = sb("GSUB", [128, 2], F32)
    PSC = sb("PSC", [128, 4], F32)
    INVW = sb("INVW", [128, 2], F32)
    EDGE = sb("EDGE", [128, 2, 16], F32)
    NBS = sb("NBS", [128, 8, 8], F32)
    NBA = sb("NBA", [128, 16], F32)
    ST6 = sb("ST6", [128, 48], F32)
    RSTD = sb("RSTD", [128, 512], F32)
    STG = sb("STG", [128, 2, 1024], F32)
    POOLED = sb("POOLED", [128, 2, NTOK], BF16)
    TMPF = sb("TMPF", [128, 4, 512], F32)
    SQ0 = sb("SQ0", [128, 1280], BF16)
    SQ1 = sb("SQ1", [128, 1280], BF16)
    SQ = None
    XB = sb("XB", [128, 2, 512], BF16)
    MROW = sb("MROW", [2, 256], F32)
    SVT = sb("SVT", [128, 256], F32)
    EPSB = sb("EPSB", [128, 1], F32)

    A_QT, A_KT, A_VT, A_UT = 0, 12288, 26624, 40960
    A_SH = 47104
    ARENA_BYTES = max(A_SH + 4 * PADW * 4, NFF * NTOK * 2)
    ARENA = sb("ARENA", [128, ARENA_BYTES // 2], BF16)

    def aview(off, nbytes, dt):
        v = ARENA[:, off // 2:(off + nbytes) // 2]
        if dt is F32:
            v = v.bitcast(F32)
        return v

    QT = aview(A_QT, 12288, BF16).rearrange("p (h n) -> p h n", h=4)
    KT = aview(A_KT, 14336, BF16).rearrange("p (h n) -> p h n", h=4)
    VT = aview(A_VT, 14336, BF16).rearrange("p (c n) -> p c n", c=14)
    UT = aview(A_UT, 6144, BF16).rearrange("p (j n) -> p j n", j=2)
    PP = aview(A_SH, 2 * PADW * 4, F32).rearrange("p (j n) -> p j n", j=2)
    PA = aview(A_SH + 2 * PADW * 4, PADW * 4, F32)
    PB = aview(A_SH + 3 * PADW * 4, PADW * 4, F32)
    VNP = aview(A_SH, 8192, BF16).rearrange("p (b c g n) -> p b c g n", b=2, c=4, g=4)
    EB = aview(A_SH + 8192, 8192, BF16).rearrange("p (i n) -> p i n", i=8)
    ATT = None
    XNBS = [aview(o, 16384, F32).rearrange("p (k n) -> p k n", k=8) for o in (0, 16384)]
    ACTB = aview(0, NFF * NTOK * 2, BF16).rearrange("p (j n) -> p j n", j=NFF)

    used = budget0 - nc.sbuf_bytes_remaining
    assert nc.sbuf_bytes_remaining > 128, f"SBUF over budget: used {used}"

    PS = nc.alloc_psum_tensor("PS", [128, 8, 512], F32)

    xR = [[Res(f"x{k}_{t}") for t in range(NT)] for k in range(8)]
    hyR = [[Res(f"hy{k}_{t}") for t in range(NT)] for k in range(8)]
    bankR = [Res(f"bank{i}", excl=True) for i in range(8)]
    ringR = [Res(f"ring{i}") for i in range(RING_SLOTS)]
    ringD = [S.dsem(f"dring{i}") for i in range(RING_SLOTS)]
    constR = Res("consts")
    constD = S.dsem("dconst")
    constR2 = Res("consts2")
    constD2 = S.dsem("dconst2")
    S.also[constR] = [constR2]
    stgR = [Res("stg0"), Res("stg1")]
    stgD = [S.dsem("dstg0"), S.dsem("dstg1")]
    outD = [S.dsem("dout0"), S.dsem("dout1")]
    cvD = [S.dsem("dcv0"), S.dsem("dcv1")]
    rstdR = Res("rstd")
    tmpR = [Res(f"tmp{i}") for i in range(4)]
    sqR = [Res("sq0"), Res("sq1")]
    xbR = [Res("xb0"), Res("xb1")]
    mrowR = Res("mrow")
    modR = [Res("mod0"), Res("mod1")]
    svR = Res("sv")
    miscR = Res("misc")
    qR = [[Res(f"q{h}_{t}") for t in range(NT)] for h in range(4)]
    kR = [[Res(f"k{h}_{t}") for t in range(NT)] for h in range(4)]
    kcR = [Res(f"kc{h}") for h in range(4)]
    vR = [Res(f"v{c}") for c in range(14)]
    uR = [[Res(f"u{j}_{t}") for t in range(NT)] for j in range(2)]
    ppR = [Res("pp0"), Res("pp1")]
    paR, pbR = Res("pa"), Res("pb")
    pooledR = [Res("pooled0"), Res("pooled1")]
    vnpR = [Res("vnp0"), Res("vnp1")]
    ebR = [Res(f"eb{i}") for i in range(8)]
    nbsR = [Res(f"nbs{i}") for i in range(8)]
    nbaR = [Res(f"nba{i}") for i in range(8)]
    attR = Res("att")
    nbR = Res("nb")
    st6R = Res("st6")
    actR = [[Res(f"act{j}_{t}") for t in range(NT)] for j in range(NFF)]
    xnbRs = [Res("xnb0"), Res("xnb1")]
    outR = Res("outdram")

    tile = lambda t: slice(t * 512, (t + 1) * 512)
    cond_of = (0, 1, 1)

    ring_state = {"n": 0}

    def load_piece(src_ap, nk=8, ncols=PIECE):
        i = ring_state["n"] % RING_SLOTS
        ring_state["n"] += 1
        dst = RING[:, i, :].rearrange("p (k n) -> p k n", k=8)
        S.add("pool", lambda e, d=dst[:, 0:nk, 0:ncols], s=src_ap: e.dma_start(out=d, in_=s),
              writes=[ringR[i]], dsem=ringD[i])
        return dst, ringR[i]

    def wrows(w2d, c0, ncols=PIECE, k0=0, nk=8):
        return w2d.rearrange("(k p) n -> p k n", p=128)[:, k0:k0 + nk, c0:c0 + ncols]

    wpbd0R = Res("wpbd0")

    def cdma(eng, dst, src, reads=()):
        S.add(eng, I("dma_start", out=dst, in_=src), reads=list(reads), writes=[Res("c")], dsem=(constD if eng == "sp" else constD2))

    S.add("dve", I("memset", ONES[:], 1.0), writes=[miscR])
    S.add("dve", I("memset", OTOP[:], 0.0), writes=[miscR])
    S.add("dve", I("memset", OTOP[0:64, :], 1.0), writes=[miscR])
    S.add("dve", I("memset", OBOT[:], 0.0), writes=[miscR])
    S.add("dve", I("memset", OBOT[64:128, :], 1.0), writes=[miscR])
    S.add("dve", I("memset", WPBD[:], 0.0), writes=[wpbd0R])
    cdma("sp", IDF[:], c_ident)
    cdma("sp", SV[:].rearrange("p (a n) -> p a n", a=2), sv_d.rearrange("a p n -> p a n"))
    cdma("pool", ROPEC[:], c_ropec)
    cdma("pool", ROPES[:], c_ropes)
    cdma("sp", INVW[:], c_invw)
    cdma("sp", EDGE[:], c_edge)
    cdma("sp", TMPF[:, 0, :], lamv.partition_broadcast(128))
    for l in range(2):
        cdma("sp", GN[:, l, :], sgn[l:l + 1, :].partition_broadcast(128))
        for g in range(4):
            cdma("sp", BSG[(g % 2) * 64:(g % 2) * 64 + 64, l * 2 + g // 2, :],
                 b_sgu[l, g:g + 1, :].partition_broadcast(64))
            cdma("pool", WPBD[(g % 2) * 64:(g % 2) * 64 + 64, l * 2 + g // 2, (g % 2) * 64:(g % 2) * 64 + 64],
                 w_pool[l, g], reads=[wpbd0R])
            cdma("pool", WST[:, l * 4 + g, :], w_sT[l, g])
    cdma("pool", PM[:], c_pm)
    constR.w = constD.last
    constR2.w = constD2.last
    stop_early = False
    try:
        CK("consts")
    except _Stop:
        stop_early = True

    for a in range(2):
        S.add("pe", I("transpose", PS[:, 6, a * 128:(a + 1) * 128], SV[:, a * 128:(a + 1) * 128], IDF[:]),
              reads=[constR], writes=[bankR[6]])
    S.add("dve", I("tensor_copy", out=SVT[:], in_=PS[:, 6, 0:256]), reads=[bankR[6]], writes=[svR])
    S.add("act", I("activation", out=SCT[:].rearrange("p k c -> p c k"),
                                        in_=SVT[:, 0:16].rearrange("p (c k) -> p c k", c=2), func=AF.Silu),
          reads=[svR], writes=[miscR])
    lv = TMPF[:, 0, :].rearrange("p (l b two n) -> p l b two n", l=2, b=2, two=2)
    S.add("dve", I("tensor_tensor", out=TMPF[:, 1, 0:256].rearrange("p (l b n) -> p l b n", l=2, b=2),
                                           in0=lv[:, :, :, 0, :], in1=lv[:, :, :, 1, :], op=ALU.mult),
          reads=[constR], writes=[tmpR[1]])
    S.add("dve", I("tensor_reduce", out=LAM[:, 0:4], in_=TMPF[:, 1, 0:256].rearrange("p (a n) -> p a n", a=4),
                                           axis=AX.X, op=ALU.add), reads=[tmpR[1]], writes=[miscR])
    S.add("act", I("activation", out=LAM[:, 4:8], in_=LAM[:, 0:4], func=AF.Exp), reads=[miscR], writes=[miscR])
    for l in range(2):
        S.add("dve", I("scalar_tensor_tensor", out=LAM[:, 8 + l:9 + l], in0=LAM[:, 5 + 2 * l:6 + 2 * l],
                                                           scalar=-lam_init(l), in1=LAM[:, 4 + 2 * l:5 + 2 * l],
                                                           op0=ALU.add, op1=ALU.subtract),
              reads=[miscR], writes=[miscR])
        S.add("dve", I("tensor_scalar", out=GSUB[:, l:l + 1], in0=SVT[:, 60 + l:61 + l],
                                                    scalar1=1.0 - lam_init(l), scalar2=None, op0=ALU.mult),
              reads=[svR], writes=[miscR])

    for c in range(12):
        b = c % 2
        S.add("sp", I("dma_start", out=STG[:, b, :], in_=xin[c * 128:(c + 1) * 128, :]),
              writes=[stgR[b]], dsem=stgD[b])
        for half in range(2):
            bk = 2 * b + half
            for kk in range(4):
                k = half * 4 + kk
                S.add("pe", I("transpose", PS[:, bk, kk * 128:(kk + 1) * 128],
                                                                          STG[:, b, k * 128:(k + 1) * 128], IDF[:]),
                      reads=[stgR[b], constR], writes=[bankR[bk]])
            eng = "dve" if half == 0 else "act"
            dst = X[:, half * 4:half * 4 + 4, c * 128:(c + 1) * 128]
            src = PS[:, bk, :].rearrange("p (k n) -> p k n", k=4)
            t = c // 4
            if eng == "dve":
                S.add("dve", I("tensor_copy", out=dst, in_=src), reads=[bankR[bk]],
                      writes=[xR[half * 4 + i][t] for i in range(4)])
            else:
                S.add("act", I("activation", out=dst, in_=src, func=AF.Copy), reads=[bankR[bk]],
                      writes=[xR[half * 4 + i][t] for i in range(4)])

    def ada(l):
        for _ in ada_steps(l):
            pass

    def ada_steps(l):
        for j in range(24):
            slot, sres = load_piece(wrows(w_ada[l], j * PIECE))
            for k in range(8):
                S.add("pe", I("matmul", PS[0:2, 7, 0:PIECE], SCT[:, k, :], slot[:, k, :], start=(k == 0), stop=(k == 7)),
                      reads=[sres, miscR], writes=[bankR[7]])
            S.add("act", I("activation", out=MROW[:, 0:PIECE], in_=PS[0:2, 7, 0:PIECE], func=AF.Copy),
                  reads=[bankR[7]], writes=[mrowR])
            for i in range(2):
                S.add("pe", I("transpose", PS[:, 7, 256 + 2 * i:258 + 2 * i], MROW[:, i * 128:(i + 1) * 128], IDF[0:2, 0:2]),
                      reads=[mrowR, constR], writes=[bankR[7]])
            a0 = 2 * j
            S.add("dve", I("tensor_tensor", out=MOD[:, l, a0:a0 + 2, :], in0=PS[:, 7, 256:260].rearrange("p (a c) -> p a c", c=2),
                           in1=SVT[:, 128 + 48 * l + a0:130 + 48 * l + a0].unsqueeze(2).to_broadcast([128, 2, 2]), op=ALU.add),
                  reads=[bankR[7], svR], writes=[modR[l]])
            for (AA, goff, moff, jdone) in ((A1, 16, 8, 7), (A2, 32, 32, 19)):
                if j == jdone:
                    S.add("dve", I("scalar_tensor_tensor", out=AA[:, l, :, :], in0=MOD[:, l, moff:moff + 8, :], scalar=1.0,
                                   in1=SVT[:, goff + 8 * l:goff + 8 * l + 8].unsqueeze(2).to_broadcast([128, 8, 2]),
                                   op0=ALU.add, op1=ALU.mult), reads=[modR[l], svR], writes=[modR[l]])
            if j < 23:
                yield j

    def modap(l, i, k, c):
        return MOD[:, l, i * 8 + k, c:c + 1]

    def rms_stats(t, bank=6):
        rb, rr = ((RSTD[:], rstdR), (TMPF[:, 3, :], tmpR[3]))[t % 2]
        for k in range(8):
            b = k % 2
            S.add("act", I("activation", out=(SQ0, SQ1)[b][:, 0:512], in_=X[:, k, tile(t)], func=AF.Square),
                  reads=[xR[k][t]], writes=[sqR[b]])
            S.add("pe", I("matmul", PS[:, bank, :], ONES[:], (SQ0, SQ1)[b][:, 0:512], start=(k == 0), stop=(k == 7)),
                  reads=[sqR[b], miscR], writes=[bankR[bank]])
        S.add("act", I("activation", out=rb, in_=PS[:, bank, :], func=AF.Ln, scale=1.0 / D, bias=EPSB[:]),
              reads=[bankR[bank], miscR], writes=[rr])
        S.add("act", I("activation", out=rb, in_=rb, func=AF.Exp, scale=-0.5), reads=[rr], writes=[rr])
        return rb, rr

    S.add("dve", I("memset", EPSB[:], EPS), writes=[miscR])

    def norm_apply(l, AA, shi, t, st):
        rb, rr = st
        c = cond_of[t]
        for k in range(8):
            b = k % 2
            S.add("dve", I("tensor_tensor", out=TMPF[:, b, :], in0=X[:, k, tile(t)], in1=rb, op=ALU.mult),
                  reads=[xR[k][t], rr], writes=[tmpR[b]])
            if k % 2 == 0:
                S.add("act", I("activation", out=HY[:, k, tile(t)], in_=TMPF[:, b, :], func=AF.Identity,
                               scale=AA[:, l, k, c:c + 1], bias=modap(l, shi, k, c)),
                      reads=[tmpR[b], modR[l]], writes=[hyR[k][t]])
            else:
                S.add("dve", I("tensor_scalar", out=HY[:, k, tile(t)], in0=TMPF[:, b, :], scalar1=AA[:, l, k, c:c + 1],
                               scalar2=modap(l, shi, k, c), op0=ALU.mult, op1=ALU.add),
                      reads=[tmpR[b], modR[l]], writes=[hyR[k][t]])

    def norm_seq(l, AA, shi, filler=None, after=None):
        for t in range(NT):
            st = rms_stats(t)
            if filler is not None:
                filler()
            norm_apply(l, AA, shi, t, st)
            if after is not None:
                after(t)

    fb = {"i": 0}

    def fm_one(slot, sres, col0, nk, t, evac):
        bk = fb["i"] % 6
        fb["i"] += 1
        for k in range(nk):
            S.add("pe", I("matmul", PS[:, bk, :], slot[:, k, col0:col0 + 128], HY[:, k, tile(t)], start=(k == 0), stop=(k == nk - 1)),
                  reads=[sres, hyR[k][t]], writes=[bankR[bk]])
        evac(t, bk)

    dstate = {"set": 0}

    def fm_group(slot, sres, col0, nk, rhs_fn, rhs_res_fn, evac, t_outer=False):
        base = 3 * dstate["set"]
        dstate["set"] ^= 1
        order = [(k, t) for t in range(NT) for k in range(nk)] if t_outer else [(k, t) for k in range(nk) for t in range(NT)]
        for (k, t) in order:
            S.add("pe", I("matmul", PS[:, base + t, :], slot[:, k, col0:col0 + 128], rhs_fn(k, t),
                          start=(k == 0), stop=(k == nk - 1)),
                  reads=[sres] + rhs_res_fn(k, t), writes=[bankR[base + t]])
            if t_outer and k == nk - 1:
                evac(t, base + t)
        if not t_outer:
            for t in range(NT):
                evac(t, base + t)

    hy_rhs = lambda k, t: HY[:, k, tile(t)]
    hy_res = lambda k, t: [hyR[k][t]]

    kcol_of_tile = (1280, 256, 768)

    def gelu_to(dst_fn, src_ap, n, dres, sres_list, tb=0):
        a, w = TMPF[:, tb, 0:n], TMPF[:, tb + 1, 0:n]
        S.add("act", I("activation", out=a, in_=src_ap, func=AF.Square, scale=math.sqrt(0.044715)),
              reads=sres_list, writes=[tmpR[tb]])
        S.add("dve", I("scalar_tensor_tensor", out=w, in0=a, scalar=1.0, in1=src_ap, op0=ALU.add, op1=ALU.mult),
              reads=[tmpR[tb]] + sres_list, writes=[tmpR[tb + 1]])
        S.add("act", I("activation", out=a, in_=w, func=AF.Sigmoid, scale=1.5957691216057308),
              reads=[tmpR[tb + 1]], writes=[tmpR[tb]])
        S.add("dve", I("tensor_tensor", out=dst_fn, in0=src_ap, in1=a, op=ALU.mult),
              reads=[tmpR[tb]] + sres_list, writes=dres)

    def layer(l, ada_cur=iter(())):
        def ada_more(n):
            for _ in range(n):
                next(ada_cur, None)

        S.fence([r for hh in qR for r in hh] + [r for hh in kR for r in hh] + kcR + vR + [r for uu in uR for r in uu]
                + ppR + [paR, pbR])
        S.add("dve", I("memset", PP[:, :, :], 0.0), writes=ppR)

        for cc in range(2):
            S.add("sp", I("dma_start", out=STG[:, cc, 0:512], in_=ck[l, cc * 128:(cc + 1) * 128, :]),
                  writes=[stgR[cc]], dsem=stgD[cc])
            S.add("pool", I("dma_start", out=VT[:, cc, :], in_=cv[l, cc * 128:(cc + 1) * 128, :]),
                  writes=[vR[cc]], dsem=cvD[cc])
        for cc in range(2):
            for h in range(4):
                S.add("pe", I("transpose", PS[:, 6 + cc, h * 128:(h + 1) * 128],
                                                              STG[:, cc, h * 128:(h + 1) * 128], IDF[:]),
                      reads=[stgR[cc], constR], writes=[bankR[6 + cc]])
            S.add("act", I("activation", out=KT[:, :, cc * 128:(cc + 1) * 128],
                                                       in_=PS[:, 6 + cc, :].rearrange("p (h n) -> p h n", h=4), func=AF.Copy),
                  reads=[bankR[6 + cc]], writes=kcR)

        W = w_in[l]

        def ev_pool_of(j):
            def ev_pool(t, bk):
                if t == 0:
                    dst = PP[:, j, 8:552].rearrange("p (s w) -> p s w", w=272)[:, :, 0:256]
                    src = PS[:, bk, :].rearrange("p (s w) -> p s w", w=256)
                else:
                    dst = PP[:, j, 552 + (t - 1) * 512:552 + t * 512]
                    src = PS[:, bk, :]
                S.add("act", I("activation", out=dst, in_=src, func=AF.Copy), reads=[bankR[bk]], writes=[ppR[j]])
            return ev_pool

        def ev_qk(dstT, dres, h, colfn):
            def ev(t, bk):
                if t == 0:
                    S.add("act", I("activation", out=dstT[:, h, colfn(t):colfn(t) + 512], in_=PS[:, bk, :], func=AF.Copy),
                          reads=[bankR[bk]], writes=[dres[h][t]])
                    return
                b = t % 2
                rb = 6 + b
                cs = slice((t - 1) * 512, t * 512)
                S.add("act", I("activation", out=XB[:, b, :], in_=PS[:, bk, :], func=AF.Copy), reads=[bankR[bk]], writes=[xbR[b]])
                S.add("pe", I("matmul", PS[:, rb, :], PM[:], XB[:, b, :], start=True, stop=True),
                      reads=[xbR[b], constR], writes=[bankR[rb]])
                S.add("dve", I("tensor_tensor", out=TMPF[:, 2 * b, :], in0=PS[:, bk, :], in1=ROPEC[:, cs], op=ALU.mult),
                      reads=[bankR[bk], constR], writes=[tmpR[2 * b]])
                S.add("dve", I("tensor_tensor", out=TMPF[:, 2 * b + 1, :], in0=PS[:, rb, :], in1=ROPES[:, cs], op=ALU.mult),
                      reads=[bankR[rb], constR], writes=[tmpR[2 * b + 1]])
                S.add("dve", I("tensor_tensor", out=dstT[:, h, colfn(t):colfn(t) + 512], in0=TMPF[:, 2 * b, :],
                               in1=TMPF[:, 2 * b + 1, :], op=ALU.add),
                      reads=[tmpR[2 * b], tmpR[2 * b + 1]], writes=[dres[h][t]])
            return ev

        qcol = lambda t: t * 512
        kcol = lambda t: kcol_of_tile[t]
        def load_first():
            first = []
            for (c0, evs) in ((0, [ev_pool_of(0), ev_pool_of(1)]),
                              (256, [ev_qk(QT, qR, 0, qcol), ev_qk(QT, qR, 1, qcol)]),
                              (512, [ev_qk(QT, qR, 2, qcol), ev_qk(QT, qR, 3, qcol)]),
                              (768, [ev_qk(KT, kR, 0, kcol), ev_qk(KT, kR, 1, kcol)])):
                slot, sres = load_piece(wrows(W, c0))
                first.append((slot, sres, evs))
            return first

        def first_groups(first, t):
            for (slot, sres, evs) in first:
                for i, ev in enumerate(evs):
                    fm_one(slot, sres, i * 128, 8, t, ev)

        if l == 0:
            norm_seq(l, A1, 0, filler=lambda: ada_more(6))
            first = load_first()
            for t in range(NT):
                first_groups(first, t)
        else:
            first = load_first()
            norm_seq(l, A1, 0, after=lambda t: first_groups(first, t))
        CK(f"normA{l}")
        k0slot, k0res = first[3][0], first[3][1]

        def tokmajor_k(hp, slot, sres):
            for c in range(4):
                bk = 4 + (c % 4)
                for k in range(8):
                    S.add("pe", I("matmul", PS[:, bk, 0:256], HY[:, k, c * 128:(c + 1) * 128], slot[:, k, :], start=(k == 0), stop=(k == 7)),
                          reads=[sres, hyR[k][0]], writes=[bankR[bk]])
                b = c % 2
                S.add("act", I("activation", out=STG[:, b, 0:256], in_=PS[:, bk, 0:256], func=AF.Copy), reads=[bankR[bk]], writes=[stgR[b]])
                S.add("sp", I("dma_start", out=nk_o[c // 2, l, (c % 2) * 128:(c % 2) * 128 + 128, hp * 256:(hp + 1) * 256], in_=STG[:, b, 0:256]),
                      reads=[stgR[b]], writes=[outR], dsem=outD[b])

        tokmajor_k(0, k0slot, k0res)

        Wd = PADW

        def shift_add(dst, src, lo, hi, sh_a, sh_b, dres, sres_, p0=0, p1=128):
            S.add("dve", I("tensor_tensor", out=dst[p0:p1, lo:hi], in0=src[p0:p1, lo + sh_a:hi + sh_a],
                                                   in1=src[p0:p1, lo + sh_b:hi + sh_b], op=ALU.add),
                  reads=sres_, writes=dres)

        def pool_finish(j, lo_src, hi_src, lo_res, hi_res):
            for (p0, p1, src, sres_) in ((0, 64, lo_src, lo_res), (64, 128, hi_src, hi_res)):
                for s in range(3):
                    a0, t0, L = SEQ_PAD0[s], SEQ_TOK0[s], SEQ_LEN[s]
                    S.add("dve", I("scalar_tensor_tensor",
                        out=POOLED[p0:p1, j, t0:t0 + L], in0=src[p0:p1, a0:a0 + L], scalar=INVW[p0:p1, j:j + 1],
                        in1=PP[p0:p1, j, a0:a0 + L], op0=ALU.mult, op1=ALU.subtract),
                        reads=[sres_, ppR[j], constR], writes=[pooledR[j]])
                    for (ca, ea) in ((0, 0), (L - 8, 8)):
                        S.add("dve", I("tensor_tensor",
                            out=TMPF[p0:p1, 2, 0:8], in0=src[p0:p1, a0 + ca:a0 + ca + 8], in1=EDGE[p0:p1, j, ea:ea + 8], op=ALU.mult),
                            reads=[sres_, constR], writes=[tmpR[2]])
                        S.add("dve", I("tensor_tensor",
                            out=POOLED[p0:p1, j, t0 + ca:t0 + ca + 8], in0=TMPF[p0:p1, 2, 0:8],
                            in1=PP[p0:p1, j, a0 + ca:a0 + ca + 8], op=ALU.subtract),
                            reads=[tmpR[2], ppR[j]], writes=[pooledR[j]])

        shift_add(PA, PP[:, 0, :], 1, Wd, -1, 0, [paR], [ppR[0]])
        shift_add(PB, PA, 2, Wd - 1, -1, 1, [pbR], [paR], 64, 128)
        pool_finish(0, PA, PB, paR, pbR)
        shift_add(PA, PP[:, 1, :], 1, Wd, -1, 0, [paR], [ppR[1], pooledR[0]])
        shift_add(PB, PA, 2, Wd - 1, -1, 1, [pbR], [paR, pooledR[0]])
        shift_add(PA, PB, 4, Wd - 3, -2, 2, [paR], [pbR])
        shift_add(PB, PA, 8, Wd - 7, -4, 4, [pbR], [paR], 64, 128)
        pool_finish(1, PA, PB, paR, pbR)

        CK(f"pool{l}")
        slot, sres = load_piece(wrows(W, 768 + 256))
        for hh in range(2):
            fm_group(slot, sres, hh * 128, 8, hy_rhs, hy_res, ev_qk(KT, kR, 2 + hh, kcol))
        tokmajor_k(1, slot, sres)
        CK(f"qk{l}")
        grp = (
            dict(qa=0, qn=512, ka=1280, kn=512, qres=lambda h: [qR[h][0]], kres=lambda h: [kR[h][0]]),
            dict(qa=512, qn=1024, ka=0, kn=1280, qres=lambda h: [qR[h][1], qR[h][2]], kres=lambda h: [kcR[h], kR[h][1], kR[h][2]]),
        )

        def bounds_a(u):
            gi, h = u // 4, u % 4
            G = grp[gi]
            for (src, base, n, res, col, SQX, sqx) in ((QT, G["qa"], G["qn"], G["qres"](h), 0, SQ0, sqR[0]), (KT, G["ka"], G["kn"], G["kres"](h), 2, SQ1, sqR[1])):
                S.add("act", I("activation", out=SQX[:, 0:n], in_=src[:, h, base:base + n], func=AF.Square), reads=res, writes=[sqx])

        def bounds(u):
            gi, h = u // 4, u % 4
            G = grp[gi]
            for (src, base, n, res, col, SQX, sqx) in ((QT, G["qa"], G["qn"], G["qres"](h), 0, SQ0, sqR[0]), (KT, G["ka"], G["kn"], G["kres"](h), 2, SQ1, sqR[1])):
                n256 = n // 256
                for br, ones in ((0, OTOP), (1, OBOT)):
                    for i in range(n256):
                        slot_i = br * n256 + i
                        bk, half = slot_i // 2, slot_i % 2
                        S.add("pe", I("matmul", PS[:, bk, half * 256:(half + 1) * 256], ones[:], SQX[:, i * 256:(i + 1) * 256], start=True, stop=True),
                              reads=[sqx, miscR], writes=[bankR[bk]])
                nb = (2 * n256 + 1) // 2
                flat = PS[:, 0:nb, :].rearrange("p b n -> p (b n)")[:, 0:2 * n256 * 256]
                S.add("dve", I("tensor_reduce", out=NBS[:, u, col:col + 2], in_=flat.rearrange("p (b n) -> p b n", b=2), axis=AX.X, op=ALU.max),
                      reads=[bankR[i] for i in range(nb)], writes=[nbsR[u]])
            S.add("dve", I("tensor_tensor", out=NBS[:, u, 4:6], in0=NBS[:, u, 0:2], in1=NBS[:, u, 2:4], op=ALU.mult), reads=[nbsR[u]], writes=[nbsR[u]])
            S.add("act", I("activation", out=NBS[:, u, 6:8], in_=NBS[:, u, 4:6], func=AF.Ln, scale=1.0 / 64.0), reads=[nbsR[u]], writes=[nbsR[u]])
            S.add("act", I("activation", out=NBS[:, u, 4:6], in_=NBS[:, u, 6:8], func=AF.Exp, scale=0.5), reads=[nbsR[u]], writes=[nbsR[u]])
            S.add("dve", I("tensor_scalar", out=NBS[:, u, 6:8], in0=NBS[:, u, 4:6], scalar1=-1.0, scalar2=None, op0=ALU.mult),
                  reads=[nbsR[u]], writes=[nbsR[u]])
            S.add("dve", I("tensor_tensor", out=NBA[:, 2 * u:2 * u + 1], in0=NBS[:, u, 6:7], in1=NBS[:, u, 7:8], op=ALU.min),
                  reads=[nbsR[u]], writes=[nbaR[u]])

        bseq = iter(range(8))

        def bstep():
            u = next(bseq, None)
            if u is not None:
                bounds(u)
            if u is not None and u + 1 < 8:
                bounds_a(u + 1)

        bounds_a(0)
        for hp in range(2):
            slot, sres = load_piece(wrows(W, 1280 + hp * 256))
            for c in range(12):
                if c % 6 == 3:
                    bstep()
                bk = 4 + (c % 4)
                t = c // 4
                for k in range(8):
                    S.add("pe", I("matmul", PS[:, bk, 0:256], HY[:, k, c * 128:(c + 1) * 128], slot[:, k, :],
                                                                             start=(k == 0), stop=(k == 7)),
                          reads=[sres, hyR[k][t]], writes=[bankR[bk]])
                vc = 10 + c if c < 4 else c - 2
                S.add("dve", I("tensor_copy", out=VT[:, vc, hp * 256:(hp + 1) * 256], in_=PS[:, bk, 0:256]),
                      reads=[bankR[bk]], writes=[vR[vc]])
                if c < 4:
                    b = c % 2
                    S.add("act", I("activation", out=STG[:, b, 0:256], in_=PS[:, bk, 0:256], func=AF.Copy),
                          reads=[bankR[bk]], writes=[stgR[b]])
                    S.add("sp", I("dma_start", out=nv_o[c // 2, l, (c % 2) * 128:(c % 2) * 128 + 128, hp * 256:(hp + 1) * 256],
                                                                     in_=STG[:, b, 0:256]),
                          reads=[stgR[b]], writes=[outR], dsem=outD[b])

        CK(f"v{l}")
        bstep()
        slot, sres = load_piece(wrows(W, 1792))
        for j in range(2):
            def ev_u(t, bk, j=j):
                S.add("act", I("activation", out=UT[:, j, tile(t)], in_=PS[:, bk, :], func=AF.Gelu_apprx_tanh), reads=[bankR[bk]], writes=[uR[j][t]])
            fm_group(slot, sres, j * 128, 8, hy_rhs, hy_res, ev_u)

        CK(f"u{l}")
        bstep()
        S.transfer(ppR + [paR, pbR], vnpR + ebR + [attR])
        S.add("pool", I("memset", VNP[:, :, :, :, :], 0.0), writes=vnpR)

        slot, sres = load_piece(wrows(W, 2048))
        def sgv_mm(t):
            vb = t % 2
            for cc in range(4):
                c = t * 4 + cc
                bk = 6 + cc // 2
                co = (cc % 2) * 256
                for k in range(8):
                    S.add("pe", I("matmul", PS[:, bk, co:co + 256], HY[:, k, c * 128:(c + 1) * 128], slot[:, k, :],
                                  start=(k == 0), stop=(k == 7)),
                          reads=[sres, hyR[k][t]], writes=[bankR[bk]])

        def sgv_chain(t):
            vb = t % 2
            src = PS[:, 6:8, :].rearrange("p b n -> p (b n)")
            ga = TMPF[:, 0:2, :].rearrange("p b n -> p (b n)")
            gv = TMPF[:, 2:4, :].rearrange("p b n -> p (b n)")
            bsrc = [bankR[6], bankR[7]]
            S.add("act", I("activation", out=gv, in_=src, func=AF.Gelu_apprx_tanh), reads=bsrc, writes=[tmpR[2], tmpR[3]])
            for cc in range(4):
                S.add("dve", I("bn_stats", out=ST6[:, cc * 6:(cc + 1) * 6], in_=gv[:, cc * 256:(cc + 1) * 256]), reads=[tmpR[2], tmpR[3]], writes=[st6R])
            for cc in range(4):
                S.add("dve", I("bn_aggr", out=ST6[:, 24 + cc * 2:26 + cc * 2], in_=ST6[:, cc * 6:(cc + 1) * 6]), reads=[st6R], writes=[st6R])
            mv = ST6[:, 24:32].rearrange("p (c two) -> p c two", two=2)
            S.add("act", I("activation", out=ST6[:, 32:36], in_=mv[:, :, 1], func=AF.Ln, bias=EPSB[:]), reads=[st6R, miscR], writes=[st6R])
            S.add("act", I("activation", out=ST6[:, 36:40], in_=ST6[:, 32:36], func=AF.Exp, scale=-0.5), reads=[st6R], writes=[st6R])
            gv3 = gv.rearrange("p (c n) -> p c n", c=4)
            S.add("dve", I("tensor_tensor", out=gv3, in0=gv3, in1=mv[:, :, 0:1].to_broadcast([128, 4, 256]), op=ALU.subtract),
                  reads=[st6R, tmpR[2], tmpR[3]], writes=[tmpR[2], tmpR[3]])
            S.add("dve", I("tensor_tensor", out=gv3, in0=gv3, in1=ST6[:, 36:40].unsqueeze(2).to_broadcast([128, 4, 256]), op=ALU.mult),
                  reads=[st6R, tmpR[2], tmpR[3]], writes=[tmpR[2], tmpR[3]])
            for par in range(2):
                dst = VNP[:, vb, :, :, :].rearrange("p c (a b) n -> p c a b n", b=2)[:, :, :, par, par * 64:par * 64 + 64]
                srcv = gv.rearrange("p (c a b n) -> p c a b n", c=4, a=2, b=2)[:, :, :, par, :]
                gn = GN[:, l, :].rearrange("p (a b n) -> p a b n", a=2, b=2)[:, :, par, :].unsqueeze(1).to_broadcast([128, 4, 2, 64])
                S.add("dve", I("tensor_tensor", out=dst, in0=srcv, in1=gn, op=ALU.mult), reads=[tmpR[2], tmpR[3], constR], writes=[vnpR[vb]])

        def sgv_gate(t):
            vb = t % 2
            for j in range(2):
                bkk = 3 * dstate["set"] + j
                for cc in range(4):
                    for gg in range(2):
                        g = 2 * j + gg
                        S.add("pe", I("matmul",
                            PS[:, bkk, cc * 128:(cc + 1) * 128], VNP[:, vb, cc, g, :], WST[:, l * 4 + g, :],
                            start=(gg == 0), stop=(gg == 1)),
                            reads=[vnpR[vb], constR], writes=[bankR[bkk]])
                S.add("dve", I("tensor_tensor",
                    out=TMPF[:, j, :].rearrange("p (c n) -> p c n", c=4), in0=PS[:, bkk, :].rearrange("p (c n) -> p c n", c=4),
                    in1=BSG[:, l * 2 + j, :].unsqueeze(1).to_broadcast([128, 4, 128]), op=ALU.add),
                    reads=[bankR[bkk], constR], writes=[tmpR[j]])
                S.add("dve", I("tensor_tensor", out=HY[:, 6 + j, tile(t)], in0=TMPF[:, j, :], in1=UT[:, j, tile(t)], op=ALU.mult),
                      reads=[tmpR[j], uR[j][t]], writes=[hyR[6 + j][t]])
            dstate["set"] ^= 1


        sgv_mm(0)
        for t in range(NT):
            sgv_chain(t)
            if t + 1 < NT:
                sgv_mm(t + 1)
            sgv_gate(t)

        CK(f"sgu{l}")
        for j in range(2):
            for t in range(NT):
                bk = 3 * dstate["set"] + t
                S.add("pe", I("matmul", PS[:, bk, :], WPBD[:, l * 2 + j, :], POOLED[:, j, tile(t)], start=True, stop=True),
                      reads=[pooledR[j], constR], writes=[bankR[bk]])
                S.add("act", I("activation", out=HY[:, j, tile(t)], in_=PS[:, bk, :], func=AF.Copy,
                                                                    scale=SVT[:, 56 + 2 * l + j:57 + 2 * l + j]),
                      reads=[bankR[bk], svR], writes=[hyR[j][t]])
            dstate["set"] ^= 1

        CK(f"poolmix{l}")
        for _ in range(8):
            bstep()
        units = []
        for h in range(4):
            units.append(dict(h=h, u=h, nkc=2, ytile=0, ycols=slice(0, 512), qres=[qR[h][0]], kres=[kR[h][0]],
                              halves=[(0, 256, 0, 1280, [10, 11]), (256, 256, 256, 1536, [12, 13])], copies=False))
        for h in range(4):
            for qi in range(2):
                units.append(dict(h=h, u=4 + h, nkc=10, ytile=1 + qi, ycols=slice(512 + qi * 512, 1024 + qi * 512), qres=[qR[h][1 + qi]],
                                  kres=[kcR[h], kR[h][1], kR[h][2]],
                                  halves=[(0, 512, 512 + qi * 512, 0, list(range(10)))], copies=True))
        B_O1, B_O2, B_Z1, B_Z2 = 4, 5, 6, 7
        T = lambda i: TMPF[:, i, :]

        def stage1(U):
            h = U["h"]
            S.add("dve", I("tensor_copy", out=T(0), in_=PS[:, B_Z1, :]), reads=[bankR[B_Z1]], writes=[tmpR[0]])
            S.add("act", I("activation", out=T(2), in_=PS[:, B_Z2, :], func=AF.Copy), reads=[bankR[B_Z2]], writes=[tmpR[2]])
            S.add("dve", I("tensor_copy", out=T(1), in_=PS[:, B_O1, :]), reads=[bankR[B_O1]], writes=[tmpR[1]])
            S.add("act", I("activation", out=T(3), in_=PS[:, B_O2, :], func=AF.Copy), reads=[bankR[B_O2]], writes=[tmpR[3]])
            S.add("dve", I("reciprocal", out=T(0), in_=T(0)), reads=[tmpR[0]], writes=[tmpR[0]])
            S.add("dve", I("tensor_tensor", out=T(1), in0=T(1), in1=T(0), op=ALU.mult), reads=[tmpR[0], tmpR[1]], writes=[tmpR[1]])
            if U["copies"]:
                S.add("dve", I("reciprocal", out=T(2), in_=T(2)), reads=[tmpR[2]], writes=[tmpR[2]])
            else:
                S.add("act", I("activation", out=T(2), in_=T(2), func=AF.Ln), reads=[tmpR[2]], writes=[tmpR[2]])
                S.add("act", I("activation", out=T(2), in_=T(2), func=AF.Exp, scale=-1.0), reads=[tmpR[2]], writes=[tmpR[2]])
            S.add("dve", I("tensor_tensor", out=T(3), in0=T(3), in1=T(2), op=ALU.mult), reads=[tmpR[2], tmpR[3]], writes=[tmpR[3]])
            S.add("dve", I("scalar_tensor_tensor", out=T(1), in0=T(3), scalar=LAM[:, 8 + l:9 + l], in1=T(1), op0=ALU.mult, op1=ALU.add),
                  reads=[tmpR[3], tmpR[1], miscR], writes=[tmpR[1]])

        def stage2(U):
            h = U["h"]
            nbk = 3
            S.add("act", I("activation", out=SQ1[:, 0:512], in_=T(1), func=AF.Square), reads=[tmpR[1]], writes=[sqR[1]])
            S.add("pe", I("matmul", PS[:, nbk, :], ONES[:], SQ1[:, 0:512], start=True, stop=True), reads=[sqR[1], miscR], writes=[bankR[nbk]])
            S.add("act", I("activation", out=T(0), in_=PS[:, nbk, :], func=AF.Ln, scale=1.0 / 128.0, bias=EPSB[:]),
                  reads=[bankR[nbk], miscR], writes=[tmpR[0]])
            S.add("act", I("activation", out=T(0), in_=T(0), func=AF.Exp, scale=-0.5), reads=[tmpR[0]], writes=[tmpR[0]])
            S.add("dve", I("scalar_tensor_tensor", out=HY[:, 2 + h, U["ycols"]], in0=T(1), scalar=GSUB[:, l:l + 1], in1=T(0), op0=ALU.mult, op1=ALU.mult),
                  reads=[tmpR[1], tmpR[0], miscR], writes=[hyR[2 + h][U["ytile"]]])

        pending = None
        for U in units:
            h, nkc, u = U["h"], U["nkc"], U["u"]

            def scores(kc):
                sbk = (kc % 2) * 2
                for (o0, w, qcol, kcol, vch) in U["halves"]:
                    kcs = slice(kcol + kc * 128, kcol + (kc + 1) * 128)
                    qcs = slice(qcol, qcol + w)
                    S.add("pe", I("matmul", PS[:, sbk, o0:o0 + w], KT[0:64, h, kcs], QT[0:64, h, qcs], start=True, stop=True),
                          reads=U["kres"] + U["qres"], writes=[bankR[sbk]])
                    S.add("pe", I("matmul", PS[:, sbk + 1, o0:o0 + w], KT[64:128, h, kcs], QT[64:128, h, qcs], start=True, stop=True),
                          reads=U["kres"] + U["qres"], writes=[bankR[sbk + 1]])
                for br in range(2):
                    eb = (kc % 4) * 2 + br
                    S.add("act", I("activation", out=EB[:, eb, :], in_=PS[:, sbk + br, :], func=AF.Exp, scale=0.125, bias=NBA[:, 2 * u:2 * u + 1]),
                          reads=[bankR[sbk + br], nbaR[u]], writes=[ebR[eb]])

            def pv(kc):
                for br, (bo, bz) in enumerate(((B_O1, B_Z1), (B_O2, B_Z2))):
                    eb = (kc % 4) * 2 + br
                    for hi_, (o0, w, qcol, kcol, vch) in enumerate(U["halves"]):
                        vc = vch[kc]
                        S.add("pe", I("matmul", PS[:, bo, o0:o0 + w], VT[:, vc, h * 128:(h + 1) * 128], EB[:, eb, o0:o0 + w],
                                      start=(kc == 0 and hi_ == 0), stop=(kc == nkc - 1 and hi_ == len(U["halves"]) - 1)),
                              reads=[vR[vc], ebR[eb]], writes=[bankR[bo]])
                    S.add("pe", I("matmul", PS[:, bz, :], ONES[:], EB[:, eb, :], start=(kc == 0), stop=(kc == nkc - 1)),
                          reads=[miscR, ebR[eb]], writes=[bankR[bz]])
                    if 0 < kc < nkc - 1 and br == 0:
                        for _ in range(NWARM):
                            S.add("pe", I("matmul", PS[:, bo, 0:128], OBOT[64:128, :], OTOP[64:128, :], start=False, stop=False),
                                  reads=[miscR], writes=[bankR[bo]])

            flush_at = min(5, nkc - 1)
            scores(0)
            for kc in range(nkc):
                if kc + 1 < nkc:
                    scores(kc + 1)
                pv(kc)
                if kc == flush_at and pending is not None:
                    stage2(pending)
                    pending = None
            stage1(U)
            pending = U
        stage2(pending)


        ada_more(24)
        CK(f"attn{l}")
        def resid_evac(gi):
            def mk(m):
                def ev(t, bk):
                    c = cond_of[t]
                    S.add("dve", I("scalar_tensor_tensor", out=X[:, m, tile(t)], in0=PS[:, bk, :], scalar=modap(l, gi, m, c),
                                                                  in1=X[:, m, tile(t)], op0=ALU.mult, op1=ALU.add),
                          reads=[bankR[bk], modR[l], xR[m][t]], writes=[xR[m][t]])
                return ev
            return mk

        for pc in range(4):
            slot, sres = load_piece(wrows(w_out[l], pc * 256))
            if pc == 3:
                for t in range(NT):
                    for mm in range(2):
                        bk = 3 * mm + t
                        for k in range(8):
                            S.add("pe", I("matmul", PS[:, bk, :], slot[:, k, mm * 128:(mm + 1) * 128], HY[:, k, tile(t)],
                                          start=(k == 0), stop=(k == 7)),
                                  reads=[sres, hyR[k][t]], writes=[bankR[bk]])
                        resid_evac(2)(pc * 2 + mm)(t, bk)
                continue
            for mm in range(2):
                fm_group(slot, sres, mm * 128, 8, hy_rhs, hy_res, resid_evac(2)(pc * 2 + mm))

        CK(f"wout{l}")
        ada_it = ada_steps(l + 1) if l + 1 < 2 else iter(())

        def ada_next(n):
            for _ in range(n):
                next(ada_it, None)
        S.fence([r for jj in actR for r in jj])

        def ffn_evac(j, t, bg, bu):
            b = t % 2
            S.add("act", I("activation", out=XB[:, b, :], in_=PS[:, bg, :], func=AF.Silu), reads=[bankR[bg]], writes=[xbR[b]])
            S.add("dve", I("tensor_tensor", out=ACTB[:, j, tile(t)], in0=PS[:, bu, :], in1=XB[:, b, :], op=ALU.mult),
                  reads=[bankR[bu], xbR[b]], writes=[actR[j][t]])

        def load_fpairs():
            return [(jp, load_piece(wrows(w_fi[l], jp * 256)), load_piece(wrows(w_fi[l], DFF + jp * 256))) for jp in range(2)]

        def ffn_first(t):
            for (jp, (gslot, gres), (uslot, ures)) in fpairs:
                for jj in range(2):
                    j = jp * 2 + jj
                    bg = fb["i"] % 6
                    bu = (fb["i"] + 1) % 6
                    fb["i"] += 2
                    for k in range(8):
                        S.add("pe", I("matmul", PS[:, bg, :], gslot[:, k, jj * 128:(jj + 1) * 128], HY[:, k, tile(t)], start=(k == 0), stop=(k == 7)),
                              reads=[gres, hyR[k][t]], writes=[bankR[bg]])
                    for k in range(8):
                        S.add("pe", I("matmul", PS[:, bu, :], uslot[:, k, jj * 128:(jj + 1) * 128], HY[:, k, tile(t)], start=(k == 0), stop=(k == 7)),
                              reads=[ures, hyR[k][t]], writes=[bankR[bu]])
                    ffn_evac(j, t, bg, bu)

        if l == 0:
            norm_seq(l, A2, 3, filler=lambda: ada_next(4))
            fpairs = load_fpairs()
            for t in range(NT):
                ffn_first(t)
        else:
            fpairs = load_fpairs()
            norm_seq(l, A2, 3, after=ffn_first)
        CK(f"norm2{l}")
        for jp in range(2, 11):
            ada_next(2 if jp in (2, 3, 10) else 1)
            gslot, gres = load_piece(wrows(w_fi[l], jp * 256))
            uslot, ures = load_piece(wrows(w_fi[l], DFF + jp * 256))
            for jj in range(2):
                j = jp * 2 + jj
                for k in range(8):
                    for t in range(NT):
                        S.add("pe", I("matmul", PS[:, t, :], gslot[:, k, jj * 128:(jj + 1) * 128], HY[:, k, tile(t)], start=(k == 0), stop=(k == 7)),
                              reads=[gres, hyR[k][t]], writes=[bankR[t]])
                for k in range(8):
                    for t in range(NT):
                        S.add("pe", I("matmul", PS[:, 3 + t, :], uslot[:, k, jj * 128:(jj + 1) * 128], HY[:, k, tile(t)], start=(k == 0), stop=(k == 7)),
                              reads=[ures, hyR[k][t]], writes=[bankR[3 + t]])
                for t in range(NT):
                    ffn_evac(j, t, t, 3 + t)

        CK(f"ffnin{l}")
        for _ in ada_it:
            pass
        for pc in range(4):
            slots = []
            for kg, (kb, nk) in enumerate(((0, 8), (8, 8), (16, 6))):
                slots.append(load_piece(wrows(w_fo[l], pc * 256, k0=kb, nk=nk), nk=nk) + (kb, nk))
            if pc == 3:
                for t in range(NT):
                    for mm in range(2):
                        m = pc * 2 + mm
                        bk = 3 * mm + t
                        for (slot, sres, kb, nk) in slots:
                            for k in range(nk):
                                kk = kb + k
                                S.add("pe", I("matmul", PS[:, bk, :], slot[:, k, mm * 128:(mm + 1) * 128], ACTB[:, kk, tile(t)],
                                              start=(kk == 0), stop=(kk == NFF - 1)),
                                      reads=[sres, actR[kk][t]], writes=[bankR[bk]])
                        resid_evac(5)(m)(t, bk)
                continue
            for mm in range(2):
                m = pc * 2 + mm
                base = 3 * dstate["set"]
                dstate["set"] ^= 1
                if m == 7:
                    for t in range(NT):
                        for (slot, sres, kb, nk) in slots:
                            for k in range(nk):
                                kk = kb + k
                                S.add("pe", I("matmul", PS[:, base + t, :], slot[:, k, mm * 128:(mm + 1) * 128], ACTB[:, kk, tile(t)],
                                              start=(kk == 0), stop=(kk == NFF - 1)),
                                      reads=[sres, actR[kk][t]], writes=[bankR[base + t]])
                        resid_evac(5)(m)(t, base + t)
                    continue
                for (slot, sres, kb, nk) in slots:
                    for k in range(nk):
                        kk = kb + k
                        for t in range(NT):
                            S.add("pe", I("matmul",
                                PS[:, base + t, :], slot[:, k, mm * 128:(mm + 1) * 128], ACTB[:, kk, tile(t)],
                                start=(kk == 0), stop=(kk == NFF - 1)),
                                reads=[sres, actR[kk][t]], writes=[bankR[base + t]])
                for t in range(NT):
                    resid_evac(5)(m)(t, base + t)

    def final_phase():
        S.fence(xnbRs)

        def head(t):
            XNB, xnbR = XNBS[t % 2], xnbRs[t % 2]
            rb, rr = rms_stats(t)
            for k in range(8):
                S.add("dve", I("scalar_tensor_tensor", out=XNB[:, k, :], in0=X[:, k, tile(t)], scalar=SVT[:, 48 + k:49 + k],
                               in1=rb, op0=ALU.mult, op1=ALU.mult),
                      reads=[xR[k][t], rr, svR], writes=[xnbR])

        def tail(t):
            XNB, xnbR = XNBS[t % 2], xnbRs[t % 2]
            for cc in range(4):
                b = cc % 2
                for half in range(2):
                    bkk = 2 * b + half
                    for kk in range(4):
                        k = half * 4 + kk
                        S.add("pe", I("transpose", PS[:, bkk, kk * 128:(kk + 1) * 128], XNB[:, k, cc * 128:(cc + 1) * 128], IDF[:]),
                              reads=[xnbR, constR], writes=[bankR[bkk]])
                    if half == 0:
                        S.add("dve", I("tensor_copy", out=STG[:, b, half * 512:(half + 1) * 512], in_=PS[:, bkk, :]),
                              reads=[bankR[bkk]], writes=[stgR[b]])
                    else:
                        S.add("act", I("activation", out=STG[:, b, half * 512:(half + 1) * 512], in_=PS[:, bkk, :], func=AF.Copy),
                              reads=[bankR[bkk]], writes=[stgR[b]])
                c = t * 4 + cc
                S.add("sp", I("dma_start", out=yout[c * 128:(c + 1) * 128, :], in_=STG[:, b, :]),
                      reads=[stgR[b]], writes=[outR], dsem=outD[b])

        head(0)
        for t in range(NT):
            if t + 1 < NT:
                head(t + 1)
            tail(t)


    try:
        if stop_early:
            raise _Stop()
        CK("xload")
        ada0_it = ada_steps(0)
        for _ in range(8):
            next(ada0_it, None)
        CK("ada0")
        layer(0, ada0_it)
        CK("L0")
        layer(1)
        CK("L1")
        final_phase()
    except _Stop:
        pass

    def _unused():
        pass

    if taps:
        tapD = S.dsem("dtap")
        tapsrc = {"X": (X[:], [xR[k][t] for k in range(8) for t in range(NT)]),
                  "HY": (HY[:], [hyR[k][t] for k in range(8) for t in range(NT)])}
        for name in taps:
            ap_, res_ = tapsrc[name]
            S.add("sp", I("dma_start", out=tap_out[name], in_=ap_), reads=res_, writes=[outR], dsem=tapD)

    final_waits = [d for d in S.dsems if d.count > 0]

    S.finalize_counts()
    with nc.Block() as block:
        @block.tensor
        def _(e):
            S.emit("pe", e)

        @block.scalar
        def _(e):
            S.emit("act", e)

        @block.vector
        def _(e):
            S.emit("dve", e)

        @block.gpsimd
        def _(e):
            S.emit("pool", e)

        @block.sync
        def _(e):
            S.emit("sp", e)
            for d in final_waits:
                e.wait_ge(d.sem, d.count)
    return nc


def _consts():
    ident = np.eye(128, dtype=np.float32)
    m = np.arange(128)
    partner = np.where((m % 32) < 16, m + 16, m - 16)
    pm = np.zeros((128, 128), np.float32)
    pm[partner, m] = 1.0
    t = np.arange(1024)
    rows = (t // 64).astype(np.float32)
    cols = (t % 64).astype(np.float32)
    inv = (1.0 / (10000.0 ** (np.arange(0, 32, 2, dtype=np.float32) / 32))).astype(np.float32)
    ar, ac = rows[:, None] * inv[None], cols[:, None] * inv[None]
    cr, sr, cc_, sc_ = np.cos(ar), np.sin(ar), np.cos(ac), np.sin(ac)
    C = np.zeros((128, 1024), np.float32)
    Sg = np.zeros((128, 1024), np.float32)
    for p in range(128):
        i = p % 16
        is_col = (p % 64) >= 32
        sign = -1.0 if (p % 32) < 16 else 1.0
        C[p] = (cc_ if is_col else cr)[:, i]
        Sg[p] = sign * (sc_ if is_col else sr)[:, i]
    invw = np.zeros((128, 2), np.float32)
    edge = np.zeros((128, 2, 16), np.float32)
    for j in range(2):
        for half in range(2):
            w = POOL_WINDOWS[2 * j + half]
            ps = slice(half * 64, half * 64 + 64)
            invw[ps, j] = 1.0 / w
            L = 1024
            tt = np.arange(L)
            lo = np.clip(tt - w // 2, 0, L)
            hi = np.clip(tt + w - w // 2, 0, L)
            cnt = (hi - lo).astype(np.float32)
            edge[ps, j, 0:8] = (1.0 / cnt[0:8])[None, :]
            edge[ps, j, 8:16] = (1.0 / cnt[L - 8:L])[None, :]
    return ident, pm, C.astype(np.float32), Sg.astype(np.float32), invw, edge


_NC_CACHE = {}


def kernel(x_prompt, x_sample, cache_k, cache_v, c, c_ctx, norm1_g, w_ada, b_ada, w_in,
           w_pool, pool_scale, lam_q1, lam_k1, lam_q2, lam_k2, subln_g, sgu_norm_g,
           w_sgu, b_sgu, w_out, norm2_g, w_ffn_in, w_ffn_out, final_g, _taps=None, _ncores=8, _stop=None):
    f = lambda a: np.ascontiguousarray(np.asarray(a, dtype=np.float32))
    x_prompt, x_sample, cache_k, cache_v = f(x_prompt), f(x_sample), f(cache_k), f(cache_v)
    ident, pm, ropec, ropes, invw, edge = _consts()
    shared = {
        "w_ada": f(w_ada), "w_in": f(w_in), "w_out": f(w_out), "w_ffn_in": f(w_ffn_in), "w_ffn_out": f(w_ffn_out),
        "w_pool": f(w_pool), "w_sT": np.ascontiguousarray(np.transpose(f(w_sgu), (0, 1, 3, 2))),
        "b_sgu": f(b_sgu), "sgu_norm_g": f(sgu_norm_g),
        "lamv": np.ascontiguousarray(np.stack([f(lam_q1), f(lam_k1), f(lam_q2), f(lam_k2)], axis=1).reshape(1, 512)),
        "c_ident": ident, "c_pm": pm, "c_ropec": ropec, "c_ropes": ropes, "c_invw": invw, "c_edge": edge,
    }
    key = (tuple(sorted(_taps.items())) if _taps else None, _stop)
    if key not in _NC_CACHE:
        _NC_CACHE[key] = build_program(_taps, _stop)
    nc = _NC_CACHE[key]
    in_maps = []
    for i in range(_ncores):
        sv = np.zeros((2, 128, 128), np.float32)
        sv[0, 0:8] = f(c_ctx).reshape(8, 128)
        sv[0, 8:16] = f(c)[i].reshape(8, 128)
        sv[0, 16:32] = f(norm1_g).reshape(16, 128)
        sv[0, 32:48] = f(norm2_g).reshape(16, 128)
        sv[0, 48:56] = f(final_g).reshape(8, 128)
        sv[0, 56:60] = f(pool_scale).reshape(4, 128)
        sv[0, 60:62] = f(subln_g).reshape(2, 128)
        sv[1, 0:96] = f(b_ada).reshape(96, 128)
        m = dict(shared)
        m["xin"] = np.ascontiguousarray(np.concatenate([x_prompt[2 * i], x_prompt[2 * i + 1], x_sample[i]], axis=0))
        m["ck"] = np.ascontiguousarray(cache_k[i].reshape(2, 256, 512))
        m["cv"] = np.ascontiguousarray(cache_v[i].reshape(2, 256, 512))
        m["smallvec"] = sv
        in_maps.append(m)
    res = run_bass_kernel_spmd(nc, in_maps, core_ids=list(range(_ncores)))
    outs = res.results
    nb = 2 * _ncores
    y_prompt = np.zeros((nb, 256, D), np.float32)
    y_sample = np.zeros((_ncores, 1024, D), np.float32)
    nk = np.zeros((nb, 2, 256, 4, 128), np.float32)
    nv = np.zeros((nb, 2, 256, 4, 128), np.float32)
    for i in range(_ncores):
        y = outs[i]["yout"]
        y_prompt[2 * i] = y[0:256]
        y_prompt[2 * i + 1] = y[256:512]
        y_sample[i] = y[512:1536]
        nk[2 * i:2 * i + 2] = outs[i]["nk"].reshape(2, 2, 256, 4, 128)
        nv[2 * i:2 * i + 2] = outs[i]["nv"].reshape(2, 2, 256, 4, 128)
    if _taps:
        kernel._taps_out = [{k: outs[i]["tap_" + k] for k in _taps} for i in range(_ncores)]
    return (y_prompt, y_sample, nk, nv)
```

```python
import math
import numpy as np
import concourse.bass as bass
import concourse.mybir as mybir
from concourse.bass_utils import run_bass_kernel_spmd

F32 = mybir.dt.float32
BF16 = mybir.dt.bfloat16
AF = mybir.ActivationFunctionType
ALU = mybir.AluOpType
AX = mybir.AxisListType

D = 1024
NTOK = 1536
NT = 3
DFF = 2816
NFF = 22
EPS = 1e-6
POOL_WINDOWS = (2, 4, 8, 16)
PADW = 1584
SEQ_PAD0 = (8, 280, 552)
SEQ_TOK0 = (0, 256, 512)
SEQ_LEN = (256, 256, 1024)
RING_SLOTS = 4
NWARM = 0
PIECE = 256


class Res:
    __slots__ = ("name", "w", "r", "excl")

    def __init__(self, name, excl=False):
        self.name = name
        self.w = None
        self.r = []
        self.excl = excl


class DSem:
    def __init__(self, nc, name):
        self.sem = nc.alloc_semaphore(name)
        self.count = 0
        self.last = None


class Op:
    __slots__ = ("eng", "idx", "fn", "waits", "sig", "dsem", "dcount", "sigcount")


class Sched:
    ENG = ("pe", "act", "dve", "pool", "sp")

    def __init__(self, nc):
        self.nc = nc
        self.ops = {e: [] for e in self.ENG}
        self.waited = {e: {} for e in self.ENG}
        self.sems = {e: nc.alloc_semaphore("sem_" + e) for e in ("pe", "act", "dve", "pool")}
        self.dsems = []
        self.also = {}

    def dsem(self, name):
        d = DSem(self.nc, name)
        self.dsems.append(d)
        return d

    def add(self, eng, fn, reads=(), writes=(), dsem=None):
        op = Op()
        op.eng = eng
        op.fn = fn
        op.idx = len(self.ops[eng])
        op.sig = False
        op.dsem = dsem
        op.dcount = 0
        op.sigcount = 0
        if dsem is not None:
            dsem.count += 16
            op.dcount = dsem.count
            dsem.last = op
        reads = list(reads)
        for r in list(reads):
            if r in self.also:
                reads.extend(self.also[r])
        deps = []
        for r in reads:
            if r.w is not None:
                deps.append(r.w)
            if r.excl:
                deps.extend(x for x in r.r if x.eng != eng)
        for w in writes:
            if w.w is not None:
                deps.append(w.w)
            deps.extend(w.r)
        best = {}
        for d in deps:
            if d is op:
                continue
            if d.dsem is not None:
                key, order = d.dsem, d.dcount
            else:
                if d.eng == "pe" and eng == "pe" and dsem is None:
                    continue
                key, order = d.eng, d.idx
            if key not in best or best[key][0] < order:
                best[key] = (order, d)
        op.waits = []
        wd = self.waited[eng]
        for key, (order, d) in best.items():
            if wd.get(key, -1) >= order:
                continue
            wd[key] = order
            d.sig = True
            op.waits.append(d)
        for r in reads:
            r.r.append(op)
        for w in writes:
            w.w = op
            w.r = []
        self.ops[eng].append(op)
        return op

    def fence(self, news):
        lst = []
        for e in self.ENG:
            if self.ops[e]:
                for o in reversed(self.ops[e]):
                    if o.dsem is None:
                        lst.append(o)
                        break
        for d in self.dsems:
            if d.last is not None:
                lst.append(d.last)
        for r in news:
            r.w = None
            r.r = list(lst)

    def transfer(self, olds, news):
        lst = []
        for o in olds:
            if o.w is not None:
                lst.append(o.w)
            lst.extend(o.r)
        for r in news:
            r.w = None
            r.r = list(lst)

    def emit(self, eng, e):
        cnt = 0
        for op in self.ops[eng]:
            for d in op.waits:
                if d.dsem is not None:
                    e.wait_ge(d.dsem.sem, d.dcount)
                else:
                    e.wait_ge(self.sems[d.eng], d.sigcount)
            ins = op.fn(e)
            if op.dsem is not None:
                ins.then_inc(op.dsem.sem, 16)
            elif op.sig:
                ins.then_inc(self.sems[eng], 1)

    def finalize_counts(self):
        for eng in self.ENG:
            c = 0
            for op in self.ops[eng]:
                if op.dsem is None and op.sig:
                    c += 1
                op.sigcount = c


def I(name, *a, **kw):
    return lambda e: getattr(e, name)(*a, **kw)


def lam_init(l):
    return 0.8 - 0.6 * math.exp(-0.3 * l)


class _Stop(Exception):
    pass


def build_program(taps=None, stop=None):
    nc = bass.Bass("TRN2", target_bir_lowering=False)
    S = Sched(nc)

    def CK(name):
        if stop == name:
            raise _Stop()

    def din(name, shape):
        return nc.dram_tensor(name, list(shape), F32, kind="ExternalInput").ap()

    def dout(name, shape):
        return nc.dram_tensor(name, list(shape), F32, kind="ExternalOutput").ap()

    xin = din("xin", [NTOK, D])
    ck = din("ck", [2, 256, 512])
    cv = din("cv", [2, 256, 512])
    w_ada = din("w_ada", [2, D, 6 * D])
    w_in = din("w_in", [2, D, 2304])
    w_out = din("w_out", [2, D, D])
    w_fi = din("w_ffn_in", [2, D, 2 * DFF])
    w_fo = din("w_ffn_out", [2, DFF, D])
    w_pool = din("w_pool", [2, 4, 64, 64])
    w_sT = din("w_sT", [2, 4, 128, 128])
    b_sgu = din("b_sgu", [2, 4, 128])
    sgn = din("sgu_norm_g", [2, 256])
    lamv = din("lamv", [1, 512])
    sv_d = din("smallvec", [2, 128, 128])
    c_ident = din("c_ident", [128, 128])
    c_pm = din("c_pm", [128, 128])
    c_ropec = din("c_ropec", [128, 1024])
    c_ropes = din("c_ropes", [128, 1024])
    c_invw = din("c_invw", [128, 2])
    c_edge = din("c_edge", [128, 2, 16])
    yout = dout("yout", [NTOK, D])
    nk_o = dout("nk", [2, 2, 256, 512])
    nv_o = dout("nv", [2, 2, 256, 512])
    tap_out = {}
    if taps:
        for name, shape in taps.items():
            tap_out[name] = dout("tap_" + name, shape)

    budget0 = nc.sbuf_bytes_remaining

    def sb(name, shape, dt):
        return nc.alloc_sbuf_tensor(name, list(shape), dt)

    X = sb("X", [128, 8, NTOK], F32)
    HY = sb("HY", [128, 8, NTOK], BF16)
    RING = sb("RING", [128, RING_SLOTS, 8 * PIECE], BF16)
    ROPEC = sb("ROPEC", [128, 1024], BF16)
    ROPES = sb("ROPES", [128, 1024], BF16)
    IDF = sb("IDF", [128, 128], F32)
    ONES = sb("ONES", [128, 128], BF16)
    OTOP = sb("OTOP", [128, 128], BF16)
    OBOT = sb("OBOT", [128, 128], BF16)
    PM = sb("PM", [128, 128], BF16)
    WPBD = sb("WPBD", [128, 4, 128], BF16)
    WST = sb("WST", [128, 8, 128], BF16)
    BSG = sb("BSG", [128, 4, 128], F32)
    GN = sb("GN", [128, 2, 256], F32)
    SV = sb("SV", [128, 256], F32)
    SCT = sb("SCT", [128, 8, 2], BF16)
    MOD = sb("MOD", [128, 2, 48, 2], F32)
    A1 = sb("A1", [128, 2, 8, 2], F32)
    A2 = sb("A2", [128, 2, 8, 2], F32)
    LAM = sb("LAM", [128, 16], F32)
    GSUB = sb("GSUB", [128, 2], F32)
    PSC = sb("PSC", [128, 4], F32)
    INVW = sb("INVW", [128, 2], F32)
    EDGE = sb("EDGE", [128, 2, 16], F32)
    NBS = sb("NBS", [128, 8, 8], F32)
    NBA = sb("NBA", [128, 16], F32)
    ST6 = sb("ST6", [128, 48], F32)
    RSTD = sb("RSTD", [128, 512], F32)
    STG = sb("STG", [128, 2, 1024], F32)
    POOLED = sb("POOLED", [128, 2, NTOK], BF16)
    TMPF = sb("TMPF", [128, 4, 512], F32)
    SQ0 = sb("SQ0", [128, 1280], BF16)
    SQ1 = sb("SQ1", [128, 1280], BF16)
    SQ = None
    XB = sb("XB", [128, 2, 512], BF16)
    MROW = sb("MROW", [2, 256], F32)
    SVT = sb("SVT", [128, 256], F32)
    EPSB = sb("EPSB", [128, 1], F32)

    A_QT, A_KT, A_VT, A_UT = 0, 12288, 26624, 40960
    A_SH = 47104
    ARENA_BYTES = max(A_SH + 4 * PADW * 4, NFF * NTOK * 2)
    ARENA = sb("ARENA", [128, ARENA_BYTES // 2], BF16)

    def aview(off, nbytes, dt):
        v = ARENA[:, off // 2:(off + nbytes) // 2]
        if dt is F32:
            v = v.bitcast(F32)
        return v

    QT = aview(A_QT, 12288, BF16).rearrange("p (h n) -> p h n", h=4)
    KT = aview(A_KT, 14336, BF16).rearrange("p (h n) -> p h n", h=4)
    VT = aview(A_VT, 14336, BF16).rearrange("p (c n) -> p c n", c=14)
    UT = aview(A_UT, 6144, BF16).rearrange("p (j n) -> p j n", j=2)
    PP = aview(A_SH, 2 * PADW * 4, F32).rearrange("p (j n) -> p j n", j=2)
    PA = aview(A_SH + 2 * PADW * 4, PADW * 4, F32)
    PB = aview(A_SH + 3 * PADW * 4, PADW * 4, F32)
    VNP = aview(A_SH, 8192, BF16).rearrange("p (b c g n) -> p b c g n", b=2, c=4, g=4)
    EB = aview(A_SH + 8192, 8192, BF16).rearrange("p (i n) -> p i n", i=8)
    ATT = None
    XNBS = [aview(o, 16384, F32).rearrange("p (k n) -> p k n", k=8) for o in (0, 16384)]
    ACTB = aview(0, NFF * NTOK * 2, BF16).rearrange("p (j n) -> p j n", j=NFF)

    used = budget0 - nc.sbuf_bytes_remaining
    assert nc.sbuf_bytes_remaining > 128, f"SBUF over budget: used {used}"

    PS = nc.alloc_psum_tensor("PS", [128, 8, 512], F32)

    xR = [[Res(f"x{k}_{t}") for t in range(NT)] for k in range(8)]
    hyR = [[Res(f"hy{k}_{t}") for t in range(NT)] for k in range(8)]
    bankR = [Res(f"bank{i}", excl=True) for i in range(8)]
    ringR = [Res(f"ring{i}") for i in range(RING_SLOTS)]
    ringD = [S.dsem(f"dring{i}") for i in range(RING_SLOTS)]
    constR = Res("consts")
    constD = S.dsem("dconst")
    constR2 = Res("consts2")
    constD2 = S.dsem("dconst2")
    S.also[constR] = [constR2]
    stgR = [Res("stg0"), Res("stg1")]
    stgD = [S.dsem("dstg0"), S.dsem("dstg1")]
    outD = [S.dsem("dout0"), S.dsem("dout1")]
    cvD = [S.dsem("dcv0"), S.dsem("dcv1")]
    rstdR = Res("rstd")
    tmpR = [Res(f"tmp{i}") for i in range(4)]
    sqR = [Res("sq0"), Res("sq1")]
    xbR = [Res("xb0"), Res("xb1")]
    mrowR = Res("mrow")
    modR = [Res("mod0"), Res("mod1")]
    svR = Res("sv")
    miscR = Res("misc")
    qR = [[Res(f"q{h}_{t}") for t in range(NT)] for h in range(4)]
    kR = [[Res(f"k{h}_{t}") for t in range(NT)] for h in range(4)]
    kcR = [Res(f"kc{h}") for h in range(4)]
    vR = [Res(f"v{c}") for c in range(14)]
    uR = [[Res(f"u{j}_{t}") for t in range(NT)] for j in range(2)]
    ppR = [Res("pp0"), Res("pp1")]
    paR, pbR = Res("pa"), Res("pb")
    pooledR = [Res("pooled0"), Res("pooled1")]
    vnpR = [Res("vnp0"), Res("vnp1")]
    ebR = [Res(f"eb{i}") for i in range(8)]
    nbsR = [Res(f"nbs{i}") for i in range(8)]
    nbaR = [Res(f"nba{i}") for i in range(8)]
    attR = Res("att")
    nbR = Res("nb")
    st6R = Res("st6")
    actR = [[Res(f"act{j}_{t}") for t in range(NT)] for j in range(NFF)]
    xnbRs = [Res("xnb0"), Res("xnb1")]
    outR = Res("outdram")

    tile = lambda t: slice(t * 512, (t + 1) * 512)
    cond_of = (0, 1, 1)

    ring_state = {"n": 0}

    def load_piece(src_ap, nk=8, ncols=PIECE):
        i = ring_state["n"] % RING_SLOTS
        ring_state["n"] += 1
        dst = RING[:, i, :].rearrange("p (k n) -> p k n", k=8)
        S.add("pool", lambda e, d=dst[:, 0:nk, 0:ncols], s=src_ap: e.dma_start(out=d, in_=s),
              writes=[ringR[i]], dsem=ringD[i])
        return dst, ringR[i]

    def wrows(w2d, c0, ncols=PIECE, k0=0, nk=8):
        return w2d.rearrange("(k p) n -> p k n", p=128)[:, k0:k0 + nk, c0:c0 + ncols]

    wpbd0R = Res("wpbd0")

    def cdma(eng, dst, src, reads=()):
        S.add(eng, I("dma_start", out=dst, in_=src), reads=list(reads), writes=[Res("c")], dsem=(constD if eng == "sp" else constD2))

    S.add("dve", I("memset", ONES[:], 1.0), writes=[miscR])
    S.add("dve", I("memset", OTOP[:], 0.0), writes=[miscR])
    S.add("dve", I("memset", OTOP[0:64, :], 1.0), writes=[miscR])
    S.add("dve", I("memset", OBOT[:], 0.0), writes=[miscR])
    S.add("dve", I("memset", OBOT[64:128, :], 1.0), writes=[miscR])
    S.add("dve", I("memset", WPBD[:], 0.0), writes=[wpbd0R])
    cdma("sp", IDF[:], c_ident)
    cdma("sp", SV[:].rearrange("p (a n) -> p a n", a=2), sv_d.rearrange("a p n -> p a n"))
    cdma("pool", ROPEC[:], c_ropec)
    cdma("pool", ROPES[:], c_ropes)
    cdma("sp", INVW[:], c_invw)
    cdma("sp", EDGE[:], c_edge)
    cdma("sp", TMPF[:, 0, :], lamv.partition_broadcast(128))
    for l in range(2):
        cdma("sp", GN[:, l, :], sgn[l:l + 1, :].partition_broadcast(128))
        for g in range(4):
            cdma("sp", BSG[(g % 2) * 64:(g % 2) * 64 + 64, l * 2 + g // 2, :],
                 b_sgu[l, g:g + 1, :].partition_broadcast(64))
            cdma("pool", WPBD[(g % 2) * 64:(g % 2) * 64 + 64, l * 2 + g // 2, (g % 2) * 64:(g % 2) * 64 + 64],
                 w_pool[l, g], reads=[wpbd0R])
            cdma("pool", WST[:, l * 4 + g, :], w_sT[l, g])
    cdma("pool", PM[:], c_pm)
    constR.w = constD.last
    constR2.w = constD2.last
    stop_early = False
    try:
        CK("consts")
    except _Stop:
        stop_early = True

    for a in range(2):
        S.add("pe", I("transpose", PS[:, 6, a * 128:(a + 1) * 128], SV[:, a * 128:(a + 1) * 128], IDF[:]),
              reads=[constR], writes=[bankR[6]])
    S.add("dve", I("tensor_copy", out=SVT[:], in_=PS[:, 6, 0:256]), reads=[bankR[6]], writes=[svR])
    S.add("act", I("activation", out=SCT[:].rearrange("p k c -> p c k"),
                                        in_=SVT[:, 0:16].rearrange("p (c k) -> p c k", c=2), func=AF.Silu),
          reads=[svR], writes=[miscR])
    lv = TMPF[:, 0, :].rearrange("p (l b two n) -> p l b two n", l=2, b=2, two=2)
    S.add("dve", I("tensor_tensor", out=TMPF[:, 1, 0:256].rearrange("p (l b n) -> p l b n", l=2, b=2),
                                           in0=lv[:, :, :, 0, :], in1=lv[:, :, :, 1, :], op=ALU.mult),
          reads=[constR], writes=[tmpR[1]])
    S.add("dve", I("tensor_reduce", out=LAM[:, 0:4], in_=TMPF[:, 1, 0:256].rearrange("p (a n) -> p a n", a=4),
                                           axis=AX.X, op=ALU.add), reads=[tmpR[1]], writes=[miscR])
    S.add("act", I("activation", out=LAM[:, 4:8], in_=LAM[:, 0:4], func=AF.Exp), reads=[miscR], writes=[miscR])
    for l in range(2):
        S.add("dve", I("scalar_tensor_tensor", out=LAM[:, 8 + l:9 + l], in0=LAM[:, 5 + 2 * l:6 + 2 * l],
                                                           scalar=-lam_init(l), in1=LAM[:, 4 + 2 * l:5 + 2 * l],
                                                           op0=ALU.add, op1=ALU.subtract),
              reads=[miscR], writes=[miscR])
        S.add("dve", I("tensor_scalar", out=GSUB[:, l:l + 1], in0=SVT[:, 60 + l:61 + l],
                                                    scalar1=1.0 - lam_init(l), scalar2=None, op0=ALU.mult),
              reads=[svR], writes=[miscR])

    for c in range(12):
        b = c % 2
        S.add("sp", I("dma_start", out=STG[:, b, :], in_=xin[c * 128:(c + 1) * 128, :]),
              writes=[stgR[b]], dsem=stgD[b])
        for half in range(2):
            bk = 2 * b + half
            for kk in range(4):
                k = half * 4 + kk
                S.add("pe", I("transpose", PS[:, bk, kk * 128:(kk + 1) * 128],
                                                                          STG[:, b, k * 128:(k + 1) * 128], IDF[:]),
                      reads=[stgR[b], constR], writes=[bankR[bk]])
            eng = "dve" if half == 0 else "act"
            dst = X[:, half * 4:half * 4 + 4, c * 128:(c + 1) * 128]
            src = PS[:, bk, :].rearrange("p (k n) -> p k n", k=4)
            t = c // 4
            if eng == "dve":
                S.add("dve", I("tensor_copy", out=dst, in_=src), reads=[bankR[bk]],
                      writes=[xR[half * 4 + i][t] for i in range(4)])
            else:
                S.add("act", I("activation", out=dst, in_=src, func=AF.Copy), reads=[bankR[bk]],
                      writes=[xR[half * 4 + i][t] for i in range(4)])

    def ada(l):
        for _ in ada_steps(l):
            pass

    def ada_steps(l):
        for j in range(24):
            slot, sres = load_piece(wrows(w_ada[l], j * PIECE))
            for k in range(8):
                S.add("pe", I("matmul", PS[0:2, 7, 0:PIECE], SCT[:, k, :], slot[:, k, :], start=(k == 0), stop=(k == 7)),
                      reads=[sres, miscR], writes=[bankR[7]])
            S.add("act", I("activation", out=MROW[:, 0:PIECE], in_=PS[0:2, 7, 0:PIECE], func=AF.Copy),
                  reads=[bankR[7]], writes=[mrowR])
            for i in range(2):
                S.add("pe", I("transpose", PS[:, 7, 256 + 2 * i:258 + 2 * i], MROW[:, i * 128:(i + 1) * 128], IDF[0:2, 0:2]),
                      reads=[mrowR, constR], writes=[bankR[7]])
            a0 = 2 * j
            S.add("dve", I("tensor_tensor", out=MOD[:, l, a0:a0 + 2, :], in0=PS[:, 7, 256:260].rearrange("p (a c) -> p a c", c=2),
                           in1=SVT[:, 128 + 48 * l + a0:130 + 48 * l + a0].unsqueeze(2).to_broadcast([128, 2, 2]), op=ALU.add),
                  reads=[bankR[7], svR], writes=[modR[l]])
            for (AA, goff, moff, jdone) in ((A1, 16, 8, 7), (A2, 32, 32, 19)):
                if j == jdone:
                    S.add("dve", I("scalar_tensor_tensor", out=AA[:, l, :, :], in0=MOD[:, l, moff:moff + 8, :], scalar=1.0,
                                   in1=SVT[:, goff + 8 * l:goff + 8 * l + 8].unsqueeze(2).to_broadcast([128, 8, 2]),
                                   op0=ALU.add, op1=ALU.mult), reads=[modR[l], svR], writes=[modR[l]])
            if j < 23:
                yield j

    def modap(l, i, k, c):
        return MOD[:, l, i * 8 + k, c:c + 1]

    def rms_stats(t, bank=6):
        rb, rr = ((RSTD[:], rstdR), (TMPF[:, 3, :], tmpR[3]))[t % 2]
        for k in range(8):
            b = k % 2
            S.add("act", I("activation", out=(SQ0, SQ1)[b][:, 0:512], in_=X[:, k, tile(t)], func=AF.Square),
                  reads=[xR[k][t]], writes=[sqR[b]])
            S.add("pe", I("matmul", PS[:, bank, :], ONES[:], (SQ0, SQ1)[b][:, 0:512], start=(k == 0), stop=(k == 7)),
                  reads=[sqR[b], miscR], writes=[bankR[bank]])
        S.add("act", I("activation", out=rb, in_=PS[:, bank, :], func=AF.Ln, scale=1.0 / D, bias=EPSB[:]),
              reads=[bankR[bank], miscR], writes=[rr])
        S.add("act", I("activation", out=rb, in_=rb, func=AF.Exp, scale=-0.5), reads=[rr], writes=[rr])
        return rb, rr

    S.add("dve", I("memset", EPSB[:], EPS), writes=[miscR])

    def norm_apply(l, AA, shi, t, st):
        rb, rr = st
        c = cond_of[t]
        for k in range(8):
            b = k % 2
            S.add("dve", I("tensor_tensor", out=TMPF[:, b, :], in0=X[:, k, tile(t)], in1=rb, op=ALU.mult),
                  reads=[xR[k][t], rr], writes=[tmpR[b]])
            if k % 2 == 0:
                S.add("act", I("activation", out=HY[:, k, tile(t)], in_=TMPF[:, b, :], func=AF.Identity,
                               scale=AA[:, l, k, c:c + 1], bias=modap(l, shi, k, c)),
                      reads=[tmpR[b], modR[l]], writes=[hyR[k][t]])
            else:
                S.add("dve", I("tensor_scalar", out=HY[:, k, tile(t)], in0=TMPF[:, b, :], scalar1=AA[:, l, k, c:c + 1],
                               scalar2=modap(l, shi, k, c), op0=ALU.mult, op1=ALU.add),
                      reads=[tmpR[b], modR[l]], writes=[hyR[k][t]])

    def norm_seq(l, AA, shi, filler=None, after=None):
        for t in range(NT):
            st = rms_stats(t)
            if filler is not None:
                filler()
            norm_apply(l, AA, shi, t, st)
            if after is not None:
                after(t)

    fb = {"i": 0}

    def fm_one(slot, sres, col0, nk, t, evac):
        bk = fb["i"] % 6
        fb["i"] += 1
        for k in range(nk):
            S.add("pe", I("matmul", PS[:, bk, :], slot[:, k, col0:col0 + 128], HY[:, k, tile(t)], start=(k == 0), stop=(k == nk - 1)),
                  reads=[sres, hyR[k][t]], writes=[bankR[bk]])
        evac(t, bk)

    dstate = {"set": 0}

    def fm_group(slot, sres, col0, nk, rhs_fn, rhs_res_fn, evac, t_outer=False):
        base = 3 * dstate["set"]
        dstate["set"] ^= 1
        order = [(k, t) for t in range(NT) for k in range(nk)] if t_outer else [(k, t) for k in range(nk) for t in range(NT)]
        for (k, t) in order:
            S.add("pe", I("matmul", PS[:, base + t, :], slot[:, k, col0:col0 + 128], rhs_fn(k, t),
                          start=(k == 0), stop=(k == nk - 1)),
                  reads=[sres] + rhs_res_fn(k, t), writes=[bankR[base + t]])
            if t_outer and k == nk - 1:
                evac(t, base + t)
        if not t_outer:
            for t in range(NT):
                evac(t, base + t)

    hy_rhs = lambda k, t: HY[:, k, tile(t)]
    hy_res = lambda k, t: [hyR[k][t]]

    kcol_of_tile = (1280, 256, 768)

    def gelu_to(dst_fn, src_ap, n, dres, sres_list, tb=0):
        a, w = TMPF[:, tb, 0:n], TMPF[:, tb + 1, 0:n]
        S.add("act", I("activation", out=a, in_=src_ap, func=AF.Square, scale=math.sqrt(0.044715)),
              reads=sres_list, writes=[tmpR[tb]])
        S.add("dve", I("scalar_tensor_tensor", out=w, in0=a, scalar=1.0, in1=src_ap, op0=ALU.add, op1=ALU.mult),
              reads=[tmpR[tb]] + sres_list, writes=[tmpR[tb + 1]])
        S.add("act", I("activation", out=a, in_=w, func=AF.Sigmoid, scale=1.5957691216057308),
              reads=[tmpR[tb + 1]], writes=[tmpR[tb]])
        S.add("dve", I("tensor_tensor", out=dst_fn, in0=src_ap, in1=a, op=ALU.mult),
              reads=[tmpR[tb]] + sres_list, writes=dres)

    def layer(l, ada_cur=iter(())):
        def ada_more(n):
            for _ in range(n):
                next(ada_cur, None)

        S.fence([r for hh in qR for r in hh] + [r for hh in kR for r in hh] + kcR + vR + [r for uu in uR for r in uu]
                + ppR + [paR, pbR])
        S.add("dve", I("memset", PP[:, :, :], 0.0), writes=ppR)

        for cc in range(2):
            S.add("sp", I("dma_start", out=STG[:, cc, 0:512], in_=ck[l, cc * 128:(cc + 1) * 128, :]),
                  writes=[stgR[cc]], dsem=stgD[cc])
            S.add("pool", I("dma_start", out=VT[:, cc, :], in_=cv[l, cc * 128:(cc + 1) * 128, :]),
                  writes=[vR[cc]], dsem=cvD[cc])
        for cc in range(2):
            for h in range(4):
                S.add("pe", I("transpose", PS[:, 6 + cc, h * 128:(h + 1) * 128],
                                                              STG[:, cc, h * 128:(h + 1) * 128], IDF[:]),
                      reads=[stgR[cc], constR], writes=[bankR[6 + cc]])
            S.add("act", I("activation", out=KT[:, :, cc * 128:(cc + 1) * 128],
                                                       in_=PS[:, 6 + cc, :].rearrange("p (h n) -> p h n", h=4), func=AF.Copy),
                  reads=[bankR[6 + cc]], writes=kcR)

        W = w_in[l]

        def ev_pool_of(j):
            def ev_pool(t, bk):
                if t == 0:
                    dst = PP[:, j, 8:552].rearrange("p (s w) -> p s w", w=272)[:, :, 0:256]
                    src = PS[:, bk, :].rearrange("p (s w) -> p s w", w=256)
                else:
                    dst = PP[:, j, 552 + (t - 1) * 512:552 + t * 512]
                    src = PS[:, bk, :]
                S.add("act", I("activation", out=dst, in_=src, func=AF.Copy), reads=[bankR[bk]], writes=[ppR[j]])
            return ev_pool

        def ev_qk(dstT, dres, h, colfn):
            def ev(t, bk):
                if t == 0:
                    S.add("act", I("activation", out=dstT[:, h, colfn(t):colfn(t) + 512], in_=PS[:, bk, :], func=AF.Copy),
                          reads=[bankR[bk]], writes=[dres[h][t]])
                    return
                b = t % 2
                rb = 6 + b
                cs = slice((t - 1) * 512, t * 512)
                S.add("act", I("activation", out=XB[:, b, :], in_=PS[:, bk, :], func=AF.Copy), reads=[bankR[bk]], writes=[xbR[b]])
                S.add("pe", I("matmul", PS[:, rb, :], PM[:], XB[:, b, :], start=True, stop=True),
                      reads=[xbR[b], constR], writes=[bankR[rb]])
                S.add("dve", I("tensor_tensor", out=TMPF[:, 2 * b, :], in0=PS[:, bk, :], in1=ROPEC[:, cs], op=ALU.mult),
                      reads=[bankR[bk], constR], writes=[tmpR[2 * b]])
                S.add("dve", I("tensor_tensor", out=TMPF[:, 2 * b + 1, :], in0=PS[:, rb, :], in1=ROPES[:, cs], op=ALU.mult),
                      reads=[bankR[rb], constR], writes=[tmpR[2 * b + 1]])
                S.add("dve", I("tensor_tensor", out=dstT[:, h, colfn(t):colfn(t) + 512], in0=TMPF[:, 2 * b, :],
                               in1=TMPF[:, 2 * b + 1, :], op=ALU.add),
                      reads=[tmpR[2 * b], tmpR[2 * b + 1]], writes=[dres[h][t]])
            return ev

        qcol = lambda t: t * 512
        kcol = lambda t: kcol_of_tile[t]
        def load_first():
            first = []
            for (c0, evs) in ((0, [ev_pool_of(0), ev_pool_of(1)]),
                              (256, [ev_qk(QT, qR, 0, qcol), ev_qk(QT, qR, 1, qcol)]),
                              (512, [ev_qk(QT, qR, 2, qcol), ev_qk(QT, qR, 3, qcol)]),
                              (768, [ev_qk(KT, kR, 0, kcol), ev_qk(KT, kR, 1, kcol)])):
                slot, sres = load_piece(wrows(W, c0))
                first.append((slot, sres, evs))
            return first

        def first_groups(first, t):
            for (slot, sres, evs) in first:
                for i, ev in enumerate(evs):
                    fm_one(slot, sres, i * 128, 8, t, ev)

        if l == 0:
            norm_seq(l, A1, 0, filler=lambda: ada_more(6))
            first = load_first()
            for t in range(NT):
                first_groups(first, t)
        else:
            first = load_first()
            norm_seq(l, A1, 0, after=lambda t: first_groups(first, t))
        CK(f"normA{l}")
        k0slot, k0res = first[3][0], first[3][1]

        def tokmajor_k(hp, slot, sres):
            for c in range(4):
                bk = 4 + (c % 4)
                for k in range(8):
                    S.add("pe", I("matmul", PS[:, bk, 0:256], HY[:, k, c * 128:(c + 1) * 128], slot[:, k, :], start=(k == 0), stop=(k == 7)),
                          reads=[sres, hyR[k][0]], writes=[bankR[bk]])
                b = c % 2
                S.add("act", I("activation", out=STG[:, b, 0:256], in_=PS[:, bk, 0:256], func=AF.Copy), reads=[bankR[bk]], writes=[stgR[b]])
                S.add("sp", I("dma_start", out=nk_o[c // 2, l, (c % 2) * 128:(c % 2) * 128 + 128, hp * 256:(hp + 1) * 256], in_=STG[:, b, 0:256]),
                      reads=[stgR[b]], writes=[outR], dsem=outD[b])

        tokmajor_k(0, k0slot, k0res)

        Wd = PADW

        def shift_add(dst, src, lo, hi, sh_a, sh_b, dres, sres_, p0=0, p1=128):
            S.add("dve", I("tensor_tensor", out=dst[p0:p1, lo:hi], in0=src[p0:p1, lo + sh_a:hi + sh_a],
                                                   in1=src[p0:p1, lo + sh_b:hi + sh_b], op=ALU.add),
                  reads=sres_, writes=dres)

        def pool_finish(j, lo_src, hi_src, lo_res, hi_res):
            for (p0, p1, src, sres_) in ((0, 64, lo_src, lo_res), (64, 128, hi_src, hi_res)):
                for s in range(3):
                    a0, t0, L = SEQ_PAD0[s], SEQ_TOK0[s], SEQ_LEN[s]
                    S.add("dve", I("scalar_tensor_tensor",
                        out=POOLED[p0:p1, j, t0:t0 + L], in0=src[p0:p1, a0:a0 + L], scalar=INVW[p0:p1, j:j + 1],
                        in1=PP[p0:p1, j, a0:a0 + L], op0=ALU.mult, op1=ALU.subtract),
                        reads=[sres_, ppR[j], constR], writes=[pooledR[j]])
                    for (ca, ea) in ((0, 0), (L - 8, 8)):
                        S.add("dve", I("tensor_tensor",
                            out=TMPF[p0:p1, 2, 0:8], in0=src[p0:p1, a0 + ca:a0 + ca + 8], in1=EDGE[p0:p1, j, ea:ea + 8], op=ALU.mult),
                            reads=[sres_, constR], writes=[tmpR[2]])
                        S.add("dve", I("tensor_tensor",
                            out=POOLED[p0:p1, j, t0 + ca:t0 + ca + 8], in0=TMPF[p0:p1, 2, 0:8],
                            in1=PP[p0:p1, j, a0 + ca:a0 + ca + 8], op=ALU.subtract),
                            reads=[tmpR[2], ppR[j]], writes=[pooledR[j]])

        shift_add(PA, PP[:, 0, :], 1, Wd, -1, 0, [paR], [ppR[0]])
        shift_add(PB, PA, 2, Wd - 1, -1, 1, [pbR], [paR], 64, 128)
        pool_finish(0, PA, PB, paR, pbR)
        shift_add(PA, PP[:, 1, :], 1, Wd, -1, 0, [paR], [ppR[1], pooledR[0]])
        shift_add(PB, PA, 2, Wd - 1, -1, 1, [pbR], [paR, pooledR[0]])
        shift_add(PA, PB, 4, Wd - 3, -2, 2, [paR], [pbR])
        shift_add(PB, PA, 8, Wd - 7, -4, 4, [pbR], [paR], 64, 128)
        pool_finish(1, PA, PB, paR, pbR)

        CK(f"pool{l}")
        slot, sres = load_piece(wrows(W, 768 + 256))
        for hh in range(2):
            fm_group(slot, sres, hh * 128, 8, hy_rhs, hy_res, ev_qk(KT, kR, 2 + hh, kcol))
        tokmajor_k(1, slot, sres)
        CK(f"qk{l}")
        grp = (
            dict(qa=0, qn=512, ka=1280, kn=512, qres=lambda h: [qR[h][0]], kres=lambda h: [kR[h][0]]),
            dict(qa=512, qn=1024, ka=0, kn=1280, qres=lambda h: [qR[h][1], qR[h][2]], kres=lambda h: [kcR[h], kR[h][1], kR[h][2]]),
        )

        def bounds_a(u):
            gi, h = u // 4, u % 4
            G = grp[gi]
            for (src, base, n, res, col, SQX, sqx) in ((QT, G["qa"], G["qn"], G["qres"](h), 0, SQ0, sqR[0]), (KT, G["ka"], G["kn"], G["kres"](h), 2, SQ1, sqR[1])):
                S.add("act", I("activation", out=SQX[:, 0:n], in_=src[:, h, base:base + n], func=AF.Square), reads=res, writes=[sqx])

        def bounds(u):
            gi, h = u // 4, u % 4
            G = grp[gi]
            for (src, base, n, res, col, SQX, sqx) in ((QT, G["qa"], G["qn"], G["qres"](h), 0, SQ0, sqR[0]), (KT, G["ka"], G["kn"], G["kres"](h), 2, SQ1, sqR[1])):
                n256 = n // 256
                for br, ones in ((0, OTOP), (1, OBOT)):
                    for i in range(n256):
                        slot_i = br * n256 + i
                        bk, half = slot_i // 2, slot_i % 2
                        S.add("pe", I("matmul", PS[:, bk, half * 256:(half + 1) * 256], ones[:], SQX[:, i * 256:(i + 1) * 256], start=True, stop=True),
                              reads=[sqx, miscR], writes=[bankR[bk]])
                nb = (2 * n256 + 1) // 2
                flat = PS[:, 0:nb, :].rearrange("p b n -> p (b n)")[:, 0:2 * n256 * 256]
                S.add("dve", I("tensor_reduce", out=NBS[:, u, col:col + 2], in_=flat.rearrange("p (b n) -> p b n", b=2), axis=AX.X, op=ALU.max),
                      reads=[bankR[i] for i in range(nb)], writes=[nbsR[u]])
            S.add("dve", I("tensor_tensor", out=NBS[:, u, 4:6], in0=NBS[:, u, 0:2], in1=NBS[:, u, 2:4], op=ALU.mult), reads=[nbsR[u]], writes=[nbsR[u]])
            S.add("act", I("activation", out=NBS[:, u, 6:8], in_=NBS[:, u, 4:6], func=AF.Ln, scale=1.0 / 64.0), reads=[nbsR[u]], writes=[nbsR[u]])
            S.add("act", I("activation", out=NBS[:, u, 4:6], in_=NBS[:, u, 6:8], func=AF.Exp, scale=0.5), reads=[nbsR[u]], writes=[nbsR[u]])
            S.add("dve", I("tensor_scalar", out=NBS[:, u, 6:8], in0=NBS[:, u, 4:6], scalar1=-1.0, scalar2=None, op0=ALU.mult),
                  reads=[nbsR[u]], writes=[nbsR[u]])
            S.add("dve", I("tensor_tensor", out=NBA[:, 2 * u:2 * u + 1], in0=NBS[:, u, 6:7], in1=NBS[:, u, 7:8], op=ALU.min),
                  reads=[nbsR[u]], writes=[nbaR[u]])

        bseq = iter(range(8))

        def bstep():
            u = next(bseq, None)
            if u is not None:
                bounds(u)
            if u is not None and u + 1 < 8:
                bounds_a(u + 1)

        bounds_a(0)
        for hp in range(2):
            slot, sres = load_piece(wrows(W, 1280 + hp * 256))
            for c in range(12):
                if c % 6 == 3:
                    bstep()
                bk = 4 + (c % 4)
                t = c // 4
                for k in range(8):
                    S.add("pe", I("matmul", PS[:, bk, 0:256], HY[:, k, c * 128:(c + 1) * 128], slot[:, k, :],
                                                                             start=(k == 0), stop=(k == 7)),
                          reads=[sres, hyR[k][t]], writes=[bankR[bk]])
                vc = 10 + c if c < 4 else c - 2
                S.add("dve", I("tensor_copy", out=VT[:, vc, hp * 256:(hp + 1) * 256], in_=PS[:, bk, 0:256]),
                      reads=[bankR[bk]], writes=[vR[vc]])
                if c < 4:
                    b = c % 2
                    S.add("act", I("activation", out=STG[:, b, 0:256], in_=PS[:, bk, 0:256], func=AF.Copy),
                          reads=[bankR[bk]], writes=[stgR[b]])
                    S.add("sp", I("dma_start", out=nv_o[c // 2, l, (c % 2) * 128:(c % 2) * 128 + 128, hp * 256:(hp + 1) * 256],
                                                                     in_=STG[:, b, 0:256]),
                          reads=[stgR[b]], writes=[outR], dsem=outD[b])

        CK(f"v{l}")
        bstep()
        slot, sres = load_piece(wrows(W, 1792))
        for j in range(2):
            def ev_u(t, bk, j=j):
                S.add("act", I("activation", out=UT[:, j, tile(t)], in_=PS[:, bk, :], func=AF.Gelu_apprx_tanh), reads=[bankR[bk]], writes=[uR[j][t]])
            fm_group(slot, sres, j * 128, 8, hy_rhs, hy_res, ev_u)

        CK(f"u{l}")
        bstep()
        S.transfer(ppR + [paR, pbR], vnpR + ebR + [attR])
        S.add("pool", I("memset", VNP[:, :, :, :, :], 0.0), writes=vnpR)

        slot, sres = load_piece(wrows(W, 2048))
        def sgv_mm(t):
            vb = t % 2
            for cc in range(4):
                c = t * 4 + cc
                bk = 6 + cc // 2
                co = (cc % 2) * 256
                for k in range(8):
                    S.add("pe", I("matmul", PS[:, bk, co:co + 256], HY[:, k, c * 128:(c + 1) * 128], slot[:, k, :],
                                  start=(k == 0), stop=(k == 7)),
                          reads=[sres, hyR[k][t]], writes=[bankR[bk]])

        def sgv_chain(t):
            vb = t % 2
            src = PS[:, 6:8, :].rearrange("p b n -> p (b n)")
            ga = TMPF[:, 0:2, :].rearrange("p b n -> p (b n)")
            gv = TMPF[:, 2:4, :].rearrange("p b n -> p (b n)")
            bsrc = [bankR[6], bankR[7]]
            S.add("act", I("activation", out=gv, in_=src, func=AF.Gelu_apprx_tanh), reads=bsrc, writes=[tmpR[2], tmpR[3]])
            for cc in range(4):
                S.add("dve", I("bn_stats", out=ST6[:, cc * 6:(cc + 1) * 6], in_=gv[:, cc * 256:(cc + 1) * 256]), reads=[tmpR[2], tmpR[3]], writes=[st6R])
            for cc in range(4):
                S.add("dve", I("bn_aggr", out=ST6[:, 24 + cc * 2:26 + cc * 2], in_=ST6[:, cc * 6:(cc + 1) * 6]), reads=[st6R], writes=[st6R])
            mv = ST6[:, 24:32].rearrange("p (c two) -> p c two", two=2)
            S.add("act", I("activation", out=ST6[:, 32:36], in_=mv[:, :, 1], func=AF.Ln, bias=EPSB[:]), reads=[st6R, miscR], writes=[st6R])
            S.add("act", I("activation", out=ST6[:, 36:40], in_=ST6[:, 32:36], func=AF.Exp, scale=-0.5), reads=[st6R], writes=[st6R])
            gv3 = gv.rearrange("p (c n) -> p c n", c=4)
            S.add("dve", I("tensor_tensor", out=gv3, in0=gv3, in1=mv[:, :, 0:1].to_broadcast([128, 4, 256]), op=ALU.subtract),
                  reads=[st6R, tmpR[2], tmpR[3]], writes=[tmpR[2], tmpR[3]])
            S.add("dve", I("tensor_tensor", out=gv3, in0=gv3, in1=ST6[:, 36:40].unsqueeze(2).to_broadcast([128, 4, 256]), op=ALU.mult),
                  reads=[st6R, tmpR[2], tmpR[3]], writes=[tmpR[2], tmpR[3]])
            for par in range(2):
                dst = VNP[:, vb, :, :, :].rearrange("p c (a b) n -> p c a b n", b=2)[:, :, :, par, par * 64:par * 64 + 64]
                srcv = gv.rearrange("p (c a b n) -> p c a b n", c=4, a=2, b=2)[:, :, :, par, :]
                gn = GN[:, l, :].rearrange("p (a b n) -> p a b n", a=2, b=2)[:, :, par, :].unsqueeze(1).to_broadcast([128, 4, 2, 64])
                S.add("dve", I("tensor_tensor", out=dst, in0=srcv, in1=gn, op=ALU.mult), reads=[tmpR[2], tmpR[3], constR], writes=[vnpR[vb]])

        def sgv_gate(t):
            vb = t % 2
            for j in range(2):
                bkk = 3 * dstate["set"] + j
                for cc in range(4):
                    for gg in range(2):
                        g = 2 * j + gg
                        S.add("pe", I("matmul",
                            PS[:, bkk, cc * 128:(cc + 1) * 128], VNP[:, vb, cc, g, :], WST[:, l * 4 + g, :],
                            start=(gg == 0), stop=(gg == 1)),
                            reads=[vnpR[vb], constR], writes=[bankR[bkk]])
                S.add("dve", I("tensor_tensor",
                    out=TMPF[:, j, :].rearrange("p (c n) -> p c n", c=4), in0=PS[:, bkk, :].rearrange("p (c n) -> p c n", c=4),
                    in1=BSG[:, l * 2 + j, :].unsqueeze(1).to_broadcast([128, 4, 128]), op=ALU.add),
                    reads=[bankR[bkk], constR], writes=[tmpR[j]])
                S.add("dve", I("tensor_tensor", out=HY[:, 6 + j, tile(t)], in0=TMPF[:, j, :], in1=UT[:, j, tile(t)], op=ALU.mult),
                      reads=[tmpR[j], uR[j][t]], writes=[hyR[6 + j][t]])
            dstate["set"] ^= 1


        sgv_mm(0)
        for t in range(NT):
            sgv_chain(t)
            if t + 1 < NT:
                sgv_mm(t + 1)
            sgv_gate(t)

        CK(f"sgu{l}")
        for j in range(2):
            for t in range(NT):
                bk = 3 * dstate["set"] + t
                S.add("pe", I("matmul", PS[:, bk, :], WPBD[:, l * 2 + j, :], POOLED[:, j, tile(t)], start=True, stop=True),
                      reads=[pooledR[j], constR], writes=[bankR[bk]])
                S.add("act", I("activation", out=HY[:, j, tile(t)], in_=PS[:, bk, :], func=AF.Copy,
                                                                    scale=SVT[:, 56 + 2 * l + j:57 + 2 * l + j]),
                      reads=[bankR[bk], svR], writes=[hyR[j][t]])
            dstate["set"] ^= 1

        CK(f"poolmix{l}")
        for _ in range(8):
            bstep()
        units = []
        for h in range(4):
            units.append(dict(h=h, u=h, nkc=2, ytile=0, ycols=slice(0, 512), qres=[qR[h][0]], kres=[kR[h][0]],
                              halves=[(0, 256, 0, 1280, [10, 11]), (256, 256, 256, 1536, [12, 13])], copies=False))
        for h in range(4):
            for qi in range(2):
                units.append(dict(h=h, u=4 + h, nkc=10, ytile=1 + qi, ycols=slice(512 + qi * 512, 1024 + qi * 512), qres=[qR[h][1 + qi]],
                                  kres=[kcR[h], kR[h][1], kR[h][2]],
                                  halves=[(0, 512, 512 + qi * 512, 0, list(range(10)))], copies=True))
        B_O1, B_O2, B_Z1, B_Z2 = 4, 5, 6, 7
        T = lambda i: TMPF[:, i, :]

        def stage1(U):
            h = U["h"]
            S.add("dve", I("tensor_copy", out=T(0), in_=PS[:, B_Z1, :]), reads=[bankR[B_Z1]], writes=[tmpR[0]])
            S.add("act", I("activation", out=T(2), in_=PS[:, B_Z2, :], func=AF.Copy), reads=[bankR[B_Z2]], writes=[tmpR[2]])
            S.add("dve", I("tensor_copy", out=T(1), in_=PS[:, B_O1, :]), reads=[bankR[B_O1]], writes=[tmpR[1]])
            S.add("act", I("activation", out=T(3), in_=PS[:, B_O2, :], func=AF.Copy), reads=[bankR[B_O2]], writes=[tmpR[3]])
            S.add("dve", I("reciprocal", out=T(0), in_=T(0)), reads=[tmpR[0]], writes=[tmpR[0]])
            S.add("dve", I("tensor_tensor", out=T(1), in0=T(1), in1=T(0), op=ALU.mult), reads=[tmpR[0], tmpR[1]], writes=[tmpR[1]])
            if U["copies"]:
                S.add("dve", I("reciprocal", out=T(2), in_=T(2)), reads=[tmpR[2]], writes=[tmpR[2]])
            else:
                S.add("act", I("activation", out=T(2), in_=T(2), func=AF.Ln), reads=[tmpR[2]], writes=[tmpR[2]])
                S.add("act", I("activation", out=T(2), in_=T(2), func=AF.Exp, scale=-1.0), reads=[tmpR[2]], writes=[tmpR[2]])
            S.add("dve", I("tensor_tensor", out=T(3), in0=T(3), in1=T(2), op=ALU.mult), reads=[tmpR[2], tmpR[3]], writes=[tmpR[3]])
            S.add("dve", I("scalar_tensor_tensor", out=T(1), in0=T(3), scalar=LAM[:, 8 + l:9 + l], in1=T(1), op0=ALU.mult, op1=ALU.add),
                  reads=[tmpR[3], tmpR[1], miscR], writes=[tmpR[1]])

        def stage2(U):
            h = U["h"]
            nbk = 1
            S.add("act", I("activation", out=SQ1[:, 0:512], in_=T(1), func=AF.Square), reads=[tmpR[1]], writes=[sqR[1]])
            S.add("pe", I("matmul", PS[:, nbk, :], ONES[:], SQ1[:, 0:512], start=True, stop=True), reads=[sqR[1], miscR], writes=[bankR[nbk]])
            S.add("act", I("activation", out=T(0), in_=PS[:, nbk, :], func=AF.Ln, scale=1.0 / 128.0, bias=EPSB[:]),
                  reads=[bankR[nbk], miscR], writes=[tmpR[0]])
            S.add("act", I("activation", out=T(0), in_=T(0), func=AF.Exp, scale=-0.5), reads=[tmpR[0]], writes=[tmpR[0]])
            S.add("dve", I("scalar_tensor_tensor", out=HY[:, 2 + h, U["ycols"]], in0=T(1), scalar=GSUB[:, l:l + 1], in1=T(0), op0=ALU.mult, op1=ALU.mult),
                  reads=[tmpR[1], tmpR[0], miscR], writes=[hyR[2 + h][U["ytile"]]])

        pending = None
        for U in units:
            h, nkc, u = U["h"], U["nkc"], U["u"]

            def scores(kc):
                sbk = (kc % 2) * 2
                for (o0, w, qcol, kcol, vch) in U["halves"]:
                    kcs = slice(kcol + kc * 128, kcol + (kc + 1) * 128)
                    qcs = slice(qcol, qcol + w)
                    S.add("pe", I("matmul", PS[:, sbk, o0:o0 + w], KT[0:64, h, kcs], QT[0:64, h, qcs], start=True, stop=True),
                          reads=U["kres"] + U["qres"], writes=[bankR[sbk]])
                    S.add("pe", I("matmul", PS[:, sbk + 1, o0:o0 + w], KT[64:128, h, kcs], QT[64:128, h, qcs], start=True, stop=True),
                          reads=U["kres"] + U["qres"], writes=[bankR[sbk + 1]])
                for br in range(2):
                    eb = (kc % 4) * 2 + br
                    S.add("act", I("activation", out=EB[:, eb, :], in_=PS[:, sbk + br, :], func=AF.Exp, scale=0.125, bias=NBA[:, 2 * u:2 * u + 1]),
                          reads=[bankR[sbk + br], nbaR[u]], writes=[ebR[eb]])

            def pv(kc):
                for br, (bo, bz) in enumerate(((B_O1, B_Z1), (B_O2, B_Z2))):
                    eb = (kc % 4) * 2 + br
                    for hi_, (o0, w, qcol, kcol, vch) in enumerate(U["halves"]):
                        vc = vch[kc]
                        S.add("pe", I("matmul", PS[:, bo, o0:o0 + w], VT[:, vc, h * 128:(h + 1) * 128], EB[:, eb, o0:o0 + w],
                                      start=(kc == 0 and hi_ == 0), stop=(kc == nkc - 1 and hi_ == len(U["halves"]) - 1)),
                              reads=[vR[vc], ebR[eb]], writes=[bankR[bo]])
                    S.add("pe", I("matmul", PS[:, bz, :], ONES[:], EB[:, eb, :], start=(kc == 0), stop=(kc == nkc - 1)),
                          reads=[miscR, ebR[eb]], writes=[bankR[bz]])
                    if 0 < kc < nkc - 1 and br == 0:
                        for _ in range(NWARM):
                            S.add("pe", I("matmul", PS[:, bo, 0:128], OBOT[64:128, :], OTOP[64:128, :], start=False, stop=False),
                                  reads=[miscR], writes=[bankR[bo]])

            flush_at = min(5, nkc - 1)
            scores(0)
            for kc in range(nkc):
                if kc + 1 < nkc:
                    scores(kc + 1)
                pv(kc)
                if kc == flush_at and pending is not None:
                    stage2(pending)
                    pending = None
            stage1(U)
            pending = U
        stage2(pending)


        ada_more(24)
        CK(f"attn{l}")
        def resid_evac(gi):
            def mk(m):
                def ev(t, bk):
                    c = cond_of[t]
                    S.add("dve", I("scalar_tensor_tensor", out=X[:, m, tile(t)], in0=PS[:, bk, :], scalar=modap(l, gi, m, c),
                                                                  in1=X[:, m, tile(t)], op0=ALU.mult, op1=ALU.add),
                          reads=[bankR[bk], modR[l], xR[m][t]], writes=[xR[m][t]])
                return ev
            return mk

        for pc in range(4):
            slot, sres = load_piece(wrows(w_out[l], pc * 256))
            for mm in range(2):
                fm_group(slot, sres, mm * 128, 8, hy_rhs, hy_res, resid_evac(2)(pc * 2 + mm), t_outer=(pc == 3 and mm == 1))

        CK(f"wout{l}")
        ada_it = ada_steps(l + 1) if l + 1 < 2 else iter(())

        def ada_next(n):
            for _ in range(n):
                next(ada_it, None)
        S.fence([r for jj in actR for r in jj])

        def ffn_evac(j, t, bg, bu):
            b = t % 2
            S.add("act", I("activation", out=XB[:, b, :], in_=PS[:, bg, :], func=AF.Silu), reads=[bankR[bg]], writes=[xbR[b]])
            S.add("dve", I("tensor_tensor", out=ACTB[:, j, tile(t)], in0=PS[:, bu, :], in1=XB[:, b, :], op=ALU.mult),
                  reads=[bankR[bu], xbR[b]], writes=[actR[j][t]])

        def load_fpairs():
            return [(jp, load_piece(wrows(w_fi[l], jp * 256)), load_piece(wrows(w_fi[l], DFF + jp * 256))) for jp in range(2)]

        def ffn_first(t):
            for (jp, (gslot, gres), (uslot, ures)) in fpairs:
                for jj in range(2):
                    j = jp * 2 + jj
                    bg = fb["i"] % 6
                    bu = (fb["i"] + 1) % 6
                    fb["i"] += 2
                    for k in range(8):
                        S.add("pe", I("matmul", PS[:, bg, :], gslot[:, k, jj * 128:(jj + 1) * 128], HY[:, k, tile(t)], start=(k == 0), stop=(k == 7)),
                              reads=[gres, hyR[k][t]], writes=[bankR[bg]])
                    for k in range(8):
                        S.add("pe", I("matmul", PS[:, bu, :], uslot[:, k, jj * 128:(jj + 1) * 128], HY[:, k, tile(t)], start=(k == 0), stop=(k == 7)),
                              reads=[ures, hyR[k][t]], writes=[bankR[bu]])
                    ffn_evac(j, t, bg, bu)

        if l == 0:
            norm_seq(l, A2, 3, filler=lambda: ada_next(4))
            fpairs = load_fpairs()
            for t in range(NT):
                ffn_first(t)
        else:
            fpairs = load_fpairs()
            norm_seq(l, A2, 3, after=ffn_first)
        CK(f"norm2{l}")
        for jp in range(2, 11):
            ada_next(2 if jp in (2, 3, 10) else 1)
            gslot, gres = load_piece(wrows(w_fi[l], jp * 256))
            uslot, ures = load_piece(wrows(w_fi[l], DFF + jp * 256))
            for jj in range(2):
                j = jp * 2 + jj
                for k in range(8):
                    for t in range(NT):
                        S.add("pe", I("matmul", PS[:, t, :], gslot[:, k, jj * 128:(jj + 1) * 128], HY[:, k, tile(t)], start=(k == 0), stop=(k == 7)),
                              reads=[gres, hyR[k][t]], writes=[bankR[t]])
                for k in range(8):
                    for t in range(NT):
                        S.add("pe", I("matmul", PS[:, 3 + t, :], uslot[:, k, jj * 128:(jj + 1) * 128], HY[:, k, tile(t)], start=(k == 0), stop=(k == 7)),
                              reads=[ures, hyR[k][t]], writes=[bankR[3 + t]])
                for t in range(NT):
                    ffn_evac(j, t, t, 3 + t)

        CK(f"ffnin{l}")
        for _ in ada_it:
            pass
        for pc in range(4):
            slots = []
            for kg, (kb, nk) in enumerate(((0, 8), (8, 8), (16, 6))):
                slots.append(load_piece(wrows(w_fo[l], pc * 256, k0=kb, nk=nk), nk=nk) + (kb, nk))
            if pc == 3:
                for t in range(NT):
                    for mm in range(2):
                        m = pc * 2 + mm
                        bk = 3 * mm + t
                        for (slot, sres, kb, nk) in slots:
                            for k in range(nk):
                                kk = kb + k
                                S.add("pe", I("matmul", PS[:, bk, :], slot[:, k, mm * 128:(mm + 1) * 128], ACTB[:, kk, tile(t)],
                                              start=(kk == 0), stop=(kk == NFF - 1)),
                                      reads=[sres, actR[kk][t]], writes=[bankR[bk]])
                        resid_evac(5)(m)(t, bk)
                continue
            for mm in range(2):
                m = pc * 2 + mm
                base = 3 * dstate["set"]
                dstate["set"] ^= 1
                if m == 7:
                    for t in range(NT):
                        for (slot, sres, kb, nk) in slots:
                            for k in range(nk):
                                kk = kb + k
                                S.add("pe", I("matmul", PS[:, base + t, :], slot[:, k, mm * 128:(mm + 1) * 128], ACTB[:, kk, tile(t)],
                                              start=(kk == 0), stop=(kk == NFF - 1)),
                                      reads=[sres, actR[kk][t]], writes=[bankR[base + t]])
                        resid_evac(5)(m)(t, base + t)
                    continue
                for (slot, sres, kb, nk) in slots:
                    for k in range(nk):
                        kk = kb + k
                        for t in range(NT):
                            S.add("pe", I("matmul",
                                PS[:, base + t, :], slot[:, k, mm * 128:(mm + 1) * 128], ACTB[:, kk, tile(t)],
                                start=(kk == 0), stop=(kk == NFF - 1)),
                                reads=[sres, actR[kk][t]], writes=[bankR[base + t]])
                for t in range(NT):
                    resid_evac(5)(m)(t, base + t)

    def final_phase():
        S.fence(xnbRs)

        def head(t):
            XNB, xnbR = XNBS[t % 2], xnbRs[t % 2]
            rb, rr = rms_stats(t)
            for k in range(8):
                S.add("dve", I("scalar_tensor_tensor", out=XNB[:, k, :], in0=X[:, k, tile(t)], scalar=SVT[:, 48 + k:49 + k],
                               in1=rb, op0=ALU.mult, op1=ALU.mult),
                      reads=[xR[k][t], rr, svR], writes=[xnbR])

        def tail(t):
            XNB, xnbR = XNBS[t % 2], xnbRs[t % 2]
            for cc in range(4):
                b = cc % 2
                for half in range(2):
                    bkk = 2 * b + half
                    for kk in range(4):
                        k = half * 4 + kk
                        S.add("pe", I("transpose", PS[:, bkk, kk * 128:(kk + 1) * 128], XNB[:, k, cc * 128:(cc + 1) * 128], IDF[:]),
                              reads=[xnbR, constR], writes=[bankR[bkk]])
                    if half == 0:
                        S.add("dve", I("tensor_copy", out=STG[:, b, half * 512:(half + 1) * 512], in_=PS[:, bkk, :]),
                              reads=[bankR[bkk]], writes=[stgR[b]])
                    else:
                        S.add("act", I("activation", out=STG[:, b, half * 512:(half + 1) * 512], in_=PS[:, bkk, :], func=AF.Copy),
                              reads=[bankR[bkk]], writes=[stgR[b]])
                c = t * 4 + cc
                S.add("sp", I("dma_start", out=yout[c * 128:(c + 1) * 128, :], in_=STG[:, b, :]),
                      reads=[stgR[b]], writes=[outR], dsem=outD[b])

        head(0)
        for t in range(NT):
            if t + 1 < NT:
                head(t + 1)
            tail(t)


    try:
        if stop_early:
            raise _Stop()
        CK("xload")
        ada0_it = ada_steps(0)
        for _ in range(8):
            next(ada0_it, None)
        CK("ada0")
        layer(0, ada0_it)
        CK("L0")
        layer(1)
        CK("L1")
        final_phase()
    except _Stop:
        pass

    def _unused():
        pass

    if taps:
        tapD = S.dsem("dtap")
        tapsrc = {"X": (X[:], [xR[k][t] for k in range(8) for t in range(NT)]),
                  "HY": (HY[:], [hyR[k][t] for k in range(8) for t in range(NT)])}
        for name in taps:
            ap_, res_ = tapsrc[name]
            S.add("sp", I("dma_start", out=tap_out[name], in_=ap_), reads=res_, writes=[outR], dsem=tapD)

    final_waits = [d for d in S.dsems if d.count > 0]

    S.finalize_counts()
    with nc.Block() as block:
        @block.tensor
        def _(e):
            S.emit("pe", e)

        @block.scalar
        def _(e):
            S.emit("act", e)

        @block.vector
        def _(e):
            S.emit("dve", e)

        @block.gpsimd
        def _(e):
            S.emit("pool", e)

        @block.sync
        def _(e):
            S.emit("sp", e)
            for d in final_waits:
                e.wait_ge(d.sem, d.count)
    return nc


def _consts():
    ident = np.eye(128, dtype=np.float32)
    m = np.arange(128)
    partner = np.where((m % 32) < 16, m + 16, m - 16)
    pm = np.zeros((128, 128), np.float32)
    pm[partner, m] = 1.0
    t = np.arange(1024)
    rows = (t // 64).astype(np.float32)
    cols = (t % 64).astype(np.float32)
    inv = (1.0 / (10000.0 ** (np.arange(0, 32, 2, dtype=np.float32) / 32))).astype(np.float32)
    ar, ac = rows[:, None] * inv[None], cols[:, None] * inv[None]
    cr, sr, cc_, sc_ = np.cos(ar), np.sin(ar), np.cos(ac), np.sin(ac)
    C = np.zeros((128, 1024), np.float32)
    Sg = np.zeros((128, 1024), np.float32)
    for p in range(128):
        i = p % 16
        is_col = (p % 64) >= 32
        sign = -1.0 if (p % 32) < 16 else 1.0
        C[p] = (cc_ if is_col else cr)[:, i]
        Sg[p] = sign * (sc_ if is_col else sr)[:, i]
    invw = np.zeros((128, 2), np.float32)
    edge = np.zeros((128, 2, 16), np.float32)
    for j in range(2):
        for half in range(2):
            w = POOL_WINDOWS[2 * j + half]
            ps = slice(half * 64, half * 64 + 64)
            invw[ps, j] = 1.0 / w
            L = 1024
            tt = np.arange(L)
            lo = np.clip(tt - w // 2, 0, L)
            hi = np.clip(tt + w - w // 2, 0, L)
            cnt = (hi - lo).astype(np.float32)
            edge[ps, j, 0:8] = (1.0 / cnt[0:8])[None, :]
            edge[ps, j, 8:16] = (1.0 / cnt[L - 8:L])[None, :]
    return ident, pm, C.astype(np.float32), Sg.astype(np.float32), invw, edge


_NC_CACHE = {}


def kernel(x_prompt, x_sample, cache_k, cache_v, c, c_ctx, norm1_g, w_ada, b_ada, w_in,
           w_pool, pool_scale, lam_q1, lam_k1, lam_q2, lam_k2, subln_g, sgu_norm_g,
           w_sgu, b_sgu, w_out, norm2_g, w_ffn_in, w_ffn_out, final_g, _taps=None, _ncores=8, _stop=None):
    f = lambda a: np.ascontiguousarray(np.asarray(a, dtype=np.float32))
    x_prompt, x_sample, cache_k, cache_v = f(x_prompt), f(x_sample), f(cache_k), f(cache_v)
    ident, pm, ropec, ropes, invw, edge = _consts()
    shared = {
        "w_ada": f(w_ada), "w_in": f(w_in), "w_out": f(w_out), "w_ffn_in": f(w_ffn_in), "w_ffn_out": f(w_ffn_out),
        "w_pool": f(w_pool), "w_sT": np.ascontiguousarray(np.transpose(f(w_sgu), (0, 1, 3, 2))),
        "b_sgu": f(b_sgu), "sgu_norm_g": f(sgu_norm_g),
        "lamv": np.ascontiguousarray(np.stack([f(lam_q1), f(lam_k1), f(lam_q2), f(lam_k2)], axis=1).reshape(1, 512)),
        "c_ident": ident, "c_pm": pm, "c_ropec": ropec, "c_ropes": ropes, "c_invw": invw, "c_edge": edge,
    }
    key = (tuple(sorted(_taps.items())) if _taps else None, _stop)
    if key not in _NC_CACHE:
        _NC_CACHE[key] = build_program(_taps, _stop)
    nc = _NC_CACHE[key]
    in_maps = []
    for i in range(_ncores):
        sv = np.zeros((2, 128, 128), np.float32)
        sv[0, 0:8] = f(c_ctx).reshape(8, 128)
        sv[0, 8:16] = f(c)[i].reshape(8, 128)
        sv[0, 16:32] = f(norm1_g).reshape(16, 128)
        sv[0, 32:48] = f(norm2_g).reshape(16, 128)
        sv[0, 48:56] = f(final_g).reshape(8, 128)
        sv[0, 56:60] = f(pool_scale).reshape(4, 128)
        sv[0, 60:62] = f(subln_g).reshape(2, 128)
        sv[1, 0:96] = f(b_ada).reshape(96, 128)
        m = dict(shared)
        m["xin"] = np.ascontiguousarray(np.concatenate([x_prompt[2 * i], x_prompt[2 * i + 1], x_sample[i]], axis=0))
        m["ck"] = np.ascontiguousarray(cache_k[i].reshape(2, 256, 512))
        m["cv"] = np.ascontiguousarray(cache_v[i].reshape(2, 256, 512))
        m["smallvec"] = sv
        in_maps.append(m)
    res = run_bass_kernel_spmd(nc, in_maps, core_ids=list(range(_ncores)))
    outs = res.results
    nb = 2 * _ncores
    y_prompt = np.zeros((nb, 256, D), np.float32)
    y_sample = np.zeros((_ncores, 1024, D), np.float32)
    nk = np.zeros((nb, 2, 256, 4, 128), np.float32)
    nv = np.zeros((nb, 2, 256, 4, 128), np.float32)
    for i in range(_ncores):
        y = outs[i]["yout"]
        y_prompt[2 * i] = y[0:256]
        y_prompt[2 * i + 1] = y[256:512]
        y_sample[i] = y[512:1536]
        nk[2 * i:2 * i + 2] = outs[i]["nk"].reshape(2, 2, 256, 4, 128)
        nv[2 * i:2 * i + 2] = outs[i]["nv"].reshape(2, 2, 256, 4, 128)
    if _taps:
        kernel._taps_out = [{k: outs[i]["tap_" + k] for k in _taps} for i in range(_ncores)]
    return (y_prompt, y_sample, nk, nv)
```
